# Optimizing a Trainium2 kernel written in Bass

```python
import jax, jax.numpy as jnp
from jax import lax
import numpy as np

D_MODEL = 2048
BATCH = 4
SEQ = 4096
DEPTH = 2

SWA_Q_HEADS = 16
SWA_KV_HEADS = 2
SWA_HEAD_DIM = 64
SWA_WINDOW = 128
SWA_BLOCK = 128
MLA_HEADS = 16
MLA_Q_RANK = 512
MLA_KV_RANK = 512
MLA_NOPE_DIM = 128
MLA_ROPE_DIM = 64
MLA_V_DIM = 128
MLA_BLOCK = 128
ROPE_THETA = 10000.0
SGU_GROUPS = 8
SGU_GROUP_DIM = 128
SGU_CHUNK = 128
SGU_WIDTH = SGU_GROUPS * SGU_GROUP_DIM
D_FF = 5632
CONV_WIDTH = 3
N_BRANCHES = 3
EPS = 1e-5
MASK_VALUE = -1e30
DN_ALPHA = (2 * DEPTH) ** 0.25
DN_BETA = (8 * DEPTH) ** -0.25

A_Q = SWA_Q_HEADS * SWA_HEAD_DIM
A_KV = SWA_KV_HEADS * SWA_HEAD_DIM
B_OUT = MLA_HEADS * MLA_V_DIM
N_IN = A_Q + 2 * A_KV + MLA_Q_RANK + MLA_KV_RANK + MLA_ROPE_DIM + 2 * SGU_WIDTH + N_BRANCHES * D_MODEL

kernel_name = "hybrid_swa_mla_sgu_deepnorm"


def _layer_norm(x, g, b):
    xf = x.astype(jnp.float32)
    mu = xf.mean(-1, keepdims=True)
    var = jnp.mean(jnp.square(xf - mu), -1, keepdims=True)
    y = (xf - mu) * lax.rsqrt(var + EPS) * g.astype(jnp.float32) + b.astype(jnp.float32)
    return y.astype(x.dtype)


def _rms_norm(x, g):
    xf = x.astype(jnp.float32)
    y = xf * lax.rsqrt(jnp.mean(jnp.square(xf), -1, keepdims=True) + EPS) * g.astype(jnp.float32)
    return y.astype(x.dtype)


def _rope(x, cos, sin):
    x1, x2 = jnp.split(x, 2, axis=-1)
    return jnp.concatenate([x1 * cos - x2 * sin, x2 * cos + x1 * sin], axis=-1)


def _sliding_window_gqa(q, k, v, sinks):
    B, S = q.shape[:2]
    nb = S // SWA_BLOCK
    G = SWA_Q_HEADS // SWA_KV_HEADS
    qb = q.reshape(B, nb, SWA_BLOCK, SWA_KV_HEADS, G, SWA_HEAD_DIM)
    kb = k.reshape(B, nb, SWA_BLOCK, SWA_KV_HEADS, SWA_HEAD_DIM)
    vb = v.reshape(B, nb, SWA_BLOCK, SWA_KV_HEADS, SWA_HEAD_DIM)

    def with_prev(t):
        prev = jnp.pad(t[:, :-1], ((0, 0), (1, 0), (0, 0), (0, 0), (0, 0)))
        return jnp.concatenate([prev, t], axis=2)

    kw, vw = with_prev(kb), with_prev(vb)
    scores = jnp.einsum('bnqhgd,bnkhd->bnhgqk', qb, kw,
                        preferred_element_type=jnp.float32) * (SWA_HEAD_DIM ** -0.5)
    q_off = jnp.arange(SWA_BLOCK)[:, None] + SWA_BLOCK
    k_off = jnp.arange(2 * SWA_BLOCK)[None, :]
    rel = q_off - k_off
    band = (rel >= 0) & (rel < SWA_WINDOW)
    not_first = (jnp.arange(nb) > 0)[:, None, None]
    valid = band[None] & (not_first | (k_off >= SWA_BLOCK)[None])
    scores = jnp.where(valid[None, :, None, None], scores, MASK_VALUE)
    sink = sinks.astype(jnp.float32).reshape(SWA_KV_HEADS, G)[None, None, :, :, None, None]
    m = jnp.maximum(scores.max(-1, keepdims=True), sink)
    p = jnp.exp(scores - m)
    p = (p / (p.sum(-1, keepdims=True) + jnp.exp(sink - m))).astype(v.dtype)
    out = jnp.einsum('bnhgqk,bnkhd->bnqhgd', p, vw)
    return out.reshape(B, S, A_Q)


def _mla(c_q, c_kv, k_rope, cos, sin, q_norm_g, kv_norm_g, w_uq, w_ukv):
    B, S = c_q.shape[:2]
    q = (_rms_norm(c_q, q_norm_g) @ w_uq).reshape(B, S, MLA_HEADS, MLA_NOPE_DIM + MLA_ROPE_DIM)
    q_nope = q[..., :MLA_NOPE_DIM]
    q_rope = _rope(q[..., MLA_NOPE_DIM:], cos[:, :, None], sin[:, :, None])
    kv = (_rms_norm(c_kv, kv_norm_g) @ w_ukv).reshape(B, S, MLA_HEADS, MLA_NOPE_DIM + MLA_V_DIM)
    k_nope, v = kv[..., :MLA_NOPE_DIM], kv[..., MLA_NOPE_DIM:]
    k_r = _rope(k_rope, cos, sin)
    nb = S // MLA_BLOCK
    scale = (MLA_NOPE_DIM + MLA_ROPE_DIM) ** -0.5
    qn_b = q_nope.reshape(B, nb, MLA_BLOCK, MLA_HEADS, MLA_NOPE_DIM).transpose(1, 0, 2, 3, 4)
    qr_b = q_rope.reshape(B, nb, MLA_BLOCK, MLA_HEADS, MLA_ROPE_DIM).transpose(1, 0, 2, 3, 4)
    key_idx = jnp.arange(S)

    def block(args):
        qn, qr, i = args
        s = (jnp.einsum('bqhd,bkhd->bhqk', qn, k_nope, preferred_element_type=jnp.float32)
             + jnp.einsum('bqhr,bkr->bhqk', qr, k_r, preferred_element_type=jnp.float32)) * scale
        q_idx = i * MLA_BLOCK + jnp.arange(MLA_BLOCK)
        s = jnp.where((key_idx[None, :] <= q_idx[:, None])[None, None], s, MASK_VALUE)
        p = jax.nn.softmax(s, axis=-1).astype(v.dtype)
        return jnp.einsum('bhqk,bkhd->bqhd', p, v)

    out = lax.map(block, (qn_b, qr_b, jnp.arange(nb)))
    return out.transpose(1, 0, 2, 3, 4).reshape(B, S, B_OUT)


def _chunked_sgu(u, v, ln_g, ln_b, w_s, b_s):
    B, S = u.shape[:2]
    nc = S // SGU_CHUNK
    vn = _layer_norm(v, ln_g, ln_b).reshape(B, nc, SGU_CHUNK, SGU_GROUPS, SGU_GROUP_DIM)
    causal = jnp.tril(jnp.ones((SGU_CHUNK, SGU_CHUNK), dtype=bool))
    w = jnp.where(causal[None], w_s, 0.0)
    mixed = jnp.einsum('gts,bnsgc->bntgc', w, vn) + b_s.T[None, None, :, :, None]
    return u * mixed.reshape(B, S, SGU_WIDTH)


def setup_inputs(seed: int = 0) -> dict:
    key = jax.random.key(seed)
    ks = jax.random.split(key, 26)
    L, D = DEPTH, D_MODEL
    f32 = jnp.float32
    nrm = lambda k, shape, s: jax.random.normal(k, shape, f32) * s
    x = jax.random.normal(ks[0], (BATCH, SEQ, D), f32)
    offset = jax.random.randint(ks[1], (BATCH, 1), 0, 1024, dtype=jnp.int32)
    positions = (offset + jnp.arange(SEQ, dtype=jnp.int32)[None, :]).astype(jnp.int32)
    return {
        "x": x,
        "positions": positions,
        "w_in": nrm(ks[2], (L, D, N_IN), D ** -0.5),
        "b_gate": nrm(ks[3], (L, N_BRANCHES, D), 0.1),
        "sinks": nrm(ks[4], (L, SWA_Q_HEADS), 0.5),
        "q_norm_g": 1.0 + nrm(ks[5], (L, MLA_Q_RANK), 0.02),
        "kv_norm_g": 1.0 + nrm(ks[6], (L, MLA_KV_RANK), 0.02),
        "w_uq": nrm(ks[7], (L, MLA_Q_RANK, MLA_HEADS * (MLA_NOPE_DIM + MLA_ROPE_DIM)), MLA_Q_RANK ** -0.5),
        "w_ukv": nrm(ks[8], (L, MLA_KV_RANK, MLA_HEADS * (MLA_NOPE_DIM + MLA_V_DIM)), MLA_KV_RANK ** -0.5),
        "sgu_ln_g": 1.0 + nrm(ks[9], (L, SGU_WIDTH), 0.02),
        "sgu_ln_b": nrm(ks[10], (L, SGU_WIDTH), 0.02),
        "sgu_w": nrm(ks[11], (L, SGU_GROUPS, SGU_CHUNK, SGU_CHUNK), SGU_CHUNK ** -0.5),
        "sgu_b": 1.0 + nrm(ks[12], (L, SGU_GROUPS, SGU_CHUNK), 0.02),
        "w_proj_a": nrm(ks[13], (L, A_Q, D), A_Q ** -0.5),
        "w_proj_b": nrm(ks[14], (L, B_OUT, D), B_OUT ** -0.5),
        "w_proj_c": nrm(ks[15], (L, SGU_WIDTH, D), SGU_WIDTH ** -0.5),
        "w_o": nrm(ks[16], (L, D, D), DN_BETA * D ** -0.5),
        "ln1_g": 1.0 + nrm(ks[17], (L, D), 0.02),
        "ln1_b": nrm(ks[18], (L, D), 0.02),
        "w_up": nrm(ks[19], (L, D, 2 * D_FF), D ** -0.5),
        "conv_w": nrm(ks[20], (L, CONV_WIDTH, 2 * D_FF), CONV_WIDTH ** -0.5),
        "conv_b": nrm(ks[21], (L, 2 * D_FF), 0.02),
        "w_down": nrm(ks[22], (L, D_FF, D), DN_BETA * D_FF ** -0.5),
        "ln2_g": 1.0 + nrm(ks[23], (L, D), 0.02),
        "ln2_b": nrm(ks[24], (L, D), 0.02),
    }


def reference(x, positions, w_in, b_gate, sinks, q_norm_g, kv_norm_g, w_uq, w_ukv,
              sgu_ln_g, sgu_ln_b, sgu_w, sgu_b, w_proj_a, w_proj_b, w_proj_c, w_o,
              ln1_g, ln1_b, w_up, conv_w, conv_b, w_down, ln2_g, ln2_b):
    B, S, D = x.shape
    inv_freq = ROPE_THETA ** (-jnp.arange(0, MLA_ROPE_DIM, 2, dtype=jnp.float32) / MLA_ROPE_DIM)
    ang = positions.astype(jnp.float32)[..., None] * inv_freq
    cos, sin = jnp.cos(ang).astype(x.dtype), jnp.sin(ang).astype(x.dtype)
    split_at = [A_Q, A_Q + A_KV, A_Q + 2 * A_KV]
    split_at += [split_at[-1] + MLA_Q_RANK]
    split_at += [split_at[-1] + MLA_KV_RANK]
    split_at += [split_at[-1] + MLA_ROPE_DIM]
    split_at += [split_at[-1] + SGU_WIDTH]
    split_at += [split_at[-1] + SGU_WIDTH]

    for l in range(DEPTH):
        h = x @ w_in[l]
        qa, ka, va, c_q, c_kv, k_rope, hu, hv, g_logit = jnp.split(h, split_at, axis=-1)
        y_a = _sliding_window_gqa(qa, ka, va, sinks[l])
        y_b = _mla(c_q, c_kv, k_rope, cos, sin, q_norm_g[l], kv_norm_g[l], w_uq[l], w_ukv[l])
        y_c = _chunked_sgu(jax.nn.gelu(hu, approximate=False), jax.nn.gelu(hv, approximate=False),
                           sgu_ln_g[l], sgu_ln_b[l], sgu_w[l], sgu_b[l])
        gates = jax.nn.sigmoid((g_logit.reshape(B, S, N_BRANCHES, D) + b_gate[l]).astype(jnp.float32)).astype(x.dtype)
        merged = (gates[:, :, 0] * (y_a @ w_proj_a[l])
                  + gates[:, :, 1] * (y_b @ w_proj_b[l])
                  + gates[:, :, 2] * (y_c @ w_proj_c[l]))
        x = _layer_norm(DN_ALPHA * x + merged @ w_o[l], ln1_g[l], ln1_b[l])

        up = x @ w_up[l]
        up_pad = jnp.pad(up, ((0, 0), (CONV_WIDTH - 1, 0), (0, 0)))
        conv = conv_b[l] + sum(up_pad[:, j:j + S] * conv_w[l, j] for j in range(CONV_WIDTH))
        gate, val = jnp.split(conv, 2, axis=-1)
        x = _layer_norm(DN_ALPHA * x + (jax.nn.silu(gate) * val) @ w_down[l], ln2_g[l], ln2_b[l])
    return x
```

```python
from contextlib import ExitStack
import numpy as np
import concourse.bass as bass
import concourse.mybir as mybir
from concourse.bass_utils import run_bass_kernel_spmd

F32 = mybir.dt.float32
BF16 = mybir.dt.bfloat16
I32 = mybir.dt.int32
AF = mybir.ActivationFunctionType
ALU = mybir.AluOpType

D = 2048
NIN = 10560
DFF = 5632
TG = 1024
EPS = 1e-5
ALPHA = 4.0 ** 0.25
ENGS = ("pe", "act", "dve", "pool", "sp")
FENCED = ("pe", "act", "dve", "sp", "pool")
DMA_ENG = {}

BG, QNG, KVNG, SLG, SLB, L1G, L1B, L2G, L2B, CW, CB, NPP = 0, 48, 52, 56, 64, 72, 88, 104, 120, 136, 400, 488


class _Op:
    __slots__ = ("eng", "fn", "deps", "idx", "eidx", "is_dma", "semkey", "ordinal", "signal", "tick")


class Sched:
    def __init__(self):
        self.ops = []
        self.eng_ops = {e: [] for e in ENGS}
        self.last_w = {}
        self.readers = {}
        self.dma_count = {}
        self.dma_since_fence = []

    def add(self, eng, fn, reads=(), writes=(), dma=False, sem=None, nofence=False):
        op = _Op()
        op.eng, op.fn, op.is_dma = eng, fn, dma
        op.idx = len(self.ops)
        op.eidx = len(self.eng_ops[eng])
        op.signal = False
        op.tick = 0
        ps_reads = [k for k in reads if isinstance(k, tuple) and k and k[0] == "ps"]
        if ps_reads:
            reads = [k for k in reads if k not in ps_reads]
            writes = list(writes) + ps_reads
        deps = set()
        for k in reads:
            w = self.last_w.get(k)
            if w is not None:
                deps.add(w)
        for k in writes:
            w = self.last_w.get(k)
            if w is not None:
                deps.add(w)
            for r in self.readers.get(k, ()):
                deps.add(r)
        op.deps = deps
        for k in writes:
            self.last_w[k] = op
            self.readers[k] = []
        for k in reads:
            lst = self.readers.setdefault(k, [])
            if not dma:
                lst[:] = [r for r in lst if r.is_dma or r.eng != eng]
            lst.append(op)
        if dma:
            if sem is None:
                sem = ("dma", tuple(writes)[0] if writes else tuple(reads)[0])
            op.semkey = sem
            n = self.dma_count.get(sem, 0) + 1
            self.dma_count[sem] = n
            op.ordinal = n
            if not nofence:
                self.dma_since_fence.append(op)
        else:
            op.semkey = None
            op.ordinal = 0
        self.ops.append(op)
        self.eng_ops[eng].append(op)
        return op

    def fence(self):
        lasts = []
        for e in FENCED:
            for o in reversed(self.eng_ops[e]):
                if o.fn is not None and not o.is_dma:
                    lasts.append(o)
                    break
        dmas = list(self.dma_since_fence)
        self.dma_since_fence = []
        for e in FENCED:
            op = self.add(e, None)
            op.deps = set(o for o in lasts if o.eng != e) | set(dmas)

    def _needs_wait(self, op, d):
        if d.is_dma or d.eng != op.eng:
            return True
        return op.eng != "pe"

    def emit(self, nc, stack):
        for op in self.ops:
            for d in op.deps:
                if self._needs_wait(op, d):
                    d.signal = True
        for e in ENGS:
            t = 0
            for op in self.eng_ops[e]:
                if op.is_dma:
                    continue
                if op.signal:
                    t += 1
                op.tick = t
        eng_sem = {e: stack.enter_context(nc.semaphore("s_" + e)) for e in ENGS}
        dma_sem = {}
        for k in self.dma_count:
            dma_sem[k] = stack.enter_context(nc.semaphore("d%d" % len(dma_sem)))
        block = stack.enter_context(nc.Block())

        def run(e, eng):
            known = {}
            for op in self.eng_ops[e]:
                need = {}
                for d in op.deps:
                    if not self._needs_wait(op, d):
                        continue
                    if d.is_dma:
                        s, v = dma_sem[d.semkey], 16 * d.ordinal
                    else:
                        s, v = eng_sem[d.eng], d.tick
                    if need.get(s, 0) < v:
                        need[s] = v
                for s, v in need.items():
                    if known.get(s, 0) >= v:
                        continue
                    known[s] = v
                    eng.wait_ge(s, v)
                if op.fn is None:
                    continue
                inst = op.fn(eng)
                if op.is_dma:
                    inst.then_inc(dma_sem[op.semkey], 16)
                elif op.signal:
                    inst.then_inc(eng_sem[e], 1)

        block.tensor(lambda eng: run("pe", eng))
        block.scalar(lambda eng: run("act", eng))
        block.vector(lambda eng: run("dve", eng))
        block.gpsimd(lambda eng: run("pool", eng))
        block.sync(lambda eng: run("sp", eng))


def build(S_len=4096, L=2, upto=None):
    nc = bass.Bass("TRN2", target_bir_lowering=False)
    NTG = S_len // TG
    dt = lambda name, shape, dtype, kind: nc.dram_tensor(name, shape, dtype, kind=kind).ap()
    x_in = dt("x", [S_len, D], F32, "ExternalInput")
    pos_in = dt("pos_bc", [128, S_len], I32, "ExternalInput")
    consts_in = dt("consts", [128, 640], F32, "ExternalInput")
    pp_in = dt("pp", [L, 128, NPP], F32, "ExternalInput")
    esink_in = dt("esink", [L, 128, 16], F32, "ExternalInput")
    bsbc_in = dt("bsbc", [L, 128, 1024], F32, "ExternalInput")
    wst_in = dt("wst", [L, 128, 1024], F32, "ExternalInput")
    w_in = dt("w_in", [L, D, NIN], F32, "ExternalInput")
    w_uq = dt("w_uq", [L, 512, 3072], F32, "ExternalInput")
    w_ukv = dt("w_ukv", [L, 512, 4096], F32, "ExternalInput")
    w_pa = dt("w_proj_a", [L, 1024, D], F32, "ExternalInput")
    w_pb = dt("w_proj_b", [L, 2048, D], F32, "ExternalInput")
    w_pc = dt("w_proj_c", [L, 1024, D], F32, "ExternalInput")
    w_o = dt("w_o", [L, D, D], F32, "ExternalInput")
    w_up = dt("w_up", [L, D, 2 * DFF], F32, "ExternalInput")
    w_down = dt("w_down", [L, DFF, D], F32, "ExternalInput")
    out = dt("out", [S_len, D], F32, "ExternalOutput")
    xres = dt("xres", [16, 128, S_len], F32, "Internal")
    xbf = dt("xbf", [16, 128, S_len], BF16, "Internal")
    zscr = dt("zscr", [16, 128, TG], F32, "Internal")
    kcache = dt("kcache", [16, 128, S_len], BF16, "Internal")
    vcache = dt("vcache", [16, S_len, 128], BF16, "Internal")

    S = Sched()
    st = ExitStack()
    with st:
        NF = 53100
        arena = st.enter_context(nc.sbuf_tensor("arena", [128, NF], F32))
        psum = [st.enter_context(nc.psum_tensor("ps%d" % i, [128, 512], F32)) for i in range(8)]
        cur = [0]

        def carve(n):
            o = cur[0]
            cur[0] += n
            assert cur[0] <= NF, cur[0]
            return o

        def f32v(off, n):
            return arena[:, off:off + n]

        def bfv(off, nbf):
            return arena[:, off:off + nbf // 2].bitcast(BF16)

        o = carve(640); CONST = f32v(o, 640)
        IDENT = CONST[:, 0:128]
        INVF = CONST[:, 384:385]
        SGN = CONST[:, 385:386]
        EPSC = CONST[:, 386:387]
        o = carve(256); CB16 = bfv(o, 512)
        IDB, ONESB, TRIB, TRIPB = CB16[:, 0:128], CB16[:, 128:256], CB16[:, 256:384], CB16[:, 384:512]
        o = carve(L * NPP); PP = f32v(o, L * NPP).rearrange("p (l n) -> p l n", l=L)
        o = carve(16); ESINK = f32v(o, 16)
        o = carve(1024); BSBC = f32v(o, 1024).rearrange("p (g t) -> p g t", g=8)
        o = carve(512); WST = bfv(o, 1024).rearrange("p (g t) -> p g t", g=8)
        o = carve(1024); COS = f32v(o, 1024)
        o = carve(1024); SIN = f32v(o, 1024)
        o = carve(176); HALO = f32v(o, 176).rearrange("p (c t) -> p c t", t=2)
        o = carve(S_len // 2); KR = bfv(o, S_len)
        o = carve(8192); XB = bfv(o, 16384).rearrange("p (c n) -> p c n", c=16)
        XB_off = o
        NW = 3
        WSL = []
        for i in range(NW):
            o = carve(2048)
            WSL.append(bfv(o, 4096))
        GEN = cur[0]
        GEN_N = NF - GEN
        YB = bfv(GEN, 16384).rearrange("p (c n) -> p c n", c=16)
        YA = bfv(GEN + 8192, 8192).rearrange("p (c n) -> p c n", c=8)
        YC = bfv(GEN + 12288, 8192).rearrange("p (c n) -> p c n", c=8)

        class Temps:
            def __init__(self, base, limit):
                self.o, self.limit = base, limit

            def f32(self, n):
                o = self.o
                self.o += n
                assert self.o <= self.limit, (self.o, self.limit)
                return f32v(o, n)

            def bf(self, n):
                assert n % 2 == 0
                o = self.o
                self.o += n // 2
                assert self.o <= self.limit, (self.o, self.limit)
                return bfv(o, n)

        uid = [0]

        def U(prefix):
            uid[0] += 1
            return (prefix, uid[0])

        def dma(eng, out_ap, in_ap, reads, writes, sem=None, nofence=False):
            eng = DMA_ENG.get(eng, eng)
            return S.add(eng, lambda e, o=out_ap, i=in_ap: e.dma_start(out=o, in_=i),
                         reads=reads, writes=writes, dma=True, sem=sem, nofence=nofence)

        wctr = [0]

        def load_w(pieces, KC):
            slot = wctr[0] % NW
            wctr[0] += 1
            tot = max(p[3] + p[2] for p in pieces)
            assert KC * tot <= 4096, (KC, tot)
            view = WSL[slot][:, 0:KC * tot].rearrange("p (k n) -> p k n", k=KC)
            keys = []
            for i, (w2, c0, n, dc) in enumerate(pieces):
                key = ("w", slot, i)
                src = w2.rearrange("(k p) n -> p k n", p=128)[:, :, c0:c0 + n]
                dma("pool", view[:, :, dc:dc + n], src, reads=[], writes=[key], sem=("wsem", slot, i), nofence=True)
                keys.append(key)
            return view, keys

        pair_ctr = [0]

        def gemm(lhs_list, rhs_fn, rkeys, wkeys, evac, ntile=2, pairs=(0, 1, 2, 3)):
            pr = pairs[pair_ctr[0] % len(pairs)]
            pair_ctr[0] += 1
            banks = [2 * pr + j for j in range(ntile)]
            n = len(lhs_list)

            def mm(e, lhs_list=lhs_list, banks=banks):
                inst = None
                for kc in range(n):
                    for j, b in enumerate(banks):
                        inst = e.matmul(psum[b][:, :], lhsT=lhs_list[kc], rhs=rhs_fn(kc, j),
                                        start=(kc == 0), stop=(kc == n - 1))
                return inst
            S.add("pe", mm, reads=list(wkeys) + list(rkeys), writes=[("ps", b) for b in banks])
            for j, b in enumerate(banks):
                evac(j, psum[b], ("ps", b))

        def act(out_ap, in_ap, func, reads, writes, bias=None, scale=None):
            kw = {}
            if bias is not None:
                kw["bias"] = bias
            if scale is not None:
                kw["scale"] = scale
            return S.add("act", lambda e: e.activation(out=out_ap, in_=in_ap, func=func, **kw), reads=reads, writes=writes)

        def tt(eng, out_ap, a, b, op, reads, writes):
            return S.add(eng, lambda e: e.tensor_tensor(out=out_ap, in0=a, in1=b, op=op), reads=reads, writes=writes)

        def stt(eng, out_ap, a, sc, b, op0, op1, reads, writes):
            return S.add(eng, lambda e: e.scalar_tensor_tensor(out=out_ap, in0=a, scalar=sc, in1=b, op0=op0, op1=op1),
                         reads=reads, writes=writes)

        def ts(eng, out_ap, a, s1, s2, op0, op1, reads, writes):
            if s2 is None:
                return S.add(eng, lambda e: e.tensor_scalar(out=out_ap, in0=a, scalar1=s1, scalar2=None, op0=op0),
                             reads=reads, writes=writes)
            return S.add(eng, lambda e: e.tensor_scalar(out=out_ap, in0=a, scalar1=s1, scalar2=s2, op0=op0, op1=op1),
                         reads=reads, writes=writes)

        def cp(eng, out_ap, in_ap, reads, writes):
            if eng == "act":
                return S.add("act", lambda e: e.copy(out=out_ap, in_=in_ap), reads=reads, writes=writes)
            return S.add(eng, lambda e: e.tensor_copy(out=out_ap, in_=in_ap), reads=reads, writes=writes)

        dma("sp", CONST, consts_in, [], ["const"])
        dma("sp", PP, pp_in.rearrange("l p n -> p l n"), [], ["pp"])
        cp("dve", IDB, IDENT, ["const"], ["idb"])
        cp("dve", TRIB, CONST[:, 128:256], ["const"], ["trib"])
        cp("dve", TRIPB, CONST[:, 256:384], ["const"], ["tripb"])
        S.add("dve", lambda e: e.memset(ONESB, 1.0), writes=["onesb"])

        T0 = Temps(GEN, NF)
        XS = [T0.f32(2048) for _ in range(2)]
        XT = [T0.f32(512) for _ in range(2)]
        XTB = [T0.bf(512) for _ in range(2)]
        for blk in range(S_len // 128 if upto != "s" else 0):
            xs = XS[blk % 2]
            kxs = ("xs", blk % 2)
            dma("sp", xs, x_in[blk * 128:(blk + 1) * 128, :], [], [kxs])
            for g in range(4):
                b = (blk * 4 + g) % 8
                kp = ("ps", b)

                def tr(e, xs=xs, g=g, b=b):
                    for i in range(4):
                        c = g * 4 + i
                        inst = e.transpose(psum[b][:, i * 128:(i + 1) * 128], xs[:, c * 128:(c + 1) * 128], IDENT)
                    return inst
                S.add("pe", tr, reads=[kxs, "const"], writes=[kp])
                i2 = (blk * 4 + g) % 2
                cp("dve", XT[i2], psum[b][:, :], [kp], [("xt", i2)])
                cp("act", XTB[i2], psum[b][:, :], [kp], [("xtb", i2)])
                if upto in ("s1",):
                    continue
                dma("sp", xres[g * 4:(g + 1) * 4, :, blk * 128:(blk + 1) * 128].rearrange("c p n -> p c n"),
                    XT[i2].rearrange("p (c n) -> p c n", c=4), [("xt", i2)], [U("xres")], sem=("xts", i2))
                if upto in ("s2",):
                    continue
                dma("sp", xbf[g * 4:(g + 1) * 4, :, blk * 128:(blk + 1) * 128].rearrange("c p n -> p c n"),
                    XTB[i2].rearrange("p (c n) -> p c n", c=4), [("xtb", i2)], [U("xbf")], sem=("xtbs", i2))
        S.fence()

        SCALE_A = 64.0 ** -0.5
        SCALE_B = 192.0 ** -0.5

        for l in range(L if upto not in ("0", "s", "s1", "s2") else 0):
            wl_in, wl_uq, wl_ukv = w_in[l], w_uq[l], w_ukv[l]
            PPl = PP[:, l, :]
            dma("sp", ESINK, esink_in[l], [], ["esink"])
            act(ESINK, ESINK, AF.Exp, ["esink"], ["esink"])
            dma("sp", BSBC, bsbc_in[l].rearrange("p (g t) -> p g t", g=8), [], ["bsbc"])
            dma("pool", WST, wst_in[l].rearrange("p (g t) -> p g t", g=8), [], ["wst"], nofence=False)
            for g in range(8):
                tt("dve", WST[:, g, :], WST[:, g, :], TRIB, ALU.mult, ["wst", "trib"], ["wst"])
            S.add("dve", lambda e: e.memset(HALO, 0.0), writes=["halo"])
            S.fence()

            for t in range(NTG):
                tok0 = t * TG
                nkb = 8 * (t + 1)
                dma("pool", XB, xbf[:, :, tok0:tok0 + TG].rearrange("c p n -> p c n"), [], ["xb"])
                TB = Temps(GEN + 8192, NF)
                POSI = f32v(GEN, 1024)
                POSF = f32v(GEN + 1024, 1024)
                dma("sp", POSI.bitcast(I32), pos_in[:, tok0:tok0 + TG], [], ["posi"])
                S1 = f32v(GEN + 2048, 1024)
                S2 = f32v(GEN + 3072, 1024)
                KI = f32v(GEN + 4096, 1024)
                cp("dve", POSF, POSI.bitcast(I32), ["posi"], ["posf"])
                ts("dve", POSF, POSF, INVF, float(1.0 / (2.0 * np.pi)), ALU.mult, ALU.mult, ["posf", "const"], ["posf"])
                cp("dve", KI.bitcast(I32), POSF, ["posf"], ["ki"])
                cp("dve", S1, KI.bitcast(I32), ["ki"], ["s1"])
                tt("dve", POSF, POSF, S1, ALU.subtract, ["posf", "s1"], ["posf"])
                act(S1, POSF, AF.Sin, ["posf"], ["s1"], scale=float(np.pi))
                act(S2, POSF, AF.Sin, ["posf"], ["s2"], scale=float(np.pi / 2))
                tt("dve", S2, S2, S2, ALU.mult, ["s2"], ["s2"])
                ts("dve", S2, S2, -2.0, 1.0, ALU.mult, ALU.add, ["s2"], ["s2"])
                stt("dve", SIN, S1, 2.0, S2, ALU.mult, ALU.mult, ["s1", "s2"], ["sin"])
                tt("dve", S1, S1, S1, ALU.mult, ["s1"], ["s1"])
                ts("dve", COS, S1, -2.0, 1.0, ALU.mult, ALU.add, ["s1"], ["cos"])
                ts("dve", SIN, SIN, SGN, None, ALU.mult, None, ["sin", "const"], ["sin"])

                RAW = TB.f32(4096).rearrange("p (c n) -> p c n", c=4)
                SQ = [TB.bf(1024) for _ in range(2)]
                CQN = TB.bf(4096).rearrange("p (c n) -> p c n", c=4)
                CKVN = TB.bf(4096).rearrange("p (c n) -> p c n", c=4)
                RINV = TB.f32(1024)
                QN = [TB.bf(2048).rearrange("p (h n) -> p h n", h=2) for _ in range(2)]
                QR = [TB.bf(1024) for _ in range(2)]
                KT = TB.bf(S_len)
                VT = TB.bf(S_len).rearrange("p (b d) -> p b d", d=128)
                PT = [TB.bf(512) for _ in range(3)]
                DEN = [TB.f32(512) for _ in range(2)]
                KST = [TB.bf(1024) for _ in range(2)]
                VST = [TB.bf(512).rearrange("p (h d) -> p h d", h=4) for _ in range(2)]
                TR1 = TB.f32(1024)
                TR2 = TB.f32(1024)

                def latent(col0, gcol, dst, tag):
                    for half in range(2):
                        wv, wk = load_w([(wl_in, col0 + half * 256, 256, 0)], 16)
                        for cc in range(2):
                            c = half * 2 + cc

                            def ev(j, ps, kp, c=c):
                                sl = slice(j * 512, (j + 1) * 512)
                                cp("dve", RAW[:, c, sl], ps[:, :], [kp], [("raw", c, j)])
                                i2 = (c * 2 + j) % 2
                                act(SQ[i2][:, 0:512], ps[:, :], AF.Square, [kp], [("sq", i2)])
                                S.add("pe", lambda e, i2=i2, j=j, c=c: e.matmul(psum[6 + j][:, :], lhsT=ONESB, rhs=SQ[i2][:, 0:512],
                                                                                 start=(c == 0), stop=(c == 3)),
                                      reads=[("sq", i2), "onesb"], writes=[("ps", 6 + j)])
                            gemm([wv[:, k, cc * 128:(cc + 1) * 128] for k in range(16)],
                                 lambda k, j: XB[:, k, j * 512:(j + 1) * 512], ["xb"], wk, ev, pairs=(0, 1, 2))
                    for j in range(2):
                        sl = slice(j * 512, (j + 1) * 512)
                        act(RINV[:, sl], psum[6 + j][:, :], AF.Sqrt, [("ps", 6 + j)], [("rinv", j)], bias=EPSC, scale=1.0 / 512.0)
                        S.add("dve", lambda e, sl=sl: e.reciprocal(out=RINV[:, sl], in_=RINV[:, sl]), reads=[("rinv", j)], writes=[("rinv", j)])
                        for c in range(4):
                            stt("dve", dst[:, c, sl], RAW[:, c, sl], PPl[:, gcol + c:gcol + c + 1], RINV[:, sl], ALU.mult, ALU.mult,
                                [("raw", c, j), ("rinv", j), "pp"], [(tag, j)])

                latent(1280, QNG, CQN, "cqn")
                latent(1792, KVNG, CKVN, "ckvn")
                wv, wk = load_w([(wl_in, 2304, 64, 0), (wl_in, 2304, 64, 64), (wl_in, 2336, 32, 128), (wl_in, 2304, 32, 160),
                                 (wl_in, 2336, 32, 192), (wl_in, 2304, 32, 224)], 16)

                def ev_kr0(j, ps, kp):
                    sl = slice(j * 512, (j + 1) * 512)
                    tt("dve", TR1[:, sl], ps[:, :], COS[:, sl], ALU.mult, [kp, "cos"], [("tr1", j)])

                def ev_kr1(j, ps, kp):
                    sl = slice(j * 512, (j + 1) * 512)
                    tt("dve", TR2[:, sl], ps[:, :], SIN[:, sl], ALU.mult, [kp, "sin"], [("tr2", j)])
                    tt("dve", KR[:, tok0 + j * 512: tok0 + (j + 1) * 512], TR1[:, sl], TR2[:, sl], ALU.add,
                       [("tr1", j), ("tr2", j)], [("kr", t, j)])
                gemm([wv[:, k, 0:128] for k in range(16)], lambda k, j: XB[:, k, j * 512:(j + 1) * 512], ["xb"], wk, ev_kr0, pairs=(0, 1, 2))
                gemm([wv[:, k, 128:256] for k in range(16)], lambda k, j: XB[:, k, j * 512:(j + 1) * 512], ["xb"], wk, ev_kr1, pairs=(0, 1, 2))

                ckeys = [("ckvn", 0), ("ckvn", 1)]
                for hg in range(4):
                    wv, wk = load_w([(wl_ukv, hg * 1024, 1024, 0)], 4)
                    for hh in range(4):
                        h = hg * 4 + hh

                        def ev_k(j, ps, kp, h=h):
                            i2 = h % 2
                            sl = slice(j * 512, (j + 1) * 512)
                            cp("act", KST[i2][:, sl], ps[:, :], [kp], [("kst", i2, j)])
                            if j == 1:
                                dma("sp", kcache[h, :, tok0:tok0 + TG], KST[i2], [("kst", i2, 0), ("kst", i2, 1)], [("kc", h)],
                                    sem=("ksts", i2))
                        gemm([wv[:, k, hh * 256:hh * 256 + 128] for k in range(4)],
                             lambda k, j: CKVN[:, k, j * 512:(j + 1) * 512], ckeys, wk, ev_k, pairs=(0, 1, 2))
                    for blk in range(8):
                        b = 6 + (blk % 2)
                        kp = ("ps", b)

                        def mmv(e, wv=wv, blk=blk, b=b):
                            for k in range(4):
                                inst = e.matmul(psum[b][:, :].rearrange("p (h d) -> p h d", h=4),
                                                lhsT=CKVN[:, k, blk * 128:(blk + 1) * 128],
                                                rhs=wv.rearrange("p k (h d) -> p k h d", h=4)[:, k, :, 128:256],
                                                start=(k == 0), stop=(k == 3))
                            return inst
                        S.add("pe", mmv, reads=ckeys + wk, writes=[kp])
                        i2 = blk % 2
                        cp("act", VST[i2], psum[b][:, :].rearrange("p (h d) -> p h d", h=4), [kp], [("vst", i2)])
                        dma("sp", vcache[hg * 4:(hg + 1) * 4, tok0 + blk * 128: tok0 + (blk + 1) * 128, :].rearrange("h t d -> t h d"),
                            VST[i2], [("vst", i2)], [("vc", hg * 4 + i) for i in range(4)], sem=("vsts", i2))

                for hp in range(8):
                    h0 = 2 * hp
                    qi = hp % 2
                    pieces = [(wl_uq, h0 * 192, 128, 0), (wl_uq, (h0 + 1) * 192, 128, 128),
                              (wl_uq, h0 * 192 + 128, 64, 256), (wl_uq, (h0 + 1) * 192 + 128, 64, 320),
                              (wl_uq, h0 * 192 + 160, 32, 384), (wl_uq, h0 * 192 + 128, 32, 416),
                              (wl_uq, (h0 + 1) * 192 + 160, 32, 448), (wl_uq, (h0 + 1) * 192 + 128, 32, 480)]
                    wv, wk = load_w(pieces, 4)
                    qkeys = [("cqn", 0), ("cqn", 1)]
                    for hh in range(2):
                        def ev_qn(j, ps, kp, hh=hh):
                            act(QN[qi][:, hh, j * 512:(j + 1) * 512], ps[:, :], AF.Identity, [kp], [("qn", qi, hh, j)], scale=SCALE_B)
                        gemm([wv[:, k, hh * 128:(hh + 1) * 128] for k in range(4)],
                             lambda k, j: CQN[:, k, j * 512:(j + 1) * 512], qkeys, wk, ev_qn, pairs=(0,))

                    def ev_q0(j, ps, kp):
                        sl = slice(j * 512, (j + 1) * 512)
                        stt("dve", TR1[:, sl], ps[:, :], SCALE_B, COS[:, sl], ALU.mult, ALU.mult, [kp, "cos"], [("tr1", j)])

                    def ev_q1(j, ps, kp):
                        sl = slice(j * 512, (j + 1) * 512)
                        stt("dve", TR2[:, sl], ps[:, :], SCALE_B, SIN[:, sl], ALU.mult, ALU.mult, [kp, "sin"], [("tr2", j)])
                        tt("dve", QR[qi][:, sl], TR1[:, sl], TR2[:, sl], ALU.add, [("tr1", j), ("tr2", j)], [("qr", qi, j)])
                    gemm([wv[:, k, 256:384] for k in range(4)], lambda k, j: CQN[:, k, j * 512:(j + 1) * 512], qkeys, wk, ev_q0, pairs=(0,))
                    gemm([wv[:, k, 384:512] for k in range(4)], lambda k, j: CQN[:, k, j * 512:(j + 1) * 512], qkeys, wk, ev_q1, pairs=(0,))

                    for hh in range(2):
                        h = h0 + hh
                        base = hh * 64
                        ntok = (t + 1) * TG
                        dma("pool", KT[:, 0:ntok], kcache[h, :, 0:ntok], [("kc", h)], ["kt"])
                        dma("pool", VT[:, 0:nkb, :], vcache[h, 0:ntok, :].rearrange("(b p) d -> p b d", p=128), [("vc", h)], ["vt"])
                        for j in range(2):
                            OB, DB = 2 + 2 * j, 3 + 2 * j
                            nk = 8 * t + 4 * (j + 1)
                            for kb in range(nk):
                                dloc = kb - (8 * t + 4 * j)
                                c0 = 128 * dloc if dloc > 0 else 0
                                qs = slice(j * 512 + c0, (j + 1) * 512)
                                ncol = 512 - c0
                                sb = 6 + (kb % 2)
                                pt = PT[kb % 3]
                                kpt = ("pt", kb % 3)

                                def mms(e, kb=kb, qs=qs, ncol=ncol, sb=sb, hh=hh, base=base, qi=qi):
                                    e.matmul(psum[sb][:, 0:ncol], lhsT=KT[:, kb * 128:(kb + 1) * 128], rhs=QN[qi][:, hh, qs],
                                             start=True, stop=False)
                                    return e.matmul(psum[sb][:, 0:ncol], lhsT=KR[base:base + 64, kb * 128:(kb + 1) * 128],
                                                    rhs=QR[qi][base:base + 64, qs], start=False, stop=True)
                                krk = [("kr", tt_, jj) for tt_ in range(t + 1) for jj in range(2)]
                                S.add("pe", mms, reads=["kt", ("qn", qi, hh, j), ("qr", qi, j)] + krk, writes=[("ps", sb)])
                                act(pt[:, 0:ncol], psum[sb][:, 0:ncol], AF.Exp, [("ps", sb)], [kpt])
                                if dloc >= 0:
                                    tt("dve", pt[:, 0:128], pt[:, 0:128], TRIB, ALU.mult, [kpt, "trib"], [kpt])

                                def mmo(e, kb=kb, c0=c0, ncol=ncol, pt=pt, OB=OB, DB=DB, nk=nk):
                                    e.matmul(psum[OB][:, c0:512], lhsT=VT[:, kb, :], rhs=pt[:, 0:ncol], start=(kb == 0), stop=(kb == nk - 1))
                                    return e.matmul(psum[DB][:, c0:512], lhsT=ONESB, rhs=pt[:, 0:ncol], start=(kb == 0), stop=(kb == nk - 1))
                                S.add("pe", mmo, reads=["vt", kpt, "onesb"], writes=[("ps", OB), ("ps", DB)])
                            dn = DEN[j]
                            S.add("dve", lambda e, dn=dn, DB=DB: e.reciprocal(out=dn, in_=psum[DB][:, :]), reads=[("ps", DB)], writes=[("den", j)])
                            tt("dve", YB[:, h, j * 512:(j + 1) * 512], psum[OB][:, :], dn, ALU.mult, [("ps", OB), ("den", j)], [("yb", h)])
                S.fence()

                if upto == "B":
                    continue
                TA = Temps(GEN + 12288, NF)
                QA = TA.bf(8192).rearrange("p (c n) -> p c n", c=8)
                KA = TA.bf(2 * 1152).rearrange("p (g n) -> p g n", g=2)
                VA = TA.bf(9 * 256).rearrange("p (b g d) -> p b g d", b=9, g=2)
                PTA = [TA.bf(512) for _ in range(3)]
                DNA = [TA.f32(512) for _ in range(2)]
                for q in range(4):
                    wv, wk = load_w([(wl_in, q * 256, 256, 0)], 16)
                    for cc in range(2):
                        c = q * 2 + cc

                        def ev(j, ps, kp, c=c):
                            act(QA[:, c, j * 512:(j + 1) * 512], ps[:, :], AF.Identity, [kp], [("qa", c)], scale=SCALE_A)
                        gemm([wv[:, k, cc * 128:(cc + 1) * 128] for k in range(16)], lambda k, j: XB[:, k, j * 512:(j + 1) * 512],
                             ["xb"], wk, ev)
                wv, wk = load_w([(wl_in, 1024, 64, 0), (wl_in, 1024, 64, 64), (wl_in, 1088, 64, 128), (wl_in, 1088, 64, 192)], 16)
                wv2, wk2 = load_w([(wl_in, 1152, 64, 0), (wl_in, 1152, 64, 64), (wl_in, 1216, 64, 128), (wl_in, 1216, 64, 192)], 16)
                XH = TA.bf(16 * 128).rearrange("p (c n) -> p c n", c=16)
                if t > 0:
                    dma("pool", XH, xbf[:, :, tok0 - 128:tok0].rearrange("c p n -> p c n"), [], ["xh"])
                for g in range(2):
                    def ev(j, ps, kp, g=g):
                        cp("act", KA[:, g, 128 + j * 512:128 + (j + 1) * 512], ps[:, :], [kp], [("ka", g)])
                    gemm([wv[:, k, g * 128:(g + 1) * 128] for k in range(16)], lambda k, j: XB[:, k, j * 512:(j + 1) * 512],
                         ["xb"], wk, ev)
                if t > 0:
                    def mmh(e, wv=wv):
                        for g in range(2):
                            for k in range(16):
                                inst = e.matmul(psum[0][:, g * 128:(g + 1) * 128], lhsT=wv[:, k, g * 128:(g + 1) * 128], rhs=XH[:, k, :],
                                                start=(k == 0), stop=(k == 15))
                        return inst
                    S.add("pe", mmh, reads=wk + ["xh"], writes=[("ps", 0)])
                    cp("act", KA[:, :, 0:128], psum[0][:, 0:256].rearrange("p (g n) -> p g n", g=2), [("ps", 0)], [("ka", 0), ("ka", 1)])
                for blk in range(9):
                    if blk == 0 and t == 0:
                        continue
                    b = 6 + (blk % 2)
                    src = (lambda k: XH[:, k, :]) if blk == 0 else (lambda k, blk=blk: XB[:, k, (blk - 1) * 128:blk * 128])

                    def mmv(e, src=src, b=b, wv2=wv2):
                        for k in range(16):
                            inst = e.matmul(psum[b][:, 0:256], lhsT=src(k), rhs=wv2[:, k, 0:256], start=(k == 0), stop=(k == 15))
                        return inst
                    S.add("pe", mmv, reads=wk2 + ["xb", "xh"], writes=[("ps", b)])
                    cp("act", VA[:, blk, :, :], psum[b][:, 0:256].rearrange("p (g d) -> p g d", g=2), [("ps", b)], [("va", blk)])
                qakeys = [("qa", c) for c in range(8)]
                it = 0
                for n in range(8):
                    for g in range(2):
                        for par in range(2):
                            base = par * 64
                            OB, DB = 2 + 2 * (it % 2), 3 + 2 * (it % 2)
                            it += 1
                            ms = [1] if (t == 0 and n == 0) else [0, 1]
                            for mi, m in enumerate(ms):
                                sb = 6 + (m % 2)
                                pt = PTA[(it + m) % 3]
                                kpt = ("pta", (it + m) % 3)

                                def mms(e, n=n, g=g, base=base, m=m, sb=sb):
                                    return e.matmul(psum[sb][:, :].rearrange("p (i q) -> p i q", i=4),
                                                    lhsT=KA[base:base + 64, g, (n + m) * 128:(n + m + 1) * 128],
                                                    rhs=QA[base:base + 64, 4 * g:4 * g + 4, n * 128:(n + 1) * 128], start=True, stop=True)
                                S.add("pe", mms, reads=[("ka", g)] + qakeys, writes=[("ps", sb)])
                                act(pt, psum[sb][:, :], AF.Exp, [("ps", sb)], [kpt])
                                msk = TRIB if m == 1 else TRIPB
                                tt("dve", pt.rearrange("p (i q) -> p i q", i=4), pt.rearrange("p (i q) -> p i q", i=4),
                                   msk.unsqueeze(1).broadcast_to([128, 4, 128]), ALU.mult, [kpt, "trib", "tripb"], [kpt])

                                def mmo(e, n=n, g=g, m=m, pt=pt, OB=OB, DB=DB, first=(mi == 0), last=(mi == len(ms) - 1)):
                                    e.matmul(psum[OB][:, :], lhsT=VA[:, n + m, g, :], rhs=pt, start=first, stop=last)
                                    return e.matmul(psum[DB][:, :], lhsT=ONESB, rhs=pt, start=first, stop=last)
                                S.add("pe", mmo, reads=[("va", n + m), kpt, "onesb"], writes=[("ps", OB), ("ps", DB)])
                            dn = DNA[it % 2]
                            kdn = ("dna", it % 2)
                            hsl = slice(8 * g + par, 8 * g + 8, 2)
                            tt("dve", dn.rearrange("p (i q) -> p i q", i=4), psum[DB][:, :].rearrange("p (i q) -> p i q", i=4),
                               ESINK[:, hsl].unsqueeze(2).broadcast_to([128, 4, 128]), ALU.add, [("ps", DB), "esink"], [kdn])
                            S.add("dve", lambda e, dn=dn: e.reciprocal(out=dn, in_=dn), reads=[kdn], writes=[kdn])
                            tt("dve", YA[base:base + 64, 4 * g:4 * g + 4, n * 128:(n + 1) * 128],
                               psum[OB][base:base + 64, :].rearrange("p (i q) -> p i q", i=4),
                               dn[base:base + 64, :].rearrange("p (i q) -> p i q", i=4), ALU.mult, [("ps", OB), kdn], [("ya", n, g, par)])
                S.fence()

                if upto == "A":
                    continue
                TC = Temps(GEN + 16384, NF)
                VG = TC.f32(8192).rearrange("p (c n) -> p c n", c=8)
                VNT = VG.rearrange("p c n -> p (c n)").bitcast(BF16)[:, 0:8192].rearrange("p (g b c) -> p g b c", g=8, b=8)
                VN = TC.bf(8192).rearrange("p (c n) -> p c n", c=8)
                TB2 = [TC.bf(1024)] * 2
                TS2 = [TC.bf(1024)] * 2
                MEAN = COS
                RSTD = SIN
                UG = [TC.bf(1024)] * 2
                TMPC = TC.f32(512)
                for q in range(4):
                    wv, wk = load_w([(wl_in, 3392 + q * 256, 256, 0)], 16)
                    for cc in range(2):
                        c = q * 2 + cc

                        def ev(j, ps, kp, c=c):
                            sl = slice(j * 512, (j + 1) * 512)
                            i2 = 0
                            act(VG[:, c, sl], ps[:, :], AF.Gelu, [kp], [("vg", c, j)])
                            cp("dve", TB2[i2][:, sl], VG[:, c, sl], [("vg", c, j)], [("tb2", i2, j)])
                            tt("dve", TS2[i2][:, sl], VG[:, c, sl], VG[:, c, sl], ALU.mult, [("vg", c, j)], [("ts2", i2, j)])

                            def mst(e, i2=i2, j=j, c=c, sl=sl):
                                e.matmul(psum[4 + j][:, :], lhsT=ONESB, rhs=TB2[i2][:, sl], start=(c == 0), stop=(c == 7))
                                return e.matmul(psum[6 + j][:, :], lhsT=ONESB, rhs=TS2[i2][:, sl], start=(c == 0), stop=(c == 7))
                            S.add("pe", mst, reads=[("tb2", i2, j), ("ts2", i2, j), "onesb"], writes=[("ps", 4 + j), ("ps", 6 + j)])
                        gemm([wv[:, k, cc * 128:(cc + 1) * 128] for k in range(16)], lambda k, j: XB[:, k, j * 512:(j + 1) * 512],
                             ["xb"], wk, ev, pairs=(0, 1))
                for j in range(2):
                    sl = slice(j * 512, (j + 1) * 512)
                    ts("dve", MEAN[:, sl], psum[4 + j][:, :], 1.0 / 1024.0, None, ALU.mult, None, [("ps", 4 + j)], [("mean", j)])
                    tt("dve", TMPC, MEAN[:, sl], MEAN[:, sl], ALU.mult, [("mean", j)], ["tmpc"])
                    stt("dve", RSTD[:, sl], psum[6 + j][:, :], 1.0 / 1024.0, TMPC, ALU.mult, ALU.subtract, [("ps", 6 + j), "tmpc"], [("rstd", j)])
                    act(RSTD[:, sl], RSTD[:, sl], AF.Sqrt, [("rstd", j)], [("rstd", j)], bias=EPSC)
                    S.add("dve", lambda e, sl=sl, RSTD=RSTD: e.reciprocal(out=RSTD[:, sl], in_=RSTD[:, sl]), reads=[("rstd", j)], writes=[("rstd", j)])
                    for c in range(8):
                        tt("dve", VG[:, c, sl], VG[:, c, sl], MEAN[:, sl], ALU.subtract, [("vg", c, j), ("mean", j)], [("vg", c, j)])
                        tt("dve", VG[:, c, sl], VG[:, c, sl], RSTD[:, sl], ALU.mult, [("vg", c, j), ("rstd", j)], [("vg", c, j)])
                        act(VN[:, c, sl], VG[:, c, sl], AF.Identity, [("vg", c, j), "pp"], [("vn", c, j)],
                            bias=PPl[:, SLB + c:SLB + c + 1], scale=PPl[:, SLG + c:SLG + c + 1])
                S.fence()
                PSB = [psum[b][:, :].bitcast(BF16) for b in range(8)]
                for g in range(8):
                    for half in range(2):
                        b = (g * 2 + half) % 2

                        def trn(e, g=g, half=half, b=b):
                            for i in range(4):
                                blk = half * 4 + i
                                inst = e.transpose(PSB[b][:, i * 128:(i + 1) * 128], VN[:, g, blk * 128:(blk + 1) * 128], IDB)
                            return inst
                        S.add("pe", trn, reads=[("vn", g, half), "idb"], writes=[("ps", b)])
                        cp("act", VNT[:, g, half * 4:half * 4 + 4, :], PSB[b][:, 0:512].rearrange("p (b c) -> p b c", b=4),
                           [("ps", b)], [("vnt", g, half)])
                for q in range(4):
                    wv, wk = load_w([(wl_in, 2368 + q * 256, 256, 0)], 16)
                    for cc in range(2):
                        g = q * 2 + cc
                        ug = UG[g % 2]

                        def ev(j, ps, kp, g=g, ug=ug):
                            act(ug[:, j * 512:(j + 1) * 512], ps[:, :], AF.Gelu, [kp], [("ug", 0, j)])
                        gemm([wv[:, k, cc * 128:(cc + 1) * 128] for k in range(16)], lambda k, j: XB[:, k, j * 512:(j + 1) * 512],
                             ["xb"], wk, ev, pairs=(1, 2))
                        for half in range(2):
                            b = 6 + half

                            def mix(e, g=g, half=half, b=b):
                                for i in range(4):
                                    inst = e.matmul(psum[b][:, i * 128:(i + 1) * 128], lhsT=VNT[:, g, half * 4 + i, :], rhs=WST[:, g, :],
                                                    start=True, stop=True)
                                return inst
                            S.add("pe", mix, reads=[("vnt", g, half), "wst"], writes=[("ps", b)])
                            tt("dve", TMPC.rearrange("p (b t) -> p b t", b=4), psum[b][:, :].rearrange("p (b t) -> p b t", b=4),
                               BSBC[:, g, :].unsqueeze(1).broadcast_to([128, 4, 128]), ALU.add, [("ps", b), "bsbc"], ["tmpc"])
                            tt("dve", YC[:, g, half * 512:(half + 1) * 512], TMPC, ug[:, half * 512:(half + 1) * 512], ALU.mult,
                               ["tmpc", ("ug", 0, half)], [("yc", g)])
                S.fence()

                if upto == "C":
                    continue
                TD = Temps(GEN + 16384, NF)
                MERGED = TD.bf(16384).rearrange("p (c n) -> p c n", c=16)
                GT = [TD.f32(1024) for _ in range(2)]
                MTS = [TD.f32(1024) for _ in range(2)]
                TT_ = TD.f32(1024)
                branches = [(w_pa[l], 8, YA, 0), (w_pb[l], 16, YB, 1), (w_pc[l], 8, YC, 2)]
                ykeys_all = {0: [("ya", n, g, p) for n in range(8) for g in range(2) for p in range(2)],
                             1: [("yb", h) for h in range(16)], 2: [("yc", g) for g in range(8)]}
                for jp in range(8):
                    for i, (wpr, kcn, Y, bi) in enumerate(branches):
                        gv, gk = load_w([(wl_in, 4416 + i * 2048 + jp * 256, 256, 0)], 16)
                        pv, pk = load_w([(wpr, jp * 256, 256, 0)], kcn)
                        for cc in range(2):
                            jd = jp * 2 + cc
                            gt = GT[cc]
                            mt = MTS[cc]

                            def ev_g(j, ps, kp, gt=gt, i=i, jd=jd, cc=cc):
                                act(gt[:, j * 512:(j + 1) * 512], ps[:, :], AF.Sigmoid, [kp, "pp"], [("gt", cc, j)],
                                    bias=PPl[:, BG + i * 16 + jd:BG + i * 16 + jd + 1])
                            gemm([gv[:, k, cc * 128:(cc + 1) * 128] for k in range(16)], lambda k, j: XB[:, k, j * 512:(j + 1) * 512],
                                 ["xb"], gk, ev_g)

                            def ev_p(j, ps, kp, gt=gt, mt=mt, i=i, jd=jd, cc=cc):
                                sl = slice(j * 512, (j + 1) * 512)
                                if i == 0:
                                    tt("dve", mt[:, sl], ps[:, :], gt[:, sl], ALU.mult, [kp, ("gt", cc, j)], [("mt", cc, j)])
                                elif i == 1:
                                    tt("dve", TT_[:, sl], ps[:, :], gt[:, sl], ALU.mult, [kp, ("gt", cc, j)], [("tt", j)])
                                    tt("dve", mt[:, sl], mt[:, sl], TT_[:, sl], ALU.add, [("mt", cc, j), ("tt", j)], [("mt", cc, j)])
                                else:
                                    tt("dve", TT_[:, sl], ps[:, :], gt[:, sl], ALU.mult, [kp, ("gt", cc, j)], [("tt", j)])
                                    tt("dve", MERGED[:, jd, sl], mt[:, sl], TT_[:, sl], ALU.add, [("mt", cc, j), ("tt", j)], [("merged", jd)])
                            gemm([pv[:, k, cc * 128:(cc + 1) * 128] for k in range(kcn)],
                                 lambda k, j, Y=Y: Y[:, k, j * 512:(j + 1) * 512], ykeys_all[i], pk, ev_p)
                S.fence()

                if upto == "D":
                    continue
                def ln_stage(TT, TT2, nK, lhs_groups_fn, rhs_fn, rkeys, gcol, bcol, final):
                    XR = [TT.f32(1024) for _ in range(3)]
                    ZB = [TT.bf(1024) for _ in range(2)]
                    ZS = [TT.bf(1024) for _ in range(2)]
                    MEAN = TT.f32(1024)
                    RSTD = TT.f32(1024)
                    TMP = TT2.f32(512)
                    XO = [TT2.bf(1024) for _ in range(2)]
                    for jd in range(16):
                        xr = XR[jd % 3]
                        kxr = ("xr", jd % 3)
                        dma("pool", xr, xres[jd, :, tok0:tok0 + TG], [], [kxr])
                        lhs, wk = lhs_groups_fn(jd)

                        def ev(j, ps, kp, jd=jd, xr=xr, kxr=kxr):
                            sl = slice(j * 512, (j + 1) * 512)
                            i2 = jd % 2
                            stt("dve", xr[:, sl], xr[:, sl], ALPHA, ps[:, :], ALU.mult, ALU.add, [kp, kxr], [kxr])
                            cp("act", ZB[i2][:, sl], xr[:, sl], [kxr], [("zb", i2, j)])
                            act(ZS[i2][:, sl], xr[:, sl], AF.Square, [kxr], [("zs", i2, j)])

                            def mst(e, i2=i2, j=j, jd=jd, sl=sl):
                                e.matmul(psum[4 + j][:, :], lhsT=ONESB, rhs=ZB[i2][:, sl], start=(jd == 0), stop=(jd == 15))
                                return e.matmul(psum[6 + j][:, :], lhsT=ONESB, rhs=ZS[i2][:, sl], start=(jd == 0), stop=(jd == 15))
                            S.add("pe", mst, reads=[("zb", i2, j), ("zs", i2, j), "onesb"], writes=[("ps", 4 + j), ("ps", 6 + j)])
                            if j == 1:
                                dma("sp", zscr[jd, :, :], xr, [kxr], [("zscr", jd)], sem=("zst", jd % 3))
                        gemm(lhs, rhs_fn, rkeys, wk, ev, pairs=(0, 1))
                    for j in range(2):
                        sl = slice(j * 512, (j + 1) * 512)
                        ts("dve", MEAN[:, sl], psum[4 + j][:, :], 1.0 / D, None, ALU.mult, None, [("ps", 4 + j)], [("mean", j)])
                        tt("dve", TMP, MEAN[:, sl], MEAN[:, sl], ALU.mult, [("mean", j)], ["tmp"])
                        stt("dve", RSTD[:, sl], psum[6 + j][:, :], 1.0 / D, TMP, ALU.mult, ALU.subtract, [("ps", 6 + j), "tmp"], [("rstd", j)])
                        act(RSTD[:, sl], RSTD[:, sl], AF.Sqrt, [("rstd", j)], [("rstd", j)], bias=EPSC)
                        S.add("dve", lambda e, sl=sl, RSTD=RSTD: e.reciprocal(out=RSTD[:, sl], in_=RSTD[:, sl]), reads=[("rstd", j)], writes=[("rstd", j)])
                    mk = [("mean", 0), ("mean", 1), ("rstd", 0), ("rstd", 1)]
                    for jd in range(16):
                        xr = XR[jd % 3]
                        kxr = ("xr", jd % 3)
                        dma("pool", xr, zscr[jd, :, :], [("zscr", jd)], [kxr])
                        tt("dve", xr, xr, MEAN, ALU.subtract, [kxr] + mk, [kxr])
                        tt("dve", xr, xr, RSTD, ALU.mult, [kxr] + mk, [kxr])
                        act(xr, xr, AF.Identity, [kxr, "pp"], [kxr], bias=PPl[:, bcol + jd:bcol + jd + 1], scale=PPl[:, gcol + jd:gcol + jd + 1])
                        if not final:
                            dma("sp", xres[jd, :, tok0:tok0 + TG], xr, [kxr], [U("xres")], sem=("xrs", jd % 3))
                            xo = XO[jd % 2]
                            cp("dve", xo, xr, [kxr], [("xo", jd % 2)])
                            dma("sp", xbf[jd, :, tok0:tok0 + TG], xo, [("xo", jd % 2)], [U("xbf")], sem=("xos", jd % 2))
                        else:
                            dma("sp", xres[jd, :, tok0:tok0 + TG], xr, [kxr], [("xresf", jd)], sem=("xrs", jd % 3))

                TE = Temps(GEN, GEN + 16384)

                def lhs_wo(jd, cache={}):
                    jp = jd // 2
                    if jp not in cache:
                        cache.clear()
                        cache[jp] = load_w([(w_o[l], jp * 256, 256, 0)], 16)
                    wv, wk = cache[jp]
                    cc = jd % 2
                    return [wv[:, k, cc * 128:(cc + 1) * 128] for k in range(16)], wk
                ln_stage(TE, TE, 16, lhs_wo, lambda k, j: MERGED[:, k, j * 512:(j + 1) * 512], [("merged", c) for c in range(16)], L1G, L1B, False)
                S.fence()
                dma("pool", XB, xbf[:, :, tok0:tok0 + TG].rearrange("c p n -> p c n"), [], ["xb"])

                if upto == "E":
                    continue
                TF = Temps(GEN, NF)
                ACTT = TF.bf(44 * 1024).rearrange("p (c n) -> p c n", c=44)
                UP = [[TF.f32(1026) for _ in range(2)] for _ in range(2)]
                ACC = [TF.f32(1024) for _ in range(2)]
                SG = TF.f32(1024)
                for c in range(44):
                    wv, wk = load_w([(w_up[l], c * 128, 128, 0), (w_up[l], DFF + c * 128, 128, 128)], 16)
                    for hv in range(2):
                        up = UP[hv][c % 2]
                        kup = ("up", hv, c % 2)
                        cc = c + 44 * hv
                        cp("act", up[:, 0:2], HALO[:, cc, :], ["halo"], [(kup, "h")])

                        def ev(j, ps, kp, up=up, kup=kup):
                            cp("act", up[:, 2 + j * 512:2 + (j + 1) * 512], ps[:, :], [kp], [(kup, j)])
                        gemm([wv[:, k, hv * 128:(hv + 1) * 128] for k in range(16)], lambda k, j: XB[:, k, j * 512:(j + 1) * 512],
                             ["xb"], wk, ev)
                        ku = [(kup, "h"), (kup, 0), (kup, 1)]
                        acc = ACC[hv]
                        ka = ("acc", hv)
                        ts("dve", acc, up[:, 2:1026], PPl[:, CW + 2 * 88 + cc:CW + 2 * 88 + cc + 1], PPl[:, CB + cc:CB + cc + 1],
                           ALU.mult, ALU.add, ku + ["pp"], [ka])
                        stt("dve", acc, up[:, 1:1025], PPl[:, CW + 88 + cc:CW + 88 + cc + 1], acc, ALU.mult, ALU.add, ku + [ka, "pp"], [ka])
                        stt("dve", acc, up[:, 0:1024], PPl[:, CW + cc:CW + cc + 1], acc, ALU.mult, ALU.add, ku + [ka, "pp"], [ka])
                        cp("act", HALO[:, cc, :], up[:, 1024:1026], ku, ["halo"])
                    act(SG, ACC[0], AF.Silu, [("acc", 0)], ["sg"])
                    tt("dve", ACTT[:, c, :], SG, ACC[1], ALU.mult, ["sg", ("acc", 1)], [("actt", c)])
                S.fence()

                if upto == "F":
                    continue
                TGm = Temps(GEN + 22528, NF)
                last = (l == L - 1)

                def lhs_wd(jd):
                    a = load_w([(w_down[l][0:2816, :], jd * 128, 128, 0)], 22)
                    b = load_w([(w_down[l][2816:5632, :], jd * 128, 128, 0)], 22)
                    return [a[0][:, k, :] for k in range(22)] + [b[0][:, k, :] for k in range(22)], a[1] + b[1]
                ln_stage(TGm, Temps(XB_off, XB_off + 2048), 44, lhs_wd, lambda k, j: ACTT[:, k, j * 512:(j + 1) * 512], [("actt", c) for c in range(44)], L2G, L2B, last)
                if last:
                    TO = Temps(XB_off + 2048, XB_off + 8192)
                    XF = [TO.f32(512) for _ in range(2)]
                    OS = [TO.f32(2048) for _ in range(2)]
                    for blk in range(8):
                        osb = OS[blk % 2]
                        for g in range(4):
                            xf = XF[g % 2]
                            kxf = ("xf", g % 2)
                            dma("pool", xf.rearrange("p (c n) -> p c n", c=4),
                                xres[g * 4:(g + 1) * 4, :, tok0 + blk * 128: tok0 + (blk + 1) * 128].rearrange("c p n -> p c n"),
                                [("xresf", g * 4 + i) for i in range(4)], [kxf])
                            b = (blk * 4 + g) % 4

                            def tr(e, xf=xf, b=b):
                                for i in range(4):
                                    inst = e.transpose(psum[b][:, i * 128:(i + 1) * 128], xf[:, i * 128:(i + 1) * 128], IDENT)
                                return inst
                            S.add("pe", tr, reads=[kxf, "const"], writes=[("ps", b)])
                            cp("act" if g % 2 else "dve", osb[:, g * 512:(g + 1) * 512], psum[b][:, :], [("ps", b)], [("os", blk % 2, g)])
                        dma("sp", out[tok0 + blk * 128: tok0 + (blk + 1) * 128, :], osb, [("os", blk % 2, g) for g in range(4)],
                            [U("out")], sem=("oss", blk % 2))
                S.fence()
        S.fence()
        global LAST_SCHED
        LAST_SCHED = S
        S.emit(nc, st)
    return nc


def _pack_cols(v):
    v = np.asarray(v, np.float32).reshape(-1)
    return np.ascontiguousarray(v.reshape(-1, 128).T)


def _host_prep(inputs, L=2):
    c = np.zeros((128, 640), np.float32)
    c[:, 0:128] = np.eye(128, dtype=np.float32)
    k = np.arange(128)[:, None]
    q = np.arange(128)[None, :]
    c[:, 128:256] = (k <= q)
    c[:, 256:384] = (k > q)
    inv = (10000.0 ** (-np.arange(0, 64, 2, dtype=np.float32) / 64.0)).astype(np.float32)
    p = np.arange(128)
    c[:, 384] = inv[p % 32]
    c[:, 385] = np.where((p % 64) < 32, -1.0, 1.0)
    c[:, 386] = EPS
    pp = np.zeros((L, 128, NPP), np.float32)
    for l in range(L):
        pp[l, :, BG:BG + 48] = _pack_cols(inputs["b_gate"][l])
        pp[l, :, QNG:QNG + 4] = _pack_cols(inputs["q_norm_g"][l])
        pp[l, :, KVNG:KVNG + 4] = _pack_cols(inputs["kv_norm_g"][l])
        pp[l, :, SLG:SLG + 8] = _pack_cols(inputs["sgu_ln_g"][l])
        pp[l, :, SLB:SLB + 8] = _pack_cols(inputs["sgu_ln_b"][l])
        pp[l, :, L1G:L1G + 16] = _pack_cols(inputs["ln1_g"][l])
        pp[l, :, L1B:L1B + 16] = _pack_cols(inputs["ln1_b"][l])
        pp[l, :, L2G:L2G + 16] = _pack_cols(inputs["ln2_g"][l])
        pp[l, :, L2B:L2B + 16] = _pack_cols(inputs["ln2_b"][l])
        pp[l, :, CW:CW + 264] = _pack_cols(inputs["conv_w"][l])
        pp[l, :, CB:CB + 88] = _pack_cols(inputs["conv_b"][l])
    esink = np.ascontiguousarray(np.broadcast_to(np.asarray(inputs["sinks"], np.float32)[:L, None, :], (L, 128, 16)))
    bsbc = np.ascontiguousarray(np.broadcast_to(np.asarray(inputs["sgu_b"], np.float32)[:L].reshape(L, 1, 1024), (L, 128, 1024)))
    wst = np.ascontiguousarray(np.asarray(inputs["sgu_w"], np.float32)[:L].transpose(0, 3, 1, 2).reshape(L, 128, 1024))
    shared = {"consts": c, "pp": pp, "esink": esink, "bsbc": bsbc, "wst": wst}
    for k_ in ("w_in", "w_uq", "w_ukv", "w_proj_a", "w_proj_b", "w_proj_c", "w_o", "w_up", "w_down"):
        shared[k_] = np.ascontiguousarray(np.asarray(inputs[k_], np.float32)[:L])
    return shared


_NC_CACHE = {}


def kernel(**inputs):
    x = np.asarray(inputs["x"], np.float32)
    pos = np.asarray(inputs["positions"], np.int32)
    B, S_len, _ = x.shape
    L = 2
    key = (S_len, L)
    if key not in _NC_CACHE:
        _NC_CACHE[key] = build(S_len, L)
    nc = _NC_CACHE[key]
    shared = _host_prep(inputs, L)
    in_maps = []
    for b in range(B):
        m = dict(shared)
        m["x"] = np.ascontiguousarray(x[b])
        m["pos_bc"] = np.ascontiguousarray(np.broadcast_to(pos[b][None, :], (128, S_len)))
        in_maps.append(m)
    res = run_bass_kernel_spmd(nc, in_maps, core_ids=list(range(B)))
    return np.stack([np.asarray(r["out"], np.float32) for r in res.results], axis=0)
```

```python
from contextlib import ExitStack
import numpy as np
import concourse.bass as bass
import concourse.mybir as mybir
from concourse.bass_utils import run_bass_kernel_spmd

F32 = mybir.dt.float32
BF16 = mybir.dt.bfloat16
I32 = mybir.dt.int32
AF = mybir.ActivationFunctionType
ALU = mybir.AluOpType

D = 2048
NIN = 10560
DFF = 5632
TG = 1024
EPS = 1e-5
ALPHA = 4.0 ** 0.25
ENGS = ("pe", "act", "dve", "pool", "sp")
FENCED = ("pe", "act", "dve", "sp", "pool")
DMA_ENG = {}

BG, QNG, KVNG, SLG, SLB, L1G, L1B, L2G, L2B, CW, CB, NPP = 0, 48, 52, 56, 64, 72, 88, 104, 120, 136, 400, 488


class _Op:
    __slots__ = ("eng", "fn", "deps", "idx", "eidx", "is_dma", "semkey", "ordinal", "signal", "tick")


class Sched:
    def __init__(self):
        self.ops = []
        self.eng_ops = {e: [] for e in ENGS}
        self.last_w = {}
        self.readers = {}
        self.dma_count = {}
        self.dma_since_fence = []

    def add(self, eng, fn, reads=(), writes=(), dma=False, sem=None, nofence=False):
        op = _Op()
        op.eng, op.fn, op.is_dma = eng, fn, dma
        op.idx = len(self.ops)
        op.eidx = len(self.eng_ops[eng])
        op.signal = False
        op.tick = 0
        ps_reads = [k for k in reads if isinstance(k, tuple) and k and k[0] == "ps"]
        if ps_reads:
            reads = [k for k in reads if k not in ps_reads]
            writes = list(writes) + ps_reads
        deps = set()
        for k in reads:
            w = self.last_w.get(k)
            if w is not None:
                deps.add(w)
        for k in writes:
            w = self.last_w.get(k)
            if w is not None:
                deps.add(w)
            for r in self.readers.get(k, ()):
                deps.add(r)
        op.deps = deps
        for k in writes:
            self.last_w[k] = op
            self.readers[k] = []
        for k in reads:
            lst = self.readers.setdefault(k, [])
            if not dma:
                lst[:] = [r for r in lst if r.is_dma or r.eng != eng]
            lst.append(op)
        if dma:
            if sem is None:
                sem = ("dma", tuple(writes)[0] if writes else tuple(reads)[0])
            op.semkey = sem
            n = self.dma_count.get(sem, 0) + 1
            self.dma_count[sem] = n
            op.ordinal = n
            if not nofence:
                self.dma_since_fence.append(op)
        else:
            op.semkey = None
            op.ordinal = 0
        self.ops.append(op)
        self.eng_ops[eng].append(op)
        return op

    def fence(self):
        lasts = []
        for e in FENCED:
            for o in reversed(self.eng_ops[e]):
                if o.fn is not None and not o.is_dma:
                    lasts.append(o)
                    break
        dmas = list(self.dma_since_fence)
        self.dma_since_fence = []
        for e in FENCED:
            op = self.add(e, None)
            op.deps = set(o for o in lasts if o.eng != e) | set(dmas)

    def _needs_wait(self, op, d):
        if d.is_dma or d.eng != op.eng:
            return True
        return op.eng != "pe"

    def emit(self, nc, stack):
        for op in self.ops:
            for d in op.deps:
                if self._needs_wait(op, d):
                    d.signal = True
        for e in ENGS:
            t = 0
            for op in self.eng_ops[e]:
                if op.is_dma:
                    continue
                if op.signal:
                    t += 1
                op.tick = t
        eng_sem = {e: stack.enter_context(nc.semaphore("s_" + e)) for e in ENGS}
        dma_sem = {}
        for k in self.dma_count:
            dma_sem[k] = stack.enter_context(nc.semaphore("d%d" % len(dma_sem)))
        block = stack.enter_context(nc.Block())

        def run(e, eng):
            known = {}
            for op in self.eng_ops[e]:
                need = {}
                for d in op.deps:
                    if not self._needs_wait(op, d):
                        continue
                    if d.is_dma:
                        s, v = dma_sem[d.semkey], 16 * d.ordinal
                    else:
                        s, v = eng_sem[d.eng], d.tick
                    if need.get(s, 0) < v:
                        need[s] = v
                for s, v in need.items():
                    if known.get(s, 0) >= v:
                        continue
                    known[s] = v
                    eng.wait_ge(s, v)
                if op.fn is None:
                    continue
                inst = op.fn(eng)
                if op.is_dma:
                    inst.then_inc(dma_sem[op.semkey], 16)
                elif op.signal:
                    inst.then_inc(eng_sem[e], 1)

        block.tensor(lambda eng: run("pe", eng))
        block.scalar(lambda eng: run("act", eng))
        block.vector(lambda eng: run("dve", eng))
        block.gpsimd(lambda eng: run("pool", eng))
        block.sync(lambda eng: run("sp", eng))


def build(S_len=4096, L=2, upto=None):
    nc = bass.Bass("TRN2", target_bir_lowering=False)
    NTG = S_len // TG
    dt = lambda name, shape, dtype, kind: nc.dram_tensor(name, shape, dtype, kind=kind).ap()
    x_in = dt("x", [S_len, D], F32, "ExternalInput")
    pos_in = dt("pos_bc", [128, S_len], I32, "ExternalInput")
    consts_in = dt("consts", [128, 640], F32, "ExternalInput")
    pp_in = dt("pp", [L, 128, NPP], F32, "ExternalInput")
    esink_in = dt("esink", [L, 128, 16], F32, "ExternalInput")
    bsbc_in = dt("bsbc", [L, 128, 1024], F32, "ExternalInput")
    wst_in = dt("wst", [L, 128, 1024], F32, "ExternalInput")
    w_in = dt("w_in", [L, D, NIN], F32, "ExternalInput")
    w_uq = dt("w_uq", [L, 512, 3072], F32, "ExternalInput")
    w_ukv = dt("w_ukv", [L, 512, 4096], F32, "ExternalInput")
    w_pa = dt("w_proj_a", [L, 1024, D], F32, "ExternalInput")
    w_pb = dt("w_proj_b", [L, 2048, D], F32, "ExternalInput")
    w_pc = dt("w_proj_c", [L, 1024, D], F32, "ExternalInput")
    w_o = dt("w_o", [L, D, D], F32, "ExternalInput")
    w_up = dt("w_up", [L, D, 2 * DFF], F32, "ExternalInput")
    w_down = dt("w_down", [L, DFF, D], F32, "ExternalInput")
    out = dt("out", [S_len, D], F32, "ExternalOutput")
    xres = dt("xres", [16, 128, S_len], F32, "Internal")
    xbf = dt("xbf", [16, 128, S_len], BF16, "Internal")
    zscr = dt("zscr", [16, 128, TG], F32, "Internal")
    kcache = dt("kcache", [16, 128, S_len], BF16, "Internal")
    vcache = dt("vcache", [16, S_len, 128], BF16, "Internal")

    S = Sched()
    st = ExitStack()
    with st:
        NF = 53100
        arena = st.enter_context(nc.sbuf_tensor("arena", [128, NF], F32))
        psum = [st.enter_context(nc.psum_tensor("ps%d" % i, [128, 512], F32)) for i in range(8)]
        cur = [0]

        def carve(n):
            o = cur[0]
            cur[0] += n
            assert cur[0] <= NF, cur[0]
            return o

        def f32v(off, n):
            return arena[:, off:off + n]

        def bfv(off, nbf):
            return arena[:, off:off + nbf // 2].bitcast(BF16)

        o = carve(640); CONST = f32v(o, 640)
        IDENT = CONST[:, 0:128]
        INVF = CONST[:, 384:385]
        SGN = CONST[:, 385:386]
        EPSC = CONST[:, 386:387]
        o = carve(256); CB16 = bfv(o, 512)
        IDB, ONESB, TRIB, TRIPB = CB16[:, 0:128], CB16[:, 128:256], CB16[:, 256:384], CB16[:, 384:512]
        o = carve(L * NPP); PP = f32v(o, L * NPP).rearrange("p (l n) -> p l n", l=L)
        o = carve(16); ESINK = f32v(o, 16)
        o = carve(1024); BSBC = f32v(o, 1024).rearrange("p (g t) -> p g t", g=8)
        o = carve(512); WST = bfv(o, 1024).rearrange("p (g t) -> p g t", g=8)
        o = carve(1024); COS = f32v(o, 1024)
        o = carve(1024); SIN = f32v(o, 1024)
        o = carve(176); HALO = f32v(o, 176).rearrange("p (c t) -> p c t", t=2)
        o = carve(S_len // 2); KR = bfv(o, S_len)
        o = carve(8192); XB = bfv(o, 16384).rearrange("p (c n) -> p c n", c=16)
        XB_off = o
        NW = 3
        WSL = []
        for i in range(NW):
            o = carve(2048)
            WSL.append(bfv(o, 4096))
        GEN = cur[0]
        GEN_N = NF - GEN
        YB = bfv(GEN, 16384).rearrange("p (c n) -> p c n", c=16)
        YA = bfv(GEN + 8192, 8192).rearrange("p (c n) -> p c n", c=8)
        YC = bfv(GEN + 12288, 8192).rearrange("p (c n) -> p c n", c=8)

        class Temps:
            def __init__(self, base, limit):
                self.o, self.limit = base, limit

            def f32(self, n):
                o = self.o
                self.o += n
                assert self.o <= self.limit, (self.o, self.limit)
                return f32v(o, n)

            def bf(self, n):
                assert n % 2 == 0
                o = self.o
                self.o += n // 2
                assert self.o <= self.limit, (self.o, self.limit)
                return bfv(o, n)

        uid = [0]

        def U(prefix):
            uid[0] += 1
            return (prefix, uid[0])

        def dma(eng, out_ap, in_ap, reads, writes, sem=None, nofence=False):
            eng = DMA_ENG.get(eng, eng)
            return S.add(eng, lambda e, o=out_ap, i=in_ap: e.dma_start(out=o, in_=i),
                         reads=reads, writes=writes, dma=True, sem=sem, nofence=nofence)

        wctr = [0]

        def load_w(pieces, KC):
            slot = wctr[0] % NW
            wctr[0] += 1
            tot = max(p[3] + p[2] for p in pieces)
            assert KC * tot <= 4096, (KC, tot)
            view = WSL[slot][:, 0:KC * tot].rearrange("p (k n) -> p k n", k=KC)
            keys = []
            for i, (w2, c0, n, dc) in enumerate(pieces):
                key = ("w", slot, i)
                src = w2.rearrange("(k p) n -> p k n", p=128)[:, :, c0:c0 + n]
                dma("pool", view[:, :, dc:dc + n], src, reads=[], writes=[key], sem=("wsem", slot, i), nofence=True)
                keys.append(key)
            return view, keys

        pair_ctr = [0]

        def gemm(lhs_list, rhs_fn, rkeys, wkeys, evac, ntile=2, pairs=(0, 1, 2, 3)):
            pr = pairs[pair_ctr[0] % len(pairs)]
            pair_ctr[0] += 1
            banks = [2 * pr + j for j in range(ntile)]
            n = len(lhs_list)

            def mm(e, lhs_list=lhs_list, banks=banks):
                inst = None
                for kc in range(n):
                    for j, b in enumerate(banks):
                        inst = e.matmul(psum[b][:, :], lhsT=lhs_list[kc], rhs=rhs_fn(kc, j),
                                        start=(kc == 0), stop=(kc == n - 1))
                return inst
            S.add("pe", mm, reads=list(wkeys) + list(rkeys), writes=[("ps", b) for b in banks])
            for j, b in enumerate(banks):
                evac(j, psum[b], ("ps", b))

        def act(out_ap, in_ap, func, reads, writes, bias=None, scale=None):
            kw = {}
            if bias is not None:
                kw["bias"] = bias
            if scale is not None:
                kw["scale"] = scale
            return S.add("act", lambda e: e.activation(out=out_ap, in_=in_ap, func=func, **kw), reads=reads, writes=writes)

        def tt(eng, out_ap, a, b, op, reads, writes):
            return S.add(eng, lambda e: e.tensor_tensor(out=out_ap, in0=a, in1=b, op=op), reads=reads, writes=writes)

        def stt(eng, out_ap, a, sc, b, op0, op1, reads, writes):
            return S.add(eng, lambda e: e.scalar_tensor_tensor(out=out_ap, in0=a, scalar=sc, in1=b, op0=op0, op1=op1),
                         reads=reads, writes=writes)

        def ts(eng, out_ap, a, s1, s2, op0, op1, reads, writes):
            if s2 is None:
                return S.add(eng, lambda e: e.tensor_scalar(out=out_ap, in0=a, scalar1=s1, scalar2=None, op0=op0),
                             reads=reads, writes=writes)
            return S.add(eng, lambda e: e.tensor_scalar(out=out_ap, in0=a, scalar1=s1, scalar2=s2, op0=op0, op1=op1),
                         reads=reads, writes=writes)

        def cp(eng, out_ap, in_ap, reads, writes):
            if eng == "act":
                return S.add("act", lambda e: e.copy(out=out_ap, in_=in_ap), reads=reads, writes=writes)
            return S.add(eng, lambda e: e.tensor_copy(out=out_ap, in_=in_ap), reads=reads, writes=writes)

        dma("sp", CONST, consts_in, [], ["const"])
        dma("sp", PP, pp_in.rearrange("l p n -> p l n"), [], ["pp"])
        cp("dve", IDB, IDENT, ["const"], ["idb"])
        cp("dve", TRIB, CONST[:, 128:256], ["const"], ["trib"])
        cp("dve", TRIPB, CONST[:, 256:384], ["const"], ["tripb"])
        S.add("dve", lambda e: e.memset(ONESB, 1.0), writes=["onesb"])

        T0 = Temps(GEN, NF)
        XS = [T0.f32(2048) for _ in range(2)]
        XT = [T0.f32(512) for _ in range(2)]
        XTB = [T0.bf(512) for _ in range(2)]
        for blk in range(S_len // 128 if upto != "s" else 0):
            xs = XS[blk % 2]
            kxs = ("xs", blk % 2)
            dma("sp", xs, x_in[blk * 128:(blk + 1) * 128, :], [], [kxs])
            for g in range(4):
                b = (blk * 4 + g) % 8
                kp = ("ps", b)

                def tr(e, xs=xs, g=g, b=b):
                    for i in range(4):
                        c = g * 4 + i
                        inst = e.transpose(psum[b][:, i * 128:(i + 1) * 128], xs[:, c * 128:(c + 1) * 128], IDENT)
                    return inst
                S.add("pe", tr, reads=[kxs, "const"], writes=[kp])
                i2 = (blk * 4 + g) % 2
                cp("dve", XT[i2], psum[b][:, :], [kp], [("xt", i2)])
                cp("act", XTB[i2], psum[b][:, :], [kp], [("xtb", i2)])
                if upto in ("s1",):
                    continue
                dma("sp", xres[g * 4:(g + 1) * 4, :, blk * 128:(blk + 1) * 128].rearrange("c p n -> p c n"),
                    XT[i2].rearrange("p (c n) -> p c n", c=4), [("xt", i2)], [U("xres")], sem=("xts", i2))
                if upto in ("s2",):
                    continue
                dma("sp", xbf[g * 4:(g + 1) * 4, :, blk * 128:(blk + 1) * 128].rearrange("c p n -> p c n"),
                    XTB[i2].rearrange("p (c n) -> p c n", c=4), [("xtb", i2)], [U("xbf")], sem=("xtbs", i2))
        S.fence()

        SCALE_A = 64.0 ** -0.5
        SCALE_B = 192.0 ** -0.5

        for l in range(L if upto not in ("0", "s", "s1", "s2") else 0):
            wl_in, wl_uq, wl_ukv = w_in[l], w_uq[l], w_ukv[l]
            PPl = PP[:, l, :]
            dma("sp", ESINK, esink_in[l], [], ["esink"])
            act(ESINK, ESINK, AF.Exp, ["esink"], ["esink"])
            dma("sp", BSBC, bsbc_in[l].rearrange("p (g t) -> p g t", g=8), [], ["bsbc"])
            dma("pool", WST, wst_in[l].rearrange("p (g t) -> p g t", g=8), [], ["wst"], nofence=False)
            for g in range(8):
                tt("dve", WST[:, g, :], WST[:, g, :], TRIB, ALU.mult, ["wst", "trib"], ["wst"])
            S.add("dve", lambda e: e.memset(HALO, 0.0), writes=["halo"])
            S.fence()

            for t in range(NTG):
                tok0 = t * TG
                nkb = 8 * (t + 1)
                dma("pool", XB, xbf[:, :, tok0:tok0 + TG].rearrange("c p n -> p c n"), [], ["xb"])
                TB = Temps(GEN + 8192, NF)
                POSI = f32v(GEN, 1024)
                POSF = f32v(GEN + 1024, 1024)
                dma("sp", POSI.bitcast(I32), pos_in[:, tok0:tok0 + TG], [], ["posi"])
                S1 = f32v(GEN + 2048, 1024)
                S2 = f32v(GEN + 3072, 1024)
                KI = f32v(GEN + 4096, 1024)
                cp("dve", POSF, POSI.bitcast(I32), ["posi"], ["posf"])
                ts("dve", POSF, POSF, INVF, float(1.0 / (2.0 * np.pi)), ALU.mult, ALU.mult, ["posf", "const"], ["posf"])
                cp("dve", KI.bitcast(I32), POSF, ["posf"], ["ki"])
                cp("dve", S1, KI.bitcast(I32), ["ki"], ["s1"])
                tt("dve", POSF, POSF, S1, ALU.subtract, ["posf", "s1"], ["posf"])
                act(S1, POSF, AF.Sin, ["posf"], ["s1"], scale=float(np.pi))
                act(S2, POSF, AF.Sin, ["posf"], ["s2"], scale=float(np.pi / 2))
                tt("dve", S2, S2, S2, ALU.mult, ["s2"], ["s2"])
                ts("dve", S2, S2, -2.0, 1.0, ALU.mult, ALU.add, ["s2"], ["s2"])
                stt("dve", SIN, S1, 2.0, S2, ALU.mult, ALU.mult, ["s1", "s2"], ["sin"])
                tt("dve", S1, S1, S1, ALU.mult, ["s1"], ["s1"])
                ts("dve", COS, S1, -2.0, 1.0, ALU.mult, ALU.add, ["s1"], ["cos"])
                ts("dve", SIN, SIN, SGN, None, ALU.mult, None, ["sin", "const"], ["sin"])

                RAW = TB.f32(4096).rearrange("p (c n) -> p c n", c=4)
                SQ = [TB.bf(1024) for _ in range(2)]
                CQN = TB.bf(4096).rearrange("p (c n) -> p c n", c=4)
                CKVN = TB.bf(4096).rearrange("p (c n) -> p c n", c=4)
                RINV = TB.f32(1024)
                QN = [TB.bf(2048).rearrange("p (h n) -> p h n", h=2) for _ in range(2)]
                QR = [TB.bf(1024) for _ in range(2)]
                KT = TB.bf(S_len)
                VT = TB.bf(S_len).rearrange("p (b d) -> p b d", d=128)
                PT = [TB.bf(512) for _ in range(3)]
                DEN = [TB.f32(512) for _ in range(2)]
                KST = [TB.bf(1024) for _ in range(2)]
                VST = [TB.bf(512).rearrange("p (h d) -> p h d", h=4) for _ in range(2)]
                TR1 = TB.f32(1024)
                TR2 = TB.f32(1024)

                def latent(col0, gcol, dst, tag):
                    for half in range(2):
                        wv, wk = load_w([(wl_in, col0 + half * 256, 256, 0)], 16)
                        for cc in range(2):
                            c = half * 2 + cc

                            def ev(j, ps, kp, c=c):
                                sl = slice(j * 512, (j + 1) * 512)
                                cp("dve", RAW[:, c, sl], ps[:, :], [kp], [("raw", c, j)])
                                i2 = (c * 2 + j) % 2
                                act(SQ[i2][:, 0:512], ps[:, :], AF.Square, [kp], [("sq", i2)])
                                S.add("pe", lambda e, i2=i2, j=j, c=c: e.matmul(psum[6 + j][:, :], lhsT=ONESB, rhs=SQ[i2][:, 0:512],
                                                                                 start=(c == 0), stop=(c == 3)),
                                      reads=[("sq", i2), "onesb"], writes=[("ps", 6 + j)])
                            gemm([wv[:, k, cc * 128:(cc + 1) * 128] for k in range(16)],
                                 lambda k, j: XB[:, k, j * 512:(j + 1) * 512], ["xb"], wk, ev, pairs=(0, 1, 2))
                    for j in range(2):
                        sl = slice(j * 512, (j + 1) * 512)
                        act(RINV[:, sl], psum[6 + j][:, :], AF.Sqrt, [("ps", 6 + j)], [("rinv", j)], bias=EPSC, scale=1.0 / 512.0)
                        S.add("dve", lambda e, sl=sl: e.reciprocal(out=RINV[:, sl], in_=RINV[:, sl]), reads=[("rinv", j)], writes=[("rinv", j)])
                        for c in range(4):
                            stt("dve", dst[:, c, sl], RAW[:, c, sl], PPl[:, gcol + c:gcol + c + 1], RINV[:, sl], ALU.mult, ALU.mult,
                                [("raw", c, j), ("rinv", j), "pp"], [(tag, j)])

                latent(1280, QNG, CQN, "cqn")
                latent(1792, KVNG, CKVN, "ckvn")
                wv, wk = load_w([(wl_in, 2304, 64, 0), (wl_in, 2304, 64, 64), (wl_in, 2336, 32, 128), (wl_in, 2304, 32, 160),
                                 (wl_in, 2336, 32, 192), (wl_in, 2304, 32, 224)], 16)

                def ev_kr0(j, ps, kp):
                    sl = slice(j * 512, (j + 1) * 512)
                    tt("dve", TR1[:, sl], ps[:, :], COS[:, sl], ALU.mult, [kp, "cos"], [("tr1", j)])

                def ev_kr1(j, ps, kp):
                    sl = slice(j * 512, (j + 1) * 512)
                    tt("dve", TR2[:, sl], ps[:, :], SIN[:, sl], ALU.mult, [kp, "sin"], [("tr2", j)])
                    tt("dve", KR[:, tok0 + j * 512: tok0 + (j + 1) * 512], TR1[:, sl], TR2[:, sl], ALU.add,
                       [("tr1", j), ("tr2", j)], [("kr", t, j)])
                gemm([wv[:, k, 0:128] for k in range(16)], lambda k, j: XB[:, k, j * 512:(j + 1) * 512], ["xb"], wk, ev_kr0, pairs=(0, 1, 2))
                gemm([wv[:, k, 128:256] for k in range(16)], lambda k, j: XB[:, k, j * 512:(j + 1) * 512], ["xb"], wk, ev_kr1, pairs=(0, 1, 2))

                ckeys = [("ckvn", 0), ("ckvn", 1)]
                for hg in range(4):
                    wv, wk = load_w([(wl_ukv, hg * 1024, 1024, 0)], 4)
                    for hh in range(4):
                        h = hg * 4 + hh

                        def ev_k(j, ps, kp, h=h):
                            i2 = h % 2
                            sl = slice(j * 512, (j + 1) * 512)
                            cp("act", KST[i2][:, sl], ps[:, :], [kp], [("kst", i2, j)])
                            if j == 1:
                                dma("sp", kcache[h, :, tok0:tok0 + TG], KST[i2], [("kst", i2, 0), ("kst", i2, 1)], [("kc", h)],
                                    sem=("ksts", i2))
                        gemm([wv[:, k, hh * 256:hh * 256 + 128] for k in range(4)],
                             lambda k, j: CKVN[:, k, j * 512:(j + 1) * 512], ckeys, wk, ev_k, pairs=(0, 1, 2))
                    for blk in range(8):
                        b = 6 + (blk % 2)
                        kp = ("ps", b)

                        def mmv(e, wv=wv, blk=blk, b=b):
                            for k in range(4):
                                inst = e.matmul(psum[b][:, :].rearrange("p (h d) -> p h d", h=4),
                                                lhsT=CKVN[:, k, blk * 128:(blk + 1) * 128],
                                                rhs=wv.rearrange("p k (h d) -> p k h d", h=4)[:, k, :, 128:256],
                                                start=(k == 0), stop=(k == 3))
                            return inst
                        S.add("pe", mmv, reads=ckeys + wk, writes=[kp])
                        i2 = blk % 2
                        cp("act", VST[i2], psum[b][:, :].rearrange("p (h d) -> p h d", h=4), [kp], [("vst", i2)])
                        dma("sp", vcache[hg * 4:(hg + 1) * 4, tok0 + blk * 128: tok0 + (blk + 1) * 128, :].rearrange("h t d -> t h d"),
                            VST[i2], [("vst", i2)], [("vc", hg * 4 + i) for i in range(4)], sem=("vsts", i2))

                def q_pieces(hp):
                    h0 = 2 * hp
                    return [(wl_uq, h0 * 192, 128, 0), (wl_uq, (h0 + 1) * 192, 128, 128),
                            (wl_uq, h0 * 192 + 128, 64, 256), (wl_uq, (h0 + 1) * 192 + 128, 64, 320),
                            (wl_uq, h0 * 192 + 160, 32, 384), (wl_uq, h0 * 192 + 128, 32, 416),
                            (wl_uq, (h0 + 1) * 192 + 160, 32, 448), (wl_uq, (h0 + 1) * 192 + 128, 32, 480)]
                q_next = load_w(q_pieces(0), 4)
                for hp in range(8):
                    h0 = 2 * hp
                    qi = hp % 2
                    wv, wk = q_next
                    qkeys = [("cqn", 0), ("cqn", 1)]
                    for hh in range(2):
                        def ev_qn(j, ps, kp, hh=hh):
                            act(QN[qi][:, hh, j * 512:(j + 1) * 512], ps[:, :], AF.Identity, [kp], [("qn", qi, hh, j)], scale=SCALE_B)
                        gemm([wv[:, k, hh * 128:(hh + 1) * 128] for k in range(4)],
                             lambda k, j: CQN[:, k, j * 512:(j + 1) * 512], qkeys, wk, ev_qn, pairs=(0,))

                    def ev_q0(j, ps, kp):
                        sl = slice(j * 512, (j + 1) * 512)
                        stt("dve", TR1[:, sl], ps[:, :], SCALE_B, COS[:, sl], ALU.mult, ALU.mult, [kp, "cos"], [("tr1", j)])

                    def ev_q1(j, ps, kp):
                        sl = slice(j * 512, (j + 1) * 512)
                        stt("dve", TR2[:, sl], ps[:, :], SCALE_B, SIN[:, sl], ALU.mult, ALU.mult, [kp, "sin"], [("tr2", j)])
                        tt("dve", QR[qi][:, sl], TR1[:, sl], TR2[:, sl], ALU.add, [("tr1", j), ("tr2", j)], [("qr", qi, j)])
                    gemm([wv[:, k, 256:384] for k in range(4)], lambda k, j: CQN[:, k, j * 512:(j + 1) * 512], qkeys, wk, ev_q0, pairs=(0,))
                    gemm([wv[:, k, 384:512] for k in range(4)], lambda k, j: CQN[:, k, j * 512:(j + 1) * 512], qkeys, wk, ev_q1, pairs=(0,))

                    if hp < 7:
                        q_next = load_w(q_pieces(hp + 1), 4)
                    for hh in range(2):
                        h = h0 + hh
                        base = hh * 64
                        ntok = (t + 1) * TG
                        dma("pool", KT[:, 0:ntok], kcache[h, :, 0:ntok], [("kc", h)], ["kt"])
                        dma("pool", VT[:, 0:nkb, :], vcache[h, 0:ntok, :].rearrange("(b p) d -> p b d", p=128), [("vc", h)], ["vt"])
                        krk = [("kr", tt_, jj) for tt_ in range(t + 1) for jj in range(2)]
                        steps = [(j, kb) for j in range(2) for kb in range(8 * t + 4 * (j + 1))]

                        def geom(i):
                            j, kb = steps[i]
                            dloc = kb - (8 * t + 4 * j)
                            c0 = 128 * dloc if dloc > 0 else 0
                            return j, kb, dloc, c0, 512 - c0, 6 + (i % 2), PT[i % 3], ("pt", i % 3)

                        def emit_s(i):
                            j, kb, dloc, c0, ncol, sb, pt, kpt = geom(i)
                            qs = slice(j * 512 + c0, (j + 1) * 512)

                            def mms(e, kb=kb, qs=qs, ncol=ncol, sb=sb, hh=hh, base=base, qi=qi):
                                e.matmul(psum[sb][:, 0:ncol], lhsT=KT[:, kb * 128:(kb + 1) * 128], rhs=QN[qi][:, hh, qs],
                                         start=True, stop=False)
                                return e.matmul(psum[sb][:, 0:ncol], lhsT=KR[base:base + 64, kb * 128:(kb + 1) * 128],
                                                rhs=QR[qi][base:base + 64, qs], start=False, stop=True)
                            S.add("pe", mms, reads=["kt", ("qn", qi, hh, j), ("qr", qi, j)] + krk, writes=[("ps", sb)])
                            act(pt[:, 0:ncol], psum[sb][:, 0:ncol], AF.Exp, [("ps", sb)], [kpt])
                            if dloc >= 0:
                                tt("dve", pt[:, 0:128], pt[:, 0:128], TRIB, ALU.mult, [kpt, "trib"], [kpt])

                        def emit_o(i):
                            j, kb, dloc, c0, ncol, sb, pt, kpt = geom(i)
                            OB, DB = 2 + 2 * j, 3 + 2 * j
                            nk = 8 * t + 4 * (j + 1)

                            def mmo(e, kb=kb, c0=c0, ncol=ncol, pt=pt, OB=OB, DB=DB, nk=nk):
                                e.matmul(psum[OB][:, c0:512], lhsT=VT[:, kb, :], rhs=pt[:, 0:ncol], start=(kb == 0), stop=(kb == nk - 1))
                                return e.matmul(psum[DB][:, c0:512], lhsT=ONESB, rhs=pt[:, 0:ncol], start=(kb == 0), stop=(kb == nk - 1))
                            S.add("pe", mmo, reads=["vt", kpt, "onesb"], writes=[("ps", OB), ("ps", DB)])
                            if kb == nk - 1:
                                dn = DEN[j]
                                S.add("dve", lambda e, dn=dn, DB=DB: e.reciprocal(out=dn, in_=psum[DB][:, :]), reads=[("ps", DB)], writes=[("den", j)])
                                tt("dve", YB[:, h, j * 512:(j + 1) * 512], psum[OB][:, :], dn, ALU.mult, [("ps", OB), ("den", j)], [("yb", h)])
                        emit_s(0)
                        for i in range(len(steps)):
                            if i + 1 < len(steps):
                                emit_s(i + 1)
                            emit_o(i)
                S.fence()

                if upto == "B":
                    continue
                TA = Temps(GEN + 12288, NF)
                QA = TA.bf(8192).rearrange("p (c n) -> p c n", c=8)
                KA = TA.bf(2 * 1152).rearrange("p (g n) -> p g n", g=2)
                VA = TA.bf(9 * 256).rearrange("p (b g d) -> p b g d", b=9, g=2)
                PTA = [TA.bf(512) for _ in range(3)]
                DNA = [TA.f32(512) for _ in range(2)]
                for q in range(4):
                    wv, wk = load_w([(wl_in, q * 256, 256, 0)], 16)
                    for cc in range(2):
                        c = q * 2 + cc

                        def ev(j, ps, kp, c=c):
                            act(QA[:, c, j * 512:(j + 1) * 512], ps[:, :], AF.Identity, [kp], [("qa", c)], scale=SCALE_A)
                        gemm([wv[:, k, cc * 128:(cc + 1) * 128] for k in range(16)], lambda k, j: XB[:, k, j * 512:(j + 1) * 512],
                             ["xb"], wk, ev)
                wv, wk = load_w([(wl_in, 1024, 64, 0), (wl_in, 1024, 64, 64), (wl_in, 1088, 64, 128), (wl_in, 1088, 64, 192)], 16)
                wv2, wk2 = load_w([(wl_in, 1152, 64, 0), (wl_in, 1152, 64, 64), (wl_in, 1216, 64, 128), (wl_in, 1216, 64, 192)], 16)
                XH = TA.bf(16 * 128).rearrange("p (c n) -> p c n", c=16)
                if t > 0:
                    dma("pool", XH, xbf[:, :, tok0 - 128:tok0].rearrange("c p n -> p c n"), [], ["xh"])
                for g in range(2):
                    def ev(j, ps, kp, g=g):
                        cp("act", KA[:, g, 128 + j * 512:128 + (j + 1) * 512], ps[:, :], [kp], [("ka", g)])
                    gemm([wv[:, k, g * 128:(g + 1) * 128] for k in range(16)], lambda k, j: XB[:, k, j * 512:(j + 1) * 512],
                         ["xb"], wk, ev)
                if t > 0:
                    def mmh(e, wv=wv):
                        for g in range(2):
                            for k in range(16):
                                inst = e.matmul(psum[0][:, g * 128:(g + 1) * 128], lhsT=wv[:, k, g * 128:(g + 1) * 128], rhs=XH[:, k, :],
                                                start=(k == 0), stop=(k == 15))
                        return inst
                    S.add("pe", mmh, reads=wk + ["xh"], writes=[("ps", 0)])
                    cp("act", KA[:, :, 0:128], psum[0][:, 0:256].rearrange("p (g n) -> p g n", g=2), [("ps", 0)], [("ka", 0), ("ka", 1)])
                for blk in range(9):
                    if blk == 0 and t == 0:
                        continue
                    b = 6 + (blk % 2)
                    src = (lambda k: XH[:, k, :]) if blk == 0 else (lambda k, blk=blk: XB[:, k, (blk - 1) * 128:blk * 128])

                    def mmv(e, src=src, b=b, wv2=wv2):
                        for k in range(16):
                            inst = e.matmul(psum[b][:, 0:256], lhsT=src(k), rhs=wv2[:, k, 0:256], start=(k == 0), stop=(k == 15))
                        return inst
                    S.add("pe", mmv, reads=wk2 + ["xb", "xh"], writes=[("ps", b)])
                    cp("act", VA[:, blk, :, :], psum[b][:, 0:256].rearrange("p (g d) -> p g d", g=2), [("ps", b)], [("va", blk)])
                qakeys = [("qa", c) for c in range(8)]
                it = 0
                for n in range(8):
                    for g in range(2):
                        for par in range(2):
                            base = par * 64
                            OB, DB = 2 + 2 * (it % 2), 3 + 2 * (it % 2)
                            it += 1
                            ms = [1] if (t == 0 and n == 0) else [0, 1]
                            for mi, m in enumerate(ms):
                                sb = 6 + (m % 2)
                                pt = PTA[(it + m) % 3]
                                kpt = ("pta", (it + m) % 3)

                                def mms(e, n=n, g=g, base=base, m=m, sb=sb):
                                    return e.matmul(psum[sb][:, :].rearrange("p (i q) -> p i q", i=4),
                                                    lhsT=KA[base:base + 64, g, (n + m) * 128:(n + m + 1) * 128],
                                                    rhs=QA[base:base + 64, 4 * g:4 * g + 4, n * 128:(n + 1) * 128], start=True, stop=True)
                                S.add("pe", mms, reads=[("ka", g)] + qakeys, writes=[("ps", sb)])
                                act(pt, psum[sb][:, :], AF.Exp, [("ps", sb)], [kpt])
                                msk = TRIB if m == 1 else TRIPB
                                tt("dve", pt.rearrange("p (i q) -> p i q", i=4), pt.rearrange("p (i q) -> p i q", i=4),
                                   msk.unsqueeze(1).broadcast_to([128, 4, 128]), ALU.mult, [kpt, "trib", "tripb"], [kpt])
                            for mi, m in enumerate(ms):
                                pt = PTA[(it + m) % 3]
                                kpt = ("pta", (it + m) % 3)

                                def mmo(e, n=n, g=g, m=m, pt=pt, OB=OB, DB=DB, first=(mi == 0), last=(mi == len(ms) - 1)):
                                    e.matmul(psum[OB][:, :], lhsT=VA[:, n + m, g, :], rhs=pt, start=first, stop=last)
                                    return e.matmul(psum[DB][:, :], lhsT=ONESB, rhs=pt, start=first, stop=last)
                                S.add("pe", mmo, reads=[("va", n + m), kpt, "onesb"], writes=[("ps", OB), ("ps", DB)])
                            dn = DNA[it % 2]
                            kdn = ("dna", it % 2)
                            hsl = slice(8 * g + par, 8 * g + 8, 2)
                            tt("dve", dn.rearrange("p (i q) -> p i q", i=4), psum[DB][:, :].rearrange("p (i q) -> p i q", i=4),
                               ESINK[:, hsl].unsqueeze(2).broadcast_to([128, 4, 128]), ALU.add, [("ps", DB), "esink"], [kdn])
                            S.add("dve", lambda e, dn=dn: e.reciprocal(out=dn, in_=dn), reads=[kdn], writes=[kdn])
                            tt("dve", YA[base:base + 64, 4 * g:4 * g + 4, n * 128:(n + 1) * 128],
                               psum[OB][base:base + 64, :].rearrange("p (i q) -> p i q", i=4),
                               dn[base:base + 64, :].rearrange("p (i q) -> p i q", i=4), ALU.mult, [("ps", OB), kdn], [("ya", n, g, par)])
                S.fence()

                if upto == "A":
                    continue
                TC = Temps(GEN + 16384, NF)
                VG = TC.f32(8192).rearrange("p (c n) -> p c n", c=8)
                VNT = VG.rearrange("p c n -> p (c n)").bitcast(BF16)[:, 0:8192].rearrange("p (g b c) -> p g b c", g=8, b=8)
                VN = TC.bf(8192).rearrange("p (c n) -> p c n", c=8)
                TB2 = [TC.bf(1024)] * 2
                TS2 = [TC.bf(1024)] * 2
                MEAN = COS
                RSTD = SIN
                UG = [TC.bf(1024)] * 2
                TMPC = TC.f32(512)
                for q in range(4):
                    wv, wk = load_w([(wl_in, 3392 + q * 256, 256, 0)], 16)
                    for cc in range(2):
                        c = q * 2 + cc

                        def ev(j, ps, kp, c=c):
                            sl = slice(j * 512, (j + 1) * 512)
                            i2 = 0
                            act(VG[:, c, sl], ps[:, :], AF.Gelu, [kp], [("vg", c, j)])
                            cp("dve", TB2[i2][:, sl], VG[:, c, sl], [("vg", c, j)], [("tb2", i2, j)])
                            tt("dve", TS2[i2][:, sl], VG[:, c, sl], VG[:, c, sl], ALU.mult, [("vg", c, j)], [("ts2", i2, j)])

                            def mst(e, i2=i2, j=j, c=c, sl=sl):
                                e.matmul(psum[4 + j][:, :], lhsT=ONESB, rhs=TB2[i2][:, sl], start=(c == 0), stop=(c == 7))
                                return e.matmul(psum[6 + j][:, :], lhsT=ONESB, rhs=TS2[i2][:, sl], start=(c == 0), stop=(c == 7))
                            S.add("pe", mst, reads=[("tb2", i2, j), ("ts2", i2, j), "onesb"], writes=[("ps", 4 + j), ("ps", 6 + j)])
                        gemm([wv[:, k, cc * 128:(cc + 1) * 128] for k in range(16)], lambda k, j: XB[:, k, j * 512:(j + 1) * 512],
                             ["xb"], wk, ev, pairs=(0, 1))
                for j in range(2):
                    sl = slice(j * 512, (j + 1) * 512)
                    ts("dve", MEAN[:, sl], psum[4 + j][:, :], 1.0 / 1024.0, None, ALU.mult, None, [("ps", 4 + j)], [("mean", j)])
                    tt("dve", TMPC, MEAN[:, sl], MEAN[:, sl], ALU.mult, [("mean", j)], ["tmpc"])
                    stt("dve", RSTD[:, sl], psum[6 + j][:, :], 1.0 / 1024.0, TMPC, ALU.mult, ALU.subtract, [("ps", 6 + j), "tmpc"], [("rstd", j)])
                    act(RSTD[:, sl], RSTD[:, sl], AF.Sqrt, [("rstd", j)], [("rstd", j)], bias=EPSC)
                    S.add("dve", lambda e, sl=sl, RSTD=RSTD: e.reciprocal(out=RSTD[:, sl], in_=RSTD[:, sl]), reads=[("rstd", j)], writes=[("rstd", j)])
                    for c in range(8):
                        tt("dve", VG[:, c, sl], VG[:, c, sl], MEAN[:, sl], ALU.subtract, [("vg", c, j), ("mean", j)], [("vg", c, j)])
                        tt("dve", VG[:, c, sl], VG[:, c, sl], RSTD[:, sl], ALU.mult, [("vg", c, j), ("rstd", j)], [("vg", c, j)])
                        act(VN[:, c, sl], VG[:, c, sl], AF.Identity, [("vg", c, j), "pp"], [("vn", c, j)],
                            bias=PPl[:, SLB + c:SLB + c + 1], scale=PPl[:, SLG + c:SLG + c + 1])
                S.fence()
                PSB = [psum[b][:, :].bitcast(BF16) for b in range(8)]
                for g in range(8):
                    for half in range(2):
                        b = (g * 2 + half) % 2

                        def trn(e, g=g, half=half, b=b):
                            for i in range(4):
                                blk = half * 4 + i
                                inst = e.transpose(PSB[b][:, i * 128:(i + 1) * 128], VN[:, g, blk * 128:(blk + 1) * 128], IDB)
                            return inst
                        S.add("pe", trn, reads=[("vn", g, half), "idb"], writes=[("ps", b)])
                        cp("act", VNT[:, g, half * 4:half * 4 + 4, :], PSB[b][:, 0:512].rearrange("p (b c) -> p b c", b=4),
                           [("ps", b)], [("vnt", g, half)])
                for q in range(4):
                    wv, wk = load_w([(wl_in, 2368 + q * 256, 256, 0)], 16)
                    for cc in range(2):
                        g = q * 2 + cc
                        ug = UG[g % 2]

                        def ev(j, ps, kp, g=g, ug=ug):
                            act(ug[:, j * 512:(j + 1) * 512], ps[:, :], AF.Gelu, [kp], [("ug", 0, j)])
                        gemm([wv[:, k, cc * 128:(cc + 1) * 128] for k in range(16)], lambda k, j: XB[:, k, j * 512:(j + 1) * 512],
                             ["xb"], wk, ev, pairs=(1, 2))
                        for half in range(2):
                            b = 6 + half

                            def mix(e, g=g, half=half, b=b):
                                for i in range(4):
                                    inst = e.matmul(psum[b][:, i * 128:(i + 1) * 128], lhsT=VNT[:, g, half * 4 + i, :], rhs=WST[:, g, :],
                                                    start=True, stop=True)
                                return inst
                            S.add("pe", mix, reads=[("vnt", g, half), "wst"], writes=[("ps", b)])
                            tt("dve", TMPC.rearrange("p (b t) -> p b t", b=4), psum[b][:, :].rearrange("p (b t) -> p b t", b=4),
                               BSBC[:, g, :].unsqueeze(1).broadcast_to([128, 4, 128]), ALU.add, [("ps", b), "bsbc"], ["tmpc"])
                            tt("dve", YC[:, g, half * 512:(half + 1) * 512], TMPC, ug[:, half * 512:(half + 1) * 512], ALU.mult,
                               ["tmpc", ("ug", 0, half)], [("yc", g)])
                S.fence()

                if upto == "C":
                    continue
                TD = Temps(GEN + 16384, NF)
                MERGED = TD.bf(16384).rearrange("p (c n) -> p c n", c=16)
                GT = [TD.f32(1024) for _ in range(2)]
                MTS = [TD.f32(1024) for _ in range(2)]
                TT_ = TD.f32(1024)
                branches = [(w_pa[l], 8, YA, 0), (w_pb[l], 16, YB, 1), (w_pc[l], 8, YC, 2)]
                ykeys_all = {0: [("ya", n, g, p) for n in range(8) for g in range(2) for p in range(2)],
                             1: [("yb", h) for h in range(16)], 2: [("yc", g) for g in range(8)]}
                for jp in range(8):
                    for i, (wpr, kcn, Y, bi) in enumerate(branches):
                        gv, gk = load_w([(wl_in, 4416 + i * 2048 + jp * 256, 256, 0)], 16)
                        pv, pk = load_w([(wpr, jp * 256, 256, 0)], kcn)
                        for cc in range(2):
                            jd = jp * 2 + cc
                            gt = GT[cc]
                            mt = MTS[cc]

                            def ev_g(j, ps, kp, gt=gt, i=i, jd=jd, cc=cc):
                                act(gt[:, j * 512:(j + 1) * 512], ps[:, :], AF.Sigmoid, [kp, "pp"], [("gt", cc, j)],
                                    bias=PPl[:, BG + i * 16 + jd:BG + i * 16 + jd + 1])
                            gemm([gv[:, k, cc * 128:(cc + 1) * 128] for k in range(16)], lambda k, j: XB[:, k, j * 512:(j + 1) * 512],
                                 ["xb"], gk, ev_g)

                            def ev_p(j, ps, kp, gt=gt, mt=mt, i=i, jd=jd, cc=cc):
                                sl = slice(j * 512, (j + 1) * 512)
                                if i == 0:
                                    tt("dve", mt[:, sl], ps[:, :], gt[:, sl], ALU.mult, [kp, ("gt", cc, j)], [("mt", cc, j)])
                                elif i == 1:
                                    tt("dve", TT_[:, sl], ps[:, :], gt[:, sl], ALU.mult, [kp, ("gt", cc, j)], [("tt", j)])
                                    tt("dve", mt[:, sl], mt[:, sl], TT_[:, sl], ALU.add, [("mt", cc, j), ("tt", j)], [("mt", cc, j)])
                                else:
                                    tt("dve", TT_[:, sl], ps[:, :], gt[:, sl], ALU.mult, [kp, ("gt", cc, j)], [("tt", j)])
                                    tt("dve", MERGED[:, jd, sl], mt[:, sl], TT_[:, sl], ALU.add, [("mt", cc, j), ("tt", j)], [("merged", jd)])
                            gemm([pv[:, k, cc * 128:(cc + 1) * 128] for k in range(kcn)],
                                 lambda k, j, Y=Y: Y[:, k, j * 512:(j + 1) * 512], ykeys_all[i], pk, ev_p)
                S.fence()

                if upto == "D":
                    continue
                def ln_stage(TT, TT2, nK, lhs_groups_fn, rhs_fn, rkeys, gcol, bcol, final, xr_in_tt=True):
                    NXR = 6
                    XR = [(TT if xr_in_tt else TT2).f32(1024) for _ in range(NXR)]
                    ZB = [TT.bf(1024) for _ in range(2)]
                    ZS = [TT.bf(1024) for _ in range(2)]
                    MEAN = TT.f32(1024)
                    RSTD = TT.f32(1024)
                    TMP = TT2.f32(512)
                    XO = [TT2.bf(1024) for _ in range(2)]
                    for jd in range(16):
                        xr = XR[jd % NXR]
                        kxr = ("xr", jd % NXR)
                        dma("pool", xr, xres[jd, :, tok0:tok0 + TG], [], [kxr])
                        lhs, wk = lhs_groups_fn(jd)

                        def ev(j, ps, kp, jd=jd, xr=xr, kxr=kxr):
                            sl = slice(j * 512, (j + 1) * 512)
                            i2 = jd % 2
                            stt("dve", xr[:, sl], xr[:, sl], ALPHA, ps[:, :], ALU.mult, ALU.add, [kp, kxr], [kxr])
                            cp("act", ZB[i2][:, sl], xr[:, sl], [kxr], [("zb", i2, j)])
                            act(ZS[i2][:, sl], xr[:, sl], AF.Square, [kxr], [("zs", i2, j)])

                            def mst(e, i2=i2, j=j, jd=jd, sl=sl):
                                e.matmul(psum[4 + j][:, :], lhsT=ONESB, rhs=ZB[i2][:, sl], start=(jd == 0), stop=(jd == 15))
                                return e.matmul(psum[6 + j][:, :], lhsT=ONESB, rhs=ZS[i2][:, sl], start=(jd == 0), stop=(jd == 15))
                            S.add("pe", mst, reads=[("zb", i2, j), ("zs", i2, j), "onesb"], writes=[("ps", 4 + j), ("ps", 6 + j)])
                            if j == 1:
                                dma("sp", zscr[jd, :, :], xr, [kxr], [("zscr", jd)], sem=("zst", jd % NXR))
                        gemm(lhs, rhs_fn, rkeys, wk, ev, pairs=(0, 1))
                    for j in range(2):
                        sl = slice(j * 512, (j + 1) * 512)
                        ts("dve", MEAN[:, sl], psum[4 + j][:, :], 1.0 / D, None, ALU.mult, None, [("ps", 4 + j)], [("mean", j)])
                        tt("dve", TMP, MEAN[:, sl], MEAN[:, sl], ALU.mult, [("mean", j)], ["tmp"])
                        stt("dve", RSTD[:, sl], psum[6 + j][:, :], 1.0 / D, TMP, ALU.mult, ALU.subtract, [("ps", 6 + j), "tmp"], [("rstd", j)])
                        act(RSTD[:, sl], RSTD[:, sl], AF.Sqrt, [("rstd", j)], [("rstd", j)], bias=EPSC)
                        S.add("dve", lambda e, sl=sl, RSTD=RSTD: e.reciprocal(out=RSTD[:, sl], in_=RSTD[:, sl]), reads=[("rstd", j)], writes=[("rstd", j)])
                    mk = [("mean", 0), ("mean", 1), ("rstd", 0), ("rstd", 1)]
                    for jd in range(16):
                        xr = XR[jd % NXR]
                        kxr = ("xr", jd % NXR)
                        dma("pool", xr, zscr[jd, :, :], [("zscr", jd)], [kxr])
                        tt("dve", xr, xr, MEAN, ALU.subtract, [kxr] + mk, [kxr])
                        tt("dve", xr, xr, RSTD, ALU.mult, [kxr] + mk, [kxr])
                        act(xr, xr, AF.Identity, [kxr, "pp"], [kxr], bias=PPl[:, bcol + jd:bcol + jd + 1], scale=PPl[:, gcol + jd:gcol + jd + 1])
                        if not final:
                            dma("sp", xres[jd, :, tok0:tok0 + TG], xr, [kxr], [U("xres")], sem=("xrs", jd % NXR))
                            xo = XO[jd % 2]
                            cp("dve", xo, xr, [kxr], [("xo", jd % 2)])
                            dma("sp", xbf[jd, :, tok0:tok0 + TG], xo, [("xo", jd % 2)], [U("xbf")], sem=("xos", jd % 2))
                        else:
                            dma("sp", xres[jd, :, tok0:tok0 + TG], xr, [kxr], [("xresf", jd)], sem=("xrs", jd % NXR))

                TE = Temps(GEN, GEN + 16384)

                def lhs_wo(jd, cache={}):
                    jp = jd // 2
                    if jp not in cache:
                        cache.clear()
                        cache[jp] = load_w([(w_o[l], jp * 256, 256, 0)], 16)
                    wv, wk = cache[jp]
                    cc = jd % 2
                    return [wv[:, k, cc * 128:(cc + 1) * 128] for k in range(16)], wk
                ln_stage(TE, TE, 16, lhs_wo, lambda k, j: MERGED[:, k, j * 512:(j + 1) * 512], [("merged", c) for c in range(16)], L1G, L1B, False)
                S.fence()
                dma("pool", XB, xbf[:, :, tok0:tok0 + TG].rearrange("c p n -> p c n"), [], ["xb"])

                if upto == "E":
                    continue
                TF = Temps(GEN, NF)
                ACTT = TF.bf(44 * 1024).rearrange("p (c n) -> p c n", c=44)
                UP = [[TF.f32(1026) for _ in range(2)] for _ in range(2)]
                ACC = [TF.f32(1024) for _ in range(2)]
                SG = TF.f32(1024)
                for c in range(44):
                    wv, wk = load_w([(w_up[l], c * 128, 128, 0), (w_up[l], DFF + c * 128, 128, 128)], 16)
                    for hv in range(2):
                        up = UP[hv][c % 2]
                        kup = ("up", hv, c % 2)
                        cc = c + 44 * hv
                        cp("act", up[:, 0:2], HALO[:, cc, :], ["halo"], [(kup, "h")])

                        def ev(j, ps, kp, up=up, kup=kup):
                            cp("act", up[:, 2 + j * 512:2 + (j + 1) * 512], ps[:, :], [kp], [(kup, j)])
                        gemm([wv[:, k, hv * 128:(hv + 1) * 128] for k in range(16)], lambda k, j: XB[:, k, j * 512:(j + 1) * 512],
                             ["xb"], wk, ev)
                        ku = [(kup, "h"), (kup, 0), (kup, 1)]
                        acc = ACC[hv]
                        ka = ("acc", hv)
                        ts("dve", acc, up[:, 2:1026], PPl[:, CW + 2 * 88 + cc:CW + 2 * 88 + cc + 1], PPl[:, CB + cc:CB + cc + 1],
                           ALU.mult, ALU.add, ku + ["pp"], [ka])
                        stt("dve", acc, up[:, 1:1025], PPl[:, CW + 88 + cc:CW + 88 + cc + 1], acc, ALU.mult, ALU.add, ku + [ka, "pp"], [ka])
                        stt("dve", acc, up[:, 0:1024], PPl[:, CW + cc:CW + cc + 1], acc, ALU.mult, ALU.add, ku + [ka, "pp"], [ka])
                        cp("act", HALO[:, cc, :], up[:, 1024:1026], ku, ["halo"])
                    act(SG, ACC[0], AF.Silu, [("acc", 0)], ["sg"])
                    tt("dve", ACTT[:, c, :], SG, ACC[1], ALU.mult, ["sg", ("acc", 1)], [("actt", c)])
                S.fence()

                if upto == "F":
                    continue
                TGm = Temps(GEN + 22528, NF)
                last = (l == L - 1)

                def lhs_wd(jd):
                    a = load_w([(w_down[l][0:2816, :], jd * 128, 128, 0)], 22)
                    b = load_w([(w_down[l][2816:5632, :], jd * 128, 128, 0)], 22)
                    return [a[0][:, k, :] for k in range(22)] + [b[0][:, k, :] for k in range(22)], a[1] + b[1]
                ln_stage(TGm, Temps(XB_off, XB_off + 8192), 44, lhs_wd, lambda k, j: ACTT[:, k, j * 512:(j + 1) * 512], [("actt", c) for c in range(44)], L2G, L2B, last, xr_in_tt=False)
                if last:
                    S.fence()
                if last:
                    TO = Temps(XB_off + 2048, XB_off + 8192)
                    XF = [TO.f32(512) for _ in range(2)]
                    OS = [TO.f32(2048) for _ in range(2)]
                    for blk in range(8):
                        osb = OS[blk % 2]
                        for g in range(4):
                            xf = XF[g % 2]
                            kxf = ("xf", g % 2)
                            dma("pool", xf.rearrange("p (c n) -> p c n", c=4),
                                xres[g * 4:(g + 1) * 4, :, tok0 + blk * 128: tok0 + (blk + 1) * 128].rearrange("c p n -> p c n"),
                                [("xresf", g * 4 + i) for i in range(4)], [kxf])
                            b = (blk * 4 + g) % 4

                            def tr(e, xf=xf, b=b):
                                for i in range(4):
                                    inst = e.transpose(psum[b][:, i * 128:(i + 1) * 128], xf[:, i * 128:(i + 1) * 128], IDENT)
                                return inst
                            S.add("pe", tr, reads=[kxf, "const"], writes=[("ps", b)])
                            cp("act" if g % 2 else "dve", osb[:, g * 512:(g + 1) * 512], psum[b][:, :], [("ps", b)], [("os", blk % 2, g)])
                        dma("sp", out[tok0 + blk * 128: tok0 + (blk + 1) * 128, :], osb, [("os", blk % 2, g) for g in range(4)],
                            [U("out")], sem=("oss", blk % 2))
                S.fence()
        S.fence()
        global LAST_SCHED
        LAST_SCHED = S
        S.emit(nc, st)
    return nc


def _pack_cols(v):
    v = np.asarray(v, np.float32).reshape(-1)
    return np.ascontiguousarray(v.reshape(-1, 128).T)


def _host_prep(inputs, L=2):
    c = np.zeros((128, 640), np.float32)
    c[:, 0:128] = np.eye(128, dtype=np.float32)
    k = np.arange(128)[:, None]
    q = np.arange(128)[None, :]
    c[:, 128:256] = (k <= q)
    c[:, 256:384] = (k > q)
    inv = (10000.0 ** (-np.arange(0, 64, 2, dtype=np.float32) / 64.0)).astype(np.float32)
    p = np.arange(128)
    c[:, 384] = inv[p % 32]
    c[:, 385] = np.where((p % 64) < 32, -1.0, 1.0)
    c[:, 386] = EPS
    pp = np.zeros((L, 128, NPP), np.float32)
    for l in range(L):
        pp[l, :, BG:BG + 48] = _pack_cols(inputs["b_gate"][l])
        pp[l, :, QNG:QNG + 4] = _pack_cols(inputs["q_norm_g"][l])
        pp[l, :, KVNG:KVNG + 4] = _pack_cols(inputs["kv_norm_g"][l])
        pp[l, :, SLG:SLG + 8] = _pack_cols(inputs["sgu_ln_g"][l])
        pp[l, :, SLB:SLB + 8] = _pack_cols(inputs["sgu_ln_b"][l])
        pp[l, :, L1G:L1G + 16] = _pack_cols(inputs["ln1_g"][l])
        pp[l, :, L1B:L1B + 16] = _pack_cols(inputs["ln1_b"][l])
        pp[l, :, L2G:L2G + 16] = _pack_cols(inputs["ln2_g"][l])
        pp[l, :, L2B:L2B + 16] = _pack_cols(inputs["ln2_b"][l])
        pp[l, :, CW:CW + 264] = _pack_cols(inputs["conv_w"][l])
        pp[l, :, CB:CB + 88] = _pack_cols(inputs["conv_b"][l])
    esink = np.ascontiguousarray(np.broadcast_to(np.asarray(inputs["sinks"], np.float32)[:L, None, :], (L, 128, 16)))
    bsbc = np.ascontiguousarray(np.broadcast_to(np.asarray(inputs["sgu_b"], np.float32)[:L].reshape(L, 1, 1024), (L, 128, 1024)))
    wst = np.ascontiguousarray(np.asarray(inputs["sgu_w"], np.float32)[:L].transpose(0, 3, 1, 2).reshape(L, 128, 1024))
    shared = {"consts": c, "pp": pp, "esink": esink, "bsbc": bsbc, "wst": wst}
    for k_ in ("w_in", "w_uq", "w_ukv", "w_proj_a", "w_proj_b", "w_proj_c", "w_o", "w_up", "w_down"):
        shared[k_] = np.ascontiguousarray(np.asarray(inputs[k_], np.float32)[:L])
    return shared


_NC_CACHE = {}


def kernel(**inputs):
    x = np.asarray(inputs["x"], np.float32)
    pos = np.asarray(inputs["positions"], np.int32)
    B, S_len, _ = x.shape
    L = 2
    key = (S_len, L)
    if key not in _NC_CACHE:
        _NC_CACHE[key] = build(S_len, L)
    nc = _NC_CACHE[key]
    shared = _host_prep(inputs, L)
    in_maps = []
    for b in range(B):
        m = dict(shared)
        m["x"] = np.ascontiguousarray(x[b])
        m["pos_bc"] = np.ascontiguousarray(np.broadcast_to(pos[b][None, :], (128, S_len)))
        in_maps.append(m)
    res = run_bass_kernel_spmd(nc, in_maps, core_ids=list(range(B)))
    return np.stack([np.asarray(r["out"], np.float32) for r in res.results], axis=0)
```

```python
from contextlib import ExitStack
import numpy as np
import concourse.bass as bass
import concourse.mybir as mybir
from concourse.bass_utils import run_bass_kernel_spmd

F32 = mybir.dt.float32
BF16 = mybir.dt.bfloat16
I32 = mybir.dt.int32
AF = mybir.ActivationFunctionType
ALU = mybir.AluOpType

D = 2048
NIN = 10560
DFF = 5632
TG = 1024
EPS = 1e-5
ALPHA = 4.0 ** 0.25
ENGS = ("pe", "act", "dve", "pool", "sp")
FENCED = ("pe", "act", "dve", "sp", "pool")
DMA_ENG = {}

BG, QNG, KVNG, SLG, SLB, L1G, L1B, L2G, L2B, CW, CB, NPP = 0, 48, 52, 56, 64, 72, 88, 104, 120, 136, 400, 488


class _Op:
    __slots__ = ("eng", "fn", "deps", "idx", "eidx", "is_dma", "semkey", "ordinal", "signal", "tick")


class Sched:
    def __init__(self):
        self.ops = []
        self.eng_ops = {e: [] for e in ENGS}
        self.last_w = {}
        self.readers = {}
        self.dma_count = {}
        self.dma_since_fence = []

    def add(self, eng, fn, reads=(), writes=(), dma=False, sem=None, nofence=False):
        op = _Op()
        op.eng, op.fn, op.is_dma = eng, fn, dma
        op.idx = len(self.ops)
        op.eidx = len(self.eng_ops[eng])
        op.signal = False
        op.tick = 0
        ps_reads = [k for k in reads if isinstance(k, tuple) and k and k[0] == "ps"]
        if ps_reads:
            reads = [k for k in reads if k not in ps_reads]
            writes = list(writes) + ps_reads
        deps = set()
        for k in reads:
            w = self.last_w.get(k)
            if w is not None:
                deps.add(w)
        for k in writes:
            w = self.last_w.get(k)
            if w is not None:
                deps.add(w)
            for r in self.readers.get(k, ()):
                deps.add(r)
        op.deps = deps
        for k in writes:
            self.last_w[k] = op
            self.readers[k] = []
        for k in reads:
            lst = self.readers.setdefault(k, [])
            if not dma:
                lst[:] = [r for r in lst if r.is_dma or r.eng != eng]
            lst.append(op)
        if dma:
            if sem is None:
                sem = ("dma", tuple(writes)[0] if writes else tuple(reads)[0])
            op.semkey = sem
            n = self.dma_count.get(sem, 0) + 1
            self.dma_count[sem] = n
            op.ordinal = n
            if not nofence:
                self.dma_since_fence.append(op)
        else:
            op.semkey = None
            op.ordinal = 0
        self.ops.append(op)
        self.eng_ops[eng].append(op)
        return op

    def fence(self):
        lasts = []
        for e in FENCED:
            for o in reversed(self.eng_ops[e]):
                if o.fn is not None and not o.is_dma:
                    lasts.append(o)
                    break
        dmas = list(self.dma_since_fence)
        self.dma_since_fence = []
        for e in FENCED:
            op = self.add(e, None)
            op.deps = set(o for o in lasts if o.eng != e) | set(dmas)

    def _needs_wait(self, op, d):
        if d.is_dma or d.eng != op.eng:
            return True
        return op.eng != "pe"

    def emit(self, nc, stack):
        for op in self.ops:
            for d in op.deps:
                if self._needs_wait(op, d):
                    d.signal = True
        for e in ENGS:
            t = 0
            for op in self.eng_ops[e]:
                if op.is_dma:
                    continue
                if op.signal:
                    t += 1
                op.tick = t
        eng_sem = {e: stack.enter_context(nc.semaphore("s_" + e)) for e in ENGS}
        dma_sem = {}
        for k in self.dma_count:
            dma_sem[k] = stack.enter_context(nc.semaphore("d%d" % len(dma_sem)))
        block = stack.enter_context(nc.Block())

        def run(e, eng):
            known = {}
            for op in self.eng_ops[e]:
                need = {}
                for d in op.deps:
                    if not self._needs_wait(op, d):
                        continue
                    if d.is_dma:
                        s, v = dma_sem[d.semkey], 16 * d.ordinal
                    else:
                        s, v = eng_sem[d.eng], d.tick
                    if need.get(s, 0) < v:
                        need[s] = v
                for s, v in need.items():
                    if known.get(s, 0) >= v:
                        continue
                    known[s] = v
                    eng.wait_ge(s, v)
                if op.fn is None:
                    continue
                inst = op.fn(eng)
                if op.is_dma:
                    inst.then_inc(dma_sem[op.semkey], 16)
                elif op.signal:
                    inst.then_inc(eng_sem[e], 1)

        block.tensor(lambda eng: run("pe", eng))
        block.scalar(lambda eng: run("act", eng))
        block.vector(lambda eng: run("dve", eng))
        block.gpsimd(lambda eng: run("pool", eng))
        block.sync(lambda eng: run("sp", eng))


def build(S_len=4096, L=2, upto=None):
    nc = bass.Bass("TRN2", target_bir_lowering=False)
    NTG = S_len // TG
    dt = lambda name, shape, dtype, kind: nc.dram_tensor(name, shape, dtype, kind=kind).ap()
    x_in = dt("x", [S_len, D], F32, "ExternalInput")
    pos_in = dt("pos_bc", [128, S_len], I32, "ExternalInput")
    consts_in = dt("consts", [128, 640], F32, "ExternalInput")
    pp_in = dt("pp", [L, 128, NPP], F32, "ExternalInput")
    esink_in = dt("esink", [L, 128, 16], F32, "ExternalInput")
    bsbc_in = dt("bsbc", [L, 128, 1024], F32, "ExternalInput")
    wst_in = dt("wst", [L, 128, 1024], F32, "ExternalInput")
    w_in = dt("w_in", [L, D, NIN], F32, "ExternalInput")
    w_uq = dt("w_uq", [L, 512, 3072], F32, "ExternalInput")
    w_ukv = dt("w_ukv", [L, 512, 4096], F32, "ExternalInput")
    w_pa = dt("w_proj_a", [L, 1024, D], F32, "ExternalInput")
    w_pb = dt("w_proj_b", [L, 2048, D], F32, "ExternalInput")
    w_pc = dt("w_proj_c", [L, 1024, D], F32, "ExternalInput")
    w_o = dt("w_o", [L, D, D], F32, "ExternalInput")
    w_up = dt("w_up", [L, D, 2 * DFF], F32, "ExternalInput")
    w_down = dt("w_down", [L, DFF, D], F32, "ExternalInput")
    out = dt("out", [S_len, D], F32, "ExternalOutput")
    xres = dt("xres", [16, 128, S_len], F32, "Internal")
    xbf = dt("xbf", [16, 128, S_len], BF16, "Internal")
    zscr = dt("zscr", [16, 128, TG], F32, "Internal")
    kcache = dt("kcache", [16, 128, S_len], BF16, "Internal")
    vcache = dt("vcache", [16, S_len, 128], BF16, "Internal")

    S = Sched()
    st = ExitStack()
    with st:
        NF = 53100
        arena = st.enter_context(nc.sbuf_tensor("arena", [128, NF], F32))
        psum = [st.enter_context(nc.psum_tensor("ps%d" % i, [128, 512], F32)) for i in range(8)]
        cur = [0]

        def carve(n):
            o = cur[0]
            cur[0] += n
            assert cur[0] <= NF, cur[0]
            return o

        def f32v(off, n):
            return arena[:, off:off + n]

        def bfv(off, nbf):
            return arena[:, off:off + nbf // 2].bitcast(BF16)

        o = carve(640); CONST = f32v(o, 640)
        IDENT = CONST[:, 0:128]
        INVF = CONST[:, 384:385]
        SGN = CONST[:, 385:386]
        EPSC = CONST[:, 386:387]
        o = carve(256); CB16 = bfv(o, 512)
        IDB, ONESB, TRIB, TRIPB = CB16[:, 0:128], CB16[:, 128:256], CB16[:, 256:384], CB16[:, 384:512]
        o = carve(L * NPP); PP = f32v(o, L * NPP).rearrange("p (l n) -> p l n", l=L)
        o = carve(16); ESINK = f32v(o, 16)
        o = carve(1024); BSBC = f32v(o, 1024).rearrange("p (g t) -> p g t", g=8)
        o = carve(512); WST = bfv(o, 1024).rearrange("p (g t) -> p g t", g=8)
        o = carve(1024); COS = f32v(o, 1024)
        o = carve(1024); SIN = f32v(o, 1024)
        o = carve(176); HALO = f32v(o, 176).rearrange("p (c t) -> p c t", t=2)
        o = carve(S_len // 2); KR = bfv(o, S_len)
        o = carve(8192); XB = bfv(o, 16384).rearrange("p (c n) -> p c n", c=16)
        XB_off = o
        NW = 3
        WSL = []
        for i in range(NW):
            o = carve(2048)
            WSL.append(bfv(o, 4096))
        GEN = cur[0]
        GEN_N = NF - GEN
        YB = bfv(GEN, 16384).rearrange("p (c n) -> p c n", c=16)
        YA = bfv(GEN + 8192, 8192).rearrange("p (c n) -> p c n", c=8)
        YC = bfv(GEN + 12288, 8192).rearrange("p (c n) -> p c n", c=8)

        class Temps:
            def __init__(self, base, limit):
                self.o, self.limit = base, limit

            def f32(self, n):
                o = self.o
                self.o += n
                assert self.o <= self.limit, (self.o, self.limit)
                return f32v(o, n)

            def bf(self, n):
                assert n % 2 == 0
                o = self.o
                self.o += n // 2
                assert self.o <= self.limit, (self.o, self.limit)
                return bfv(o, n)

        uid = [0]

        def U(prefix):
            uid[0] += 1
            return (prefix, uid[0])

        def dma(eng, out_ap, in_ap, reads, writes, sem=None, nofence=False):
            eng = DMA_ENG.get(eng, eng)
            return S.add(eng, lambda e, o=out_ap, i=in_ap: e.dma_start(out=o, in_=i),
                         reads=reads, writes=writes, dma=True, sem=sem, nofence=nofence)

        wctr = [0]

        def load_w(pieces, KC):
            slot = wctr[0] % NW
            wctr[0] += 1
            tot = max(p[3] + p[2] for p in pieces)
            assert KC * tot <= 4096, (KC, tot)
            view = WSL[slot][:, 0:KC * tot].rearrange("p (k n) -> p k n", k=KC)
            keys = []
            for i, (w2, c0, n, dc) in enumerate(pieces):
                key = ("w", slot, i)
                src = w2.rearrange("(k p) n -> p k n", p=128)[:, :, c0:c0 + n]
                dma("pool", view[:, :, dc:dc + n], src, reads=[], writes=[key], sem=("wsem", slot, i), nofence=True)
                keys.append(key)
            return view, keys

        pair_ctr = [0]

        def gemm(lhs_list, rhs_fn, rkeys, wkeys, evac, ntile=2, pairs=(0, 1, 2, 3), after_mm=None):
            pr = pairs[pair_ctr[0] % len(pairs)]
            pair_ctr[0] += 1
            banks = [2 * pr + j for j in range(ntile)]
            n = len(lhs_list)

            def mm(e, lhs_list=lhs_list, banks=banks):
                inst = None
                for kc in range(n):
                    for j, b in enumerate(banks):
                        inst = e.matmul(psum[b][:, :], lhsT=lhs_list[kc], rhs=rhs_fn(kc, j),
                                        start=(kc == 0), stop=(kc == n - 1))
                return inst
            S.add("pe", mm, reads=list(wkeys) + list(rkeys), writes=[("ps", b) for b in banks])
            if after_mm is not None:
                after_mm()
            for j, b in enumerate(banks):
                evac(j, psum[b], ("ps", b))

        def act(out_ap, in_ap, func, reads, writes, bias=None, scale=None):
            kw = {}
            if bias is not None:
                kw["bias"] = bias
            if scale is not None:
                kw["scale"] = scale
            return S.add("act", lambda e: e.activation(out=out_ap, in_=in_ap, func=func, **kw), reads=reads, writes=writes)

        def tt(eng, out_ap, a, b, op, reads, writes):
            return S.add(eng, lambda e: e.tensor_tensor(out=out_ap, in0=a, in1=b, op=op), reads=reads, writes=writes)

        def stt(eng, out_ap, a, sc, b, op0, op1, reads, writes):
            return S.add(eng, lambda e: e.scalar_tensor_tensor(out=out_ap, in0=a, scalar=sc, in1=b, op0=op0, op1=op1),
                         reads=reads, writes=writes)

        def ts(eng, out_ap, a, s1, s2, op0, op1, reads, writes):
            if s2 is None:
                return S.add(eng, lambda e: e.tensor_scalar(out=out_ap, in0=a, scalar1=s1, scalar2=None, op0=op0),
                             reads=reads, writes=writes)
            return S.add(eng, lambda e: e.tensor_scalar(out=out_ap, in0=a, scalar1=s1, scalar2=s2, op0=op0, op1=op1),
                         reads=reads, writes=writes)

        def cp(eng, out_ap, in_ap, reads, writes):
            if eng == "act":
                return S.add("act", lambda e: e.copy(out=out_ap, in_=in_ap), reads=reads, writes=writes)
            return S.add(eng, lambda e: e.tensor_copy(out=out_ap, in_=in_ap), reads=reads, writes=writes)

        dma("sp", CONST, consts_in, [], ["const"])
        dma("sp", PP, pp_in.rearrange("l p n -> p l n"), [], ["pp"])
        cp("dve", IDB, IDENT, ["const"], ["idb"])
        cp("dve", TRIB, CONST[:, 128:256], ["const"], ["trib"])
        cp("dve", TRIPB, CONST[:, 256:384], ["const"], ["tripb"])
        S.add("dve", lambda e: e.memset(ONESB, 1.0), writes=["onesb"])

        T0 = Temps(GEN, NF)
        XS = [T0.f32(2048) for _ in range(2)]
        XT = [T0.f32(512) for _ in range(2)]
        XTB = [T0.bf(512) for _ in range(2)]
        for blk in range(S_len // 128 if upto != "s" else 0):
            xs = XS[blk % 2]
            kxs = ("xs", blk % 2)
            dma("sp", xs, x_in[blk * 128:(blk + 1) * 128, :], [], [kxs])
            for g in range(4):
                b = (blk * 4 + g) % 8
                kp = ("ps", b)

                def tr(e, xs=xs, g=g, b=b):
                    for i in range(4):
                        c = g * 4 + i
                        inst = e.transpose(psum[b][:, i * 128:(i + 1) * 128], xs[:, c * 128:(c + 1) * 128], IDENT)
                    return inst
                S.add("pe", tr, reads=[kxs, "const"], writes=[kp])
                i2 = (blk * 4 + g) % 2
                cp("dve", XT[i2], psum[b][:, :], [kp], [("xt", i2)])
                cp("act", XTB[i2], psum[b][:, :], [kp], [("xtb", i2)])
                if upto in ("s1",):
                    continue
                dma("sp", xres[g * 4:(g + 1) * 4, :, blk * 128:(blk + 1) * 128].rearrange("c p n -> p c n"),
                    XT[i2].rearrange("p (c n) -> p c n", c=4), [("xt", i2)], [U("xres")], sem=("xts", i2))
                if upto in ("s2",):
                    continue
                dma("sp", xbf[g * 4:(g + 1) * 4, :, blk * 128:(blk + 1) * 128].rearrange("c p n -> p c n"),
                    XTB[i2].rearrange("p (c n) -> p c n", c=4), [("xtb", i2)], [U("xbf")], sem=("xtbs", i2))
        S.fence()

        SCALE_A = 64.0 ** -0.5
        SCALE_B = 192.0 ** -0.5

        for l in range(L if upto not in ("0", "s", "s1", "s2") else 0):
            wl_in, wl_uq, wl_ukv = w_in[l], w_uq[l], w_ukv[l]
            PPl = PP[:, l, :]
            dma("sp", ESINK, esink_in[l], [], ["esink"])
            act(ESINK, ESINK, AF.Exp, ["esink"], ["esink"])
            dma("sp", BSBC, bsbc_in[l].rearrange("p (g t) -> p g t", g=8), [], ["bsbc"])
            dma("pool", WST, wst_in[l].rearrange("p (g t) -> p g t", g=8), [], ["wst"], nofence=False)
            for g in range(8):
                tt("dve", WST[:, g, :], WST[:, g, :], TRIB, ALU.mult, ["wst", "trib"], ["wst"])
            S.add("dve", lambda e: e.memset(HALO, 0.0), writes=["halo"])
            S.fence()

            for t in range(NTG):
                tok0 = t * TG
                nkb = 8 * (t + 1)
                dma("pool", XB, xbf[:, :, tok0:tok0 + TG].rearrange("c p n -> p c n"), [], ["xb"])
                TB = Temps(GEN + 8192, NF)
                POSI = f32v(GEN, 1024)
                POSF = f32v(GEN + 1024, 1024)
                dma("sp", POSI.bitcast(I32), pos_in[:, tok0:tok0 + TG], [], ["posi"])
                S1 = f32v(GEN + 2048, 1024)
                S2 = f32v(GEN + 3072, 1024)
                KI = f32v(GEN + 4096, 1024)
                cp("dve", POSF, POSI.bitcast(I32), ["posi"], ["posf"])
                ts("dve", POSF, POSF, INVF, float(1.0 / (2.0 * np.pi)), ALU.mult, ALU.mult, ["posf", "const"], ["posf"])
                cp("dve", KI.bitcast(I32), POSF, ["posf"], ["ki"])
                cp("dve", S1, KI.bitcast(I32), ["ki"], ["s1"])
                tt("dve", POSF, POSF, S1, ALU.subtract, ["posf", "s1"], ["posf"])
                act(S1, POSF, AF.Sin, ["posf"], ["s1"], scale=float(np.pi))
                act(S2, POSF, AF.Sin, ["posf"], ["s2"], scale=float(np.pi / 2))
                tt("dve", S2, S2, S2, ALU.mult, ["s2"], ["s2"])
                ts("dve", S2, S2, -2.0, 1.0, ALU.mult, ALU.add, ["s2"], ["s2"])
                stt("dve", SIN, S1, 2.0, S2, ALU.mult, ALU.mult, ["s1", "s2"], ["sin"])
                tt("dve", S1, S1, S1, ALU.mult, ["s1"], ["s1"])
                ts("dve", COS, S1, -2.0, 1.0, ALU.mult, ALU.add, ["s1"], ["cos"])
                ts("dve", SIN, SIN, SGN, None, ALU.mult, None, ["sin", "const"], ["sin"])

                RAW = TB.f32(4096).rearrange("p (c n) -> p c n", c=4)
                SQ = [TB.bf(1024) for _ in range(2)]
                CQN = TB.bf(4096).rearrange("p (c n) -> p c n", c=4)
                CKVN = TB.bf(4096).rearrange("p (c n) -> p c n", c=4)
                RINV = TB.f32(1024)
                QN = [TB.bf(2048).rearrange("p (h n) -> p h n", h=2) for _ in range(2)]
                QR = [TB.bf(1024) for _ in range(2)]
                KT = TB.bf(S_len)
                VT = TB.bf(S_len).rearrange("p (b d) -> p b d", d=128)
                PT = [TB.bf(512) for _ in range(3)]
                DEN = [TB.f32(512) for _ in range(2)]
                KST = [TB.bf(1024) for _ in range(2)]
                VST = [TB.bf(512).rearrange("p (h d) -> p h d", h=4) for _ in range(2)]
                TR1 = TB.f32(1024)
                TR2 = TB.f32(1024)

                def latent(col0, gcol, dst, tag):
                    for half in range(2):
                        wv, wk = load_w([(wl_in, col0 + half * 256, 256, 0)], 16)
                        for cc in range(2):
                            c = half * 2 + cc

                            def ev(j, ps, kp, c=c):
                                sl = slice(j * 512, (j + 1) * 512)
                                cp("dve", RAW[:, c, sl], ps[:, :], [kp], [("raw", c, j)])
                                i2 = (c * 2 + j) % 2
                                act(SQ[i2][:, 0:512], ps[:, :], AF.Square, [kp], [("sq", i2)])
                                S.add("pe", lambda e, i2=i2, j=j, c=c: e.matmul(psum[6 + j][:, :], lhsT=ONESB, rhs=SQ[i2][:, 0:512],
                                                                                 start=(c == 0), stop=(c == 3)),
                                      reads=[("sq", i2), "onesb"], writes=[("ps", 6 + j)])
                            gemm([wv[:, k, cc * 128:(cc + 1) * 128] for k in range(16)],
                                 lambda k, j: XB[:, k, j * 512:(j + 1) * 512], ["xb"], wk, ev, pairs=(0, 1, 2))
                    for j in range(2):
                        sl = slice(j * 512, (j + 1) * 512)
                        act(RINV[:, sl], psum[6 + j][:, :], AF.Sqrt, [("ps", 6 + j)], [("rinv", j)], bias=EPSC, scale=1.0 / 512.0)
                        S.add("dve", lambda e, sl=sl: e.reciprocal(out=RINV[:, sl], in_=RINV[:, sl]), reads=[("rinv", j)], writes=[("rinv", j)])
                        for c in range(4):
                            stt("dve", dst[:, c, sl], RAW[:, c, sl], PPl[:, gcol + c:gcol + c + 1], RINV[:, sl], ALU.mult, ALU.mult,
                                [("raw", c, j), ("rinv", j), "pp"], [(tag, j)])

                latent(1280, QNG, CQN, "cqn")
                latent(1792, KVNG, CKVN, "ckvn")
                wv, wk = load_w([(wl_in, 2304, 64, 0), (wl_in, 2304, 64, 64), (wl_in, 2336, 32, 128), (wl_in, 2304, 32, 160),
                                 (wl_in, 2336, 32, 192), (wl_in, 2304, 32, 224)], 16)

                def ev_kr0(j, ps, kp):
                    sl = slice(j * 512, (j + 1) * 512)
                    tt("dve", TR1[:, sl], ps[:, :], COS[:, sl], ALU.mult, [kp, "cos"], [("tr1", j)])

                def ev_kr1(j, ps, kp):
                    sl = slice(j * 512, (j + 1) * 512)
                    tt("dve", TR2[:, sl], ps[:, :], SIN[:, sl], ALU.mult, [kp, "sin"], [("tr2", j)])
                    tt("dve", KR[:, tok0 + j * 512: tok0 + (j + 1) * 512], TR1[:, sl], TR2[:, sl], ALU.add,
                       [("tr1", j), ("tr2", j)], [("kr", t, j)])
                gemm([wv[:, k, 0:128] for k in range(16)], lambda k, j: XB[:, k, j * 512:(j + 1) * 512], ["xb"], wk, ev_kr0, pairs=(0, 1, 2))
                gemm([wv[:, k, 128:256] for k in range(16)], lambda k, j: XB[:, k, j * 512:(j + 1) * 512], ["xb"], wk, ev_kr1, pairs=(0, 1, 2))

                ckeys = [("ckvn", 0), ("ckvn", 1)]
                for hg in range(4):
                    wv, wk = load_w([(wl_ukv, hg * 1024, 1024, 0)], 4)
                    for hh in range(4):
                        h = hg * 4 + hh

                        def ev_k(j, ps, kp, h=h):
                            i2 = h % 2
                            sl = slice(j * 512, (j + 1) * 512)
                            cp("act", KST[i2][:, sl], ps[:, :], [kp], [("kst", i2, j)])
                            if j == 1:
                                dma("sp", kcache[h, :, tok0:tok0 + TG], KST[i2], [("kst", i2, 0), ("kst", i2, 1)], [("kc", h)],
                                    sem=("ksts", i2))
                        gemm([wv[:, k, hh * 256:hh * 256 + 128] for k in range(4)],
                             lambda k, j: CKVN[:, k, j * 512:(j + 1) * 512], ckeys, wk, ev_k, pairs=(0, 1, 2))
                    for blk in range(8):
                        b = 6 + (blk % 2)
                        kp = ("ps", b)

                        def mmv(e, wv=wv, blk=blk, b=b):
                            for k in range(4):
                                inst = e.matmul(psum[b][:, :].rearrange("p (h d) -> p h d", h=4),
                                                lhsT=CKVN[:, k, blk * 128:(blk + 1) * 128],
                                                rhs=wv.rearrange("p k (h d) -> p k h d", h=4)[:, k, :, 128:256],
                                                start=(k == 0), stop=(k == 3))
                            return inst
                        S.add("pe", mmv, reads=ckeys + wk, writes=[kp])
                        i2 = blk % 2
                        cp("act", VST[i2], psum[b][:, :].rearrange("p (h d) -> p h d", h=4), [kp], [("vst", i2)])
                        dma("sp", vcache[hg * 4:(hg + 1) * 4, tok0 + blk * 128: tok0 + (blk + 1) * 128, :].rearrange("h t d -> t h d"),
                            VST[i2], [("vst", i2)], [("vc", hg * 4 + i) for i in range(4)], sem=("vsts", i2))

                def q_pieces(hp):
                    h0 = 2 * hp
                    return [(wl_uq, h0 * 192, 128, 0), (wl_uq, (h0 + 1) * 192, 128, 128),
                            (wl_uq, h0 * 192 + 128, 64, 256), (wl_uq, (h0 + 1) * 192 + 128, 64, 320),
                            (wl_uq, h0 * 192 + 160, 32, 384), (wl_uq, h0 * 192 + 128, 32, 416),
                            (wl_uq, (h0 + 1) * 192 + 160, 32, 448), (wl_uq, (h0 + 1) * 192 + 128, 32, 480)]
                q_next = load_w(q_pieces(0), 4)
                for hp in range(8):
                    h0 = 2 * hp
                    qi = hp % 2
                    wv, wk = q_next
                    qkeys = [("cqn", 0), ("cqn", 1)]
                    for hh in range(2):
                        def ev_qn(j, ps, kp, hh=hh):
                            act(QN[qi][:, hh, j * 512:(j + 1) * 512], ps[:, :], AF.Identity, [kp], [("qn", qi, hh, j)], scale=SCALE_B)
                        gemm([wv[:, k, hh * 128:(hh + 1) * 128] for k in range(4)],
                             lambda k, j: CQN[:, k, j * 512:(j + 1) * 512], qkeys, wk, ev_qn, pairs=(0,))

                    def ev_q0(j, ps, kp):
                        sl = slice(j * 512, (j + 1) * 512)
                        stt("dve", TR1[:, sl], ps[:, :], SCALE_B, COS[:, sl], ALU.mult, ALU.mult, [kp, "cos"], [("tr1", j)])

                    def ev_q1(j, ps, kp):
                        sl = slice(j * 512, (j + 1) * 512)
                        stt("dve", TR2[:, sl], ps[:, :], SCALE_B, SIN[:, sl], ALU.mult, ALU.mult, [kp, "sin"], [("tr2", j)])
                        tt("dve", QR[qi][:, sl], TR1[:, sl], TR2[:, sl], ALU.add, [("tr1", j), ("tr2", j)], [("qr", qi, j)])
                    gemm([wv[:, k, 256:384] for k in range(4)], lambda k, j: CQN[:, k, j * 512:(j + 1) * 512], qkeys, wk, ev_q0, pairs=(0,))
                    gemm([wv[:, k, 384:512] for k in range(4)], lambda k, j: CQN[:, k, j * 512:(j + 1) * 512], qkeys, wk, ev_q1, pairs=(0,))

                    if hp < 7:
                        q_next = load_w(q_pieces(hp + 1), 4)
                    for hh in range(2):
                        h = h0 + hh
                        base = hh * 64
                        ntok = (t + 1) * TG
                        dma("pool", KT[:, 0:ntok], kcache[h, :, 0:ntok], [("kc", h)], ["kt"])
                        dma("pool", VT[:, 0:nkb, :], vcache[h, 0:ntok, :].rearrange("(b p) d -> p b d", p=128), [("vc", h)], ["vt"])
                        krk = [("kr", tt_, jj) for tt_ in range(t + 1) for jj in range(2)]
                        steps = [(j, kb) for j in range(2) for kb in range(8 * t + 4 * (j + 1))]

                        def geom(i):
                            j, kb = steps[i]
                            dloc = kb - (8 * t + 4 * j)
                            c0 = 128 * dloc if dloc > 0 else 0
                            return j, kb, dloc, c0, 512 - c0, 6 + (i % 2), PT[i % 3], ("pt", i % 3)

                        def emit_s(i):
                            j, kb, dloc, c0, ncol, sb, pt, kpt = geom(i)
                            qs = slice(j * 512 + c0, (j + 1) * 512)

                            def mms(e, kb=kb, qs=qs, ncol=ncol, sb=sb, hh=hh, base=base, qi=qi):
                                e.matmul(psum[sb][:, 0:ncol], lhsT=KT[:, kb * 128:(kb + 1) * 128], rhs=QN[qi][:, hh, qs],
                                         start=True, stop=False)
                                return e.matmul(psum[sb][:, 0:ncol], lhsT=KR[base:base + 64, kb * 128:(kb + 1) * 128],
                                                rhs=QR[qi][base:base + 64, qs], start=False, stop=True)
                            S.add("pe", mms, reads=["kt", ("qn", qi, hh, j), ("qr", qi, j)] + krk, writes=[("ps", sb)])
                            act(pt[:, 0:ncol], psum[sb][:, 0:ncol], AF.Exp, [("ps", sb)], [kpt])
                            if dloc >= 0:
                                tt("dve", pt[:, 0:128], pt[:, 0:128], TRIB, ALU.mult, [kpt, "trib"], [kpt])

                        def emit_o(i):
                            j, kb, dloc, c0, ncol, sb, pt, kpt = geom(i)
                            OB, DB = 2 + 2 * j, 3 + 2 * j
                            nk = 8 * t + 4 * (j + 1)

                            def mmo(e, kb=kb, c0=c0, ncol=ncol, pt=pt, OB=OB, DB=DB, nk=nk):
                                e.matmul(psum[OB][:, c0:512], lhsT=VT[:, kb, :], rhs=pt[:, 0:ncol], start=(kb == 0), stop=(kb == nk - 1))
                                return e.matmul(psum[DB][:, c0:512], lhsT=ONESB, rhs=pt[:, 0:ncol], start=(kb == 0), stop=(kb == nk - 1))
                            S.add("pe", mmo, reads=["vt", kpt, "onesb"], writes=[("ps", OB), ("ps", DB)])
                            if kb == nk - 1:
                                dn = DEN[j]
                                S.add("dve", lambda e, dn=dn, DB=DB: e.reciprocal(out=dn, in_=psum[DB][:, :]), reads=[("ps", DB)], writes=[("den", j)])
                                tt("dve", YB[:, h, j * 512:(j + 1) * 512], psum[OB][:, :], dn, ALU.mult, [("ps", OB), ("den", j)], [("yb", h)])
                        emit_s(0)
                        for i in range(len(steps)):
                            if i + 1 < len(steps):
                                emit_s(i + 1)
                            emit_o(i)
                S.fence()

                if upto == "B":
                    continue
                TA = Temps(GEN + 12288, NF)
                QA = TA.bf(8192).rearrange("p (c n) -> p c n", c=8)
                KA = TA.bf(2 * 1152).rearrange("p (g n) -> p g n", g=2)
                VA = TA.bf(9 * 256).rearrange("p (b g d) -> p b g d", b=9, g=2)
                PTA = [TA.bf(512) for _ in range(4)]
                DNA = [TA.f32(512) for _ in range(2)]
                for q in range(4):
                    wv, wk = load_w([(wl_in, q * 256, 256, 0)], 16)
                    for cc in range(2):
                        c = q * 2 + cc

                        def ev(j, ps, kp, c=c):
                            act(QA[:, c, j * 512:(j + 1) * 512], ps[:, :], AF.Identity, [kp], [("qa", c)], scale=SCALE_A)
                        gemm([wv[:, k, cc * 128:(cc + 1) * 128] for k in range(16)], lambda k, j: XB[:, k, j * 512:(j + 1) * 512],
                             ["xb"], wk, ev)
                wv, wk = load_w([(wl_in, 1024, 64, 0), (wl_in, 1024, 64, 64), (wl_in, 1088, 64, 128), (wl_in, 1088, 64, 192)], 16)
                wv2, wk2 = load_w([(wl_in, 1152, 64, 0), (wl_in, 1152, 64, 64), (wl_in, 1216, 64, 128), (wl_in, 1216, 64, 192)], 16)
                XH = TA.bf(16 * 128).rearrange("p (c n) -> p c n", c=16)
                if t > 0:
                    dma("pool", XH, xbf[:, :, tok0 - 128:tok0].rearrange("c p n -> p c n"), [], ["xh"])
                for g in range(2):
                    def ev(j, ps, kp, g=g):
                        cp("act", KA[:, g, 128 + j * 512:128 + (j + 1) * 512], ps[:, :], [kp], [("ka", g)])
                    gemm([wv[:, k, g * 128:(g + 1) * 128] for k in range(16)], lambda k, j: XB[:, k, j * 512:(j + 1) * 512],
                         ["xb"], wk, ev)
                if t > 0:
                    def mmh(e, wv=wv):
                        for g in range(2):
                            for k in range(16):
                                inst = e.matmul(psum[0][:, g * 128:(g + 1) * 128], lhsT=wv[:, k, g * 128:(g + 1) * 128], rhs=XH[:, k, :],
                                                start=(k == 0), stop=(k == 15))
                        return inst
                    S.add("pe", mmh, reads=wk + ["xh"], writes=[("ps", 0)])
                    cp("act", KA[:, :, 0:128], psum[0][:, 0:256].rearrange("p (g n) -> p g n", g=2), [("ps", 0)], [("ka", 0), ("ka", 1)])
                for blk in range(9):
                    if blk == 0 and t == 0:
                        continue
                    b = 6 + (blk % 2)
                    src = (lambda k: XH[:, k, :]) if blk == 0 else (lambda k, blk=blk: XB[:, k, (blk - 1) * 128:blk * 128])

                    def mmv(e, src=src, b=b, wv2=wv2):
                        for k in range(16):
                            inst = e.matmul(psum[b][:, 0:256], lhsT=src(k), rhs=wv2[:, k, 0:256], start=(k == 0), stop=(k == 15))
                        return inst
                    S.add("pe", mmv, reads=wk2 + ["xb", "xh"], writes=[("ps", b)])
                    cp("act", VA[:, blk, :, :], psum[b][:, 0:256].rearrange("p (g d) -> p g d", g=2), [("ps", b)], [("va", blk)])
                qakeys = [("qa", c) for c in range(8)]
                groups = [(n, g, par) for n in range(8) for g in range(2) for par in range(2)]

                def swa_geom(gi):
                    n, g, par = groups[gi]
                    ms = [1] if (t == 0 and n == 0) else [0, 1]
                    sbs = (6, 7) if gi % 2 == 0 else (0, 1)
                    return n, g, par, par * 64, ms, sbs, 2 + 2 * (gi % 2), 3 + 2 * (gi % 2)

                def swa_s(gi):
                    n, g, par, base, ms, sbs, OB, DB = swa_geom(gi)
                    for m in ms:
                        sb = sbs[m]
                        pi = (gi % 2) * 2 + m
                        pt = PTA[pi]
                        kpt = ("pta", pi)

                        def mms(e, n=n, g=g, base=base, m=m, sb=sb):
                            return e.matmul(psum[sb][:, :].rearrange("p (i q) -> p i q", i=4),
                                            lhsT=KA[base:base + 64, g, (n + m) * 128:(n + m + 1) * 128],
                                            rhs=QA[base:base + 64, 4 * g:4 * g + 4, n * 128:(n + 1) * 128], start=True, stop=True)
                        S.add("pe", mms, reads=[("ka", g)] + qakeys, writes=[("ps", sb)])
                        act(pt, psum[sb][:, :], AF.Exp, [("ps", sb)], [kpt])
                        msk = TRIB if m == 1 else TRIPB
                        tt("dve", pt.rearrange("p (i q) -> p i q", i=4), pt.rearrange("p (i q) -> p i q", i=4),
                           msk.unsqueeze(1).broadcast_to([128, 4, 128]), ALU.mult, [kpt, "trib", "tripb"], [kpt])

                def swa_o(gi):
                    n, g, par, base, ms, sbs, OB, DB = swa_geom(gi)
                    for mi, m in enumerate(ms):
                        pi = (gi % 2) * 2 + m
                        pt = PTA[pi]
                        kpt = ("pta", pi)

                        def mmo(e, n=n, g=g, m=m, pt=pt, OB=OB, DB=DB, first=(mi == 0), last=(mi == len(ms) - 1)):
                            e.matmul(psum[OB][:, :], lhsT=VA[:, n + m, g, :], rhs=pt, start=first, stop=last)
                            return e.matmul(psum[DB][:, :], lhsT=ONESB, rhs=pt, start=first, stop=last)
                        S.add("pe", mmo, reads=[("va", n + m), kpt, "onesb"], writes=[("ps", OB), ("ps", DB)])
                    dn = DNA[gi % 2]
                    kdn = ("dna", gi % 2)
                    hsl = slice(8 * g + par, 8 * g + 8, 2)
                    tt("dve", dn.rearrange("p (i q) -> p i q", i=4), psum[DB][:, :].rearrange("p (i q) -> p i q", i=4),
                       ESINK[:, hsl].unsqueeze(2).broadcast_to([128, 4, 128]), ALU.add, [("ps", DB), "esink"], [kdn])
                    S.add("dve", lambda e, dn=dn: e.reciprocal(out=dn, in_=dn), reads=[kdn], writes=[kdn])
                    tt("dve", YA[base:base + 64, 4 * g:4 * g + 4, n * 128:(n + 1) * 128],
                       psum[OB][base:base + 64, :].rearrange("p (i q) -> p i q", i=4),
                       dn[base:base + 64, :].rearrange("p (i q) -> p i q", i=4), ALU.mult, [("ps", OB), kdn], [("ya", n, g, par)])
                swa_s(0)
                for gi in range(len(groups)):
                    if gi + 1 < len(groups):
                        swa_s(gi + 1)
                    swa_o(gi)
                S.fence()

                if upto == "A":
                    continue
                TC = Temps(GEN + 16384, NF)
                VG = TC.f32(8192).rearrange("p (c n) -> p c n", c=8)
                VNT = VG.rearrange("p c n -> p (c n)").bitcast(BF16)[:, 0:8192].rearrange("p (g b c) -> p g b c", g=8, b=8)
                VN = TC.bf(8192).rearrange("p (c n) -> p c n", c=8)
                TB2 = [TC.bf(1024)] * 2
                TS2 = [TC.bf(1024)] * 2
                MEAN = COS
                RSTD = SIN
                UG = [TC.bf(1024)] * 2
                TMPC = TC.f32(512)
                for q in range(4):
                    wv, wk = load_w([(wl_in, 3392 + q * 256, 256, 0)], 16)
                    for cc in range(2):
                        c = q * 2 + cc

                        def ev(j, ps, kp, c=c):
                            sl = slice(j * 512, (j + 1) * 512)
                            i2 = 0
                            act(VG[:, c, sl], ps[:, :], AF.Gelu, [kp], [("vg", c, j)])
                            cp("dve", TB2[i2][:, sl], VG[:, c, sl], [("vg", c, j)], [("tb2", i2, j)])
                            tt("dve", TS2[i2][:, sl], VG[:, c, sl], VG[:, c, sl], ALU.mult, [("vg", c, j)], [("ts2", i2, j)])

                            def mst(e, i2=i2, j=j, c=c, sl=sl):
                                e.matmul(psum[4 + j][:, :], lhsT=ONESB, rhs=TB2[i2][:, sl], start=(c == 0), stop=(c == 7))
                                return e.matmul(psum[6 + j][:, :], lhsT=ONESB, rhs=TS2[i2][:, sl], start=(c == 0), stop=(c == 7))
                            S.add("pe", mst, reads=[("tb2", i2, j), ("ts2", i2, j), "onesb"], writes=[("ps", 4 + j), ("ps", 6 + j)])
                        gemm([wv[:, k, cc * 128:(cc + 1) * 128] for k in range(16)], lambda k, j: XB[:, k, j * 512:(j + 1) * 512],
                             ["xb"], wk, ev, pairs=(0, 1))
                for j in range(2):
                    sl = slice(j * 512, (j + 1) * 512)
                    ts("dve", MEAN[:, sl], psum[4 + j][:, :], 1.0 / 1024.0, None, ALU.mult, None, [("ps", 4 + j)], [("mean", j)])
                    tt("dve", TMPC, MEAN[:, sl], MEAN[:, sl], ALU.mult, [("mean", j)], ["tmpc"])
                    stt("dve", RSTD[:, sl], psum[6 + j][:, :], 1.0 / 1024.0, TMPC, ALU.mult, ALU.subtract, [("ps", 6 + j), "tmpc"], [("rstd", j)])
                    act(RSTD[:, sl], RSTD[:, sl], AF.Sqrt, [("rstd", j)], [("rstd", j)], bias=EPSC)
                    S.add("dve", lambda e, sl=sl, RSTD=RSTD: e.reciprocal(out=RSTD[:, sl], in_=RSTD[:, sl]), reads=[("rstd", j)], writes=[("rstd", j)])
                    for c in range(8):
                        tt("dve", VG[:, c, sl], VG[:, c, sl], MEAN[:, sl], ALU.subtract, [("vg", c, j), ("mean", j)], [("vg", c, j)])
                        tt("dve", VG[:, c, sl], VG[:, c, sl], RSTD[:, sl], ALU.mult, [("vg", c, j), ("rstd", j)], [("vg", c, j)])
                        act(VN[:, c, sl], VG[:, c, sl], AF.Identity, [("vg", c, j), "pp"], [("vn", c, j)],
                            bias=PPl[:, SLB + c:SLB + c + 1], scale=PPl[:, SLG + c:SLG + c + 1])
                S.fence()
                PSB = [psum[b][:, :].bitcast(BF16) for b in range(8)]
                for g in range(8):
                    for half in range(2):
                        b = (g * 2 + half) % 2

                        def trn(e, g=g, half=half, b=b):
                            for i in range(4):
                                blk = half * 4 + i
                                inst = e.transpose(PSB[b][:, i * 128:(i + 1) * 128], VN[:, g, blk * 128:(blk + 1) * 128], IDB)
                            return inst
                        S.add("pe", trn, reads=[("vn", g, half), "idb"], writes=[("ps", b)])
                        cp("act", VNT[:, g, half * 4:half * 4 + 4, :], PSB[b][:, 0:512].rearrange("p (b c) -> p b c", b=4),
                           [("ps", b)], [("vnt", g, half)])
                for q in range(4):
                    wv, wk = load_w([(wl_in, 2368 + q * 256, 256, 0)], 16)
                    for cc in range(2):
                        g = q * 2 + cc
                        ug = UG[g % 2]

                        def ev(j, ps, kp, g=g, ug=ug):
                            act(ug[:, j * 512:(j + 1) * 512], ps[:, :], AF.Gelu, [kp], [("ug", 0, j)])
                        gemm([wv[:, k, cc * 128:(cc + 1) * 128] for k in range(16)], lambda k, j: XB[:, k, j * 512:(j + 1) * 512],
                             ["xb"], wk, ev, pairs=(1, 2))
                        for half in range(2):
                            b = 6 + half

                            def mix(e, g=g, half=half, b=b):
                                for i in range(4):
                                    inst = e.matmul(psum[b][:, i * 128:(i + 1) * 128], lhsT=VNT[:, g, half * 4 + i, :], rhs=WST[:, g, :],
                                                    start=True, stop=True)
                                return inst
                            S.add("pe", mix, reads=[("vnt", g, half), "wst"], writes=[("ps", b)])
                            tt("dve", TMPC.rearrange("p (b t) -> p b t", b=4), psum[b][:, :].rearrange("p (b t) -> p b t", b=4),
                               BSBC[:, g, :].unsqueeze(1).broadcast_to([128, 4, 128]), ALU.add, [("ps", b), "bsbc"], ["tmpc"])
                            tt("dve", YC[:, g, half * 512:(half + 1) * 512], TMPC, ug[:, half * 512:(half + 1) * 512], ALU.mult,
                               ["tmpc", ("ug", 0, half)], [("yc", g)])
                S.fence()

                if upto == "C":
                    continue
                TD = Temps(GEN + 16384, NF)
                MERGED = TD.bf(16384).rearrange("p (c n) -> p c n", c=16)
                GT = [TD.f32(1024) for _ in range(2)]
                MTS = [TD.f32(1024) for _ in range(2)]
                TT_ = TD.f32(1024)
                branches = [(w_pa[l], 8, YA, 0), (w_pb[l], 16, YB, 1), (w_pc[l], 8, YC, 2)]
                ykeys_all = {0: [("ya", n, g, p) for n in range(8) for g in range(2) for p in range(2)],
                             1: [("yb", h) for h in range(16)], 2: [("yc", g) for g in range(8)]}
                for jp in range(8):
                    for i, (wpr, kcn, Y, bi) in enumerate(branches):
                        gv, gk = load_w([(wl_in, 4416 + i * 2048 + jp * 256, 256, 0)], 16)
                        pv, pk = load_w([(wpr, jp * 256, 256, 0)], kcn)
                        for cc in range(2):
                            jd = jp * 2 + cc
                            gt = GT[cc]
                            mt = MTS[cc]

                            def ev_g(j, ps, kp, gt=gt, i=i, jd=jd, cc=cc):
                                act(gt[:, j * 512:(j + 1) * 512], ps[:, :], AF.Sigmoid, [kp, "pp"], [("gt", cc, j)],
                                    bias=PPl[:, BG + i * 16 + jd:BG + i * 16 + jd + 1])
                            gemm([gv[:, k, cc * 128:(cc + 1) * 128] for k in range(16)], lambda k, j: XB[:, k, j * 512:(j + 1) * 512],
                                 ["xb"], gk, ev_g)

                            def ev_p(j, ps, kp, gt=gt, mt=mt, i=i, jd=jd, cc=cc):
                                sl = slice(j * 512, (j + 1) * 512)
                                if i == 0:
                                    tt("dve", mt[:, sl], ps[:, :], gt[:, sl], ALU.mult, [kp, ("gt", cc, j)], [("mt", cc, j)])
                                elif i == 1:
                                    tt("dve", TT_[:, sl], ps[:, :], gt[:, sl], ALU.mult, [kp, ("gt", cc, j)], [("tt", j)])
                                    tt("dve", mt[:, sl], mt[:, sl], TT_[:, sl], ALU.add, [("mt", cc, j), ("tt", j)], [("mt", cc, j)])
                                else:
                                    tt("dve", TT_[:, sl], ps[:, :], gt[:, sl], ALU.mult, [kp, ("gt", cc, j)], [("tt", j)])
                                    tt("dve", MERGED[:, jd, sl], mt[:, sl], TT_[:, sl], ALU.add, [("mt", cc, j), ("tt", j)], [("merged", jd)])
                            gemm([pv[:, k, cc * 128:(cc + 1) * 128] for k in range(kcn)],
                                 lambda k, j, Y=Y: Y[:, k, j * 512:(j + 1) * 512], ykeys_all[i], pk, ev_p)
                S.fence()

                if upto == "D":
                    continue
                def ln_stage(TT, TT2, nK, lhs_groups_fn, rhs_fn, rkeys, gcol, bcol, final, xr_in_tt=True):
                    NXR = 6
                    XR = [(TT if xr_in_tt else TT2).f32(1024) for _ in range(NXR)]
                    ZB = [TT.bf(1024) for _ in range(2)]
                    ZS = [TT.bf(1024) for _ in range(2)]
                    MEAN = TT.f32(1024)
                    RSTD = TT.f32(1024)
                    TMP = TT2.f32(512)
                    XO = [TT2.bf(1024) for _ in range(2)]
                    pend = []
                    for jd in range(16):
                        xr = XR[jd % NXR]
                        kxr = ("xr", jd % NXR)
                        dma("pool", xr, xres[jd, :, tok0:tok0 + TG], [], [kxr])
                        lhs, wk = lhs_groups_fn(jd)

                        def ev(j, ps, kp, jd=jd, xr=xr, kxr=kxr):
                            sl = slice(j * 512, (j + 1) * 512)
                            i2 = jd % 2
                            stt("dve", xr[:, sl], xr[:, sl], ALPHA, ps[:, :], ALU.mult, ALU.add, [kp, kxr], [kxr])
                            cp("act", ZB[i2][:, sl], xr[:, sl], [kxr], [("zb", i2, j)])
                            act(ZS[i2][:, sl], xr[:, sl], AF.Square, [kxr], [("zs", i2, j)])

                            def mst(e, i2=i2, j=j, jd=jd, sl=sl):
                                e.matmul(psum[4 + j][:, :], lhsT=ONESB, rhs=ZB[i2][:, sl], start=(jd == 0), stop=(jd == 15))
                                return e.matmul(psum[6 + j][:, :], lhsT=ONESB, rhs=ZS[i2][:, sl], start=(jd == 0), stop=(jd == 15))
                            pend.append((mst, [("zb", i2, j), ("zs", i2, j), "onesb"], [("ps", 4 + j), ("ps", 6 + j)]))
                            if j == 1:
                                dma("sp", zscr[jd, :, :], xr, [kxr], [("zscr", jd)], sem=("zst", jd % NXR))
                        old_pend = list(pend)
                        del pend[:]
                        gemm(lhs, rhs_fn, rkeys, wk, ev, pairs=(0, 1), after_mm=lambda old_pend=old_pend: [S.add("pe", f_, reads=r_, writes=w_) for f_, r_, w_ in old_pend])
                    for f_, r_, w_ in pend:
                        S.add("pe", f_, reads=r_, writes=w_)
                    for j in range(2):
                        sl = slice(j * 512, (j + 1) * 512)
                        ts("dve", MEAN[:, sl], psum[4 + j][:, :], 1.0 / D, None, ALU.mult, None, [("ps", 4 + j)], [("mean", j)])
                        tt("dve", TMP, MEAN[:, sl], MEAN[:, sl], ALU.mult, [("mean", j)], ["tmp"])
                        stt("dve", RSTD[:, sl], psum[6 + j][:, :], 1.0 / D, TMP, ALU.mult, ALU.subtract, [("ps", 6 + j), "tmp"], [("rstd", j)])
                        act(RSTD[:, sl], RSTD[:, sl], AF.Sqrt, [("rstd", j)], [("rstd", j)], bias=EPSC)
                        S.add("dve", lambda e, sl=sl, RSTD=RSTD: e.reciprocal(out=RSTD[:, sl], in_=RSTD[:, sl]), reads=[("rstd", j)], writes=[("rstd", j)])
                    mk = [("mean", 0), ("mean", 1), ("rstd", 0), ("rstd", 1)]
                    for jd in range(16):
                        xr = XR[jd % NXR]
                        kxr = ("xr", jd % NXR)
                        dma("pool", xr, zscr[jd, :, :], [("zscr", jd)], [kxr])
                        tt("dve", xr, xr, MEAN, ALU.subtract, [kxr] + mk, [kxr])
                        tt("dve", xr, xr, RSTD, ALU.mult, [kxr] + mk, [kxr])
                        act(xr, xr, AF.Identity, [kxr, "pp"], [kxr], bias=PPl[:, bcol + jd:bcol + jd + 1], scale=PPl[:, gcol + jd:gcol + jd + 1])
                        if not final:
                            dma("sp", xres[jd, :, tok0:tok0 + TG], xr, [kxr], [U("xres")], sem=("xrs", jd % NXR))
                            xo = XO[jd % 2]
                            cp("dve", xo, xr, [kxr], [("xo", jd % 2)])
                            dma("sp", xbf[jd, :, tok0:tok0 + TG], xo, [("xo", jd % 2)], [U("xbf")], sem=("xos", jd % 2))
                        else:
                            dma("sp", xres[jd, :, tok0:tok0 + TG], xr, [kxr], [("xresf", jd)], sem=("xrs", jd % NXR))

                TE = Temps(GEN, GEN + 16384)

                def lhs_wo(jd, cache={}):
                    jp = jd // 2
                    if jp not in cache:
                        cache.clear()
                        cache[jp] = load_w([(w_o[l], jp * 256, 256, 0)], 16)
                    wv, wk = cache[jp]
                    cc = jd % 2
                    return [wv[:, k, cc * 128:(cc + 1) * 128] for k in range(16)], wk
                ln_stage(TE, TE, 16, lhs_wo, lambda k, j: MERGED[:, k, j * 512:(j + 1) * 512], [("merged", c) for c in range(16)], L1G, L1B, False)
                S.fence()
                dma("pool", XB, xbf[:, :, tok0:tok0 + TG].rearrange("c p n -> p c n"), [], ["xb"])

                if upto == "E":
                    continue
                TF = Temps(GEN, NF)
                ACTT = TF.bf(44 * 1024).rearrange("p (c n) -> p c n", c=44)
                UP = [[TF.f32(1026) for _ in range(2)] for _ in range(2)]
                ACC = [TF.f32(1024) for _ in range(2)]
                SG = TF.f32(1024)
                for c in range(44):
                    wv, wk = load_w([(w_up[l], c * 128, 128, 0), (w_up[l], DFF + c * 128, 128, 128)], 16)
                    for hv in range(2):
                        up = UP[hv][c % 2]
                        kup = ("up", hv, c % 2)
                        cc = c + 44 * hv
                        cp("act", up[:, 0:2], HALO[:, cc, :], ["halo"], [(kup, "h")])

                        def ev(j, ps, kp, up=up, kup=kup):
                            cp("act", up[:, 2 + j * 512:2 + (j + 1) * 512], ps[:, :], [kp], [(kup, j)])
                        gemm([wv[:, k, hv * 128:(hv + 1) * 128] for k in range(16)], lambda k, j: XB[:, k, j * 512:(j + 1) * 512],
                             ["xb"], wk, ev)
                        ku = [(kup, "h"), (kup, 0), (kup, 1)]
                        acc = ACC[hv]
                        ka = ("acc", hv)
                        ts("dve", acc, up[:, 2:1026], PPl[:, CW + 2 * 88 + cc:CW + 2 * 88 + cc + 1], PPl[:, CB + cc:CB + cc + 1],
                           ALU.mult, ALU.add, ku + ["pp"], [ka])
                        stt("dve", acc, up[:, 1:1025], PPl[:, CW + 88 + cc:CW + 88 + cc + 1], acc, ALU.mult, ALU.add, ku + [ka, "pp"], [ka])
                        stt("dve", acc, up[:, 0:1024], PPl[:, CW + cc:CW + cc + 1], acc, ALU.mult, ALU.add, ku + [ka, "pp"], [ka])
                        cp("act", HALO[:, cc, :], up[:, 1024:1026], ku, ["halo"])
                    act(SG, ACC[0], AF.Silu, [("acc", 0)], ["sg"])
                    tt("dve", ACTT[:, c, :], SG, ACC[1], ALU.mult, ["sg", ("acc", 1)], [("actt", c)])
                S.fence()

                if upto == "F":
                    continue
                TGm = Temps(GEN + 22528, NF)
                last = (l == L - 1)

                def lhs_wd(jd):
                    a = load_w([(w_down[l][0:2816, :], jd * 128, 128, 0)], 22)
                    b = load_w([(w_down[l][2816:5632, :], jd * 128, 128, 0)], 22)
                    return [a[0][:, k, :] for k in range(22)] + [b[0][:, k, :] for k in range(22)], a[1] + b[1]
                ln_stage(TGm, Temps(XB_off, XB_off + 8192), 44, lhs_wd, lambda k, j: ACTT[:, k, j * 512:(j + 1) * 512], [("actt", c) for c in range(44)], L2G, L2B, last, xr_in_tt=False)
                if last:
                    S.fence()
                if last:
                    TO = Temps(XB_off + 2048, XB_off + 8192)
                    XF = [TO.f32(512) for _ in range(2)]
                    OS = [TO.f32(2048) for _ in range(2)]
                    for blk in range(8):
                        osb = OS[blk % 2]
                        for g in range(4):
                            xf = XF[g % 2]
                            kxf = ("xf", g % 2)
                            dma("pool", xf.rearrange("p (c n) -> p c n", c=4),
                                xres[g * 4:(g + 1) * 4, :, tok0 + blk * 128: tok0 + (blk + 1) * 128].rearrange("c p n -> p c n"),
                                [("xresf", g * 4 + i) for i in range(4)], [kxf])
                            b = (blk * 4 + g) % 4

                            def tr(e, xf=xf, b=b):
                                for i in range(4):
                                    inst = e.transpose(psum[b][:, i * 128:(i + 1) * 128], xf[:, i * 128:(i + 1) * 128], IDENT)
                                return inst
                            S.add("pe", tr, reads=[kxf, "const"], writes=[("ps", b)])
                            cp("act" if g % 2 else "dve", osb[:, g * 512:(g + 1) * 512], psum[b][:, :], [("ps", b)], [("os", blk % 2, g)])
                        dma("sp", out[tok0 + blk * 128: tok0 + (blk + 1) * 128, :], osb, [("os", blk % 2, g) for g in range(4)],
                            [U("out")], sem=("oss", blk % 2))
                S.fence()
        S.fence()
        global LAST_SCHED
        LAST_SCHED = S
        S.emit(nc, st)
    return nc


def _pack_cols(v):
    v = np.asarray(v, np.float32).reshape(-1)
    return np.ascontiguousarray(v.reshape(-1, 128).T)


def _host_prep(inputs, L=2):
    c = np.zeros((128, 640), np.float32)
    c[:, 0:128] = np.eye(128, dtype=np.float32)
    k = np.arange(128)[:, None]
    q = np.arange(128)[None, :]
    c[:, 128:256] = (k <= q)
    c[:, 256:384] = (k > q)
    inv = (10000.0 ** (-np.arange(0, 64, 2, dtype=np.float32) / 64.0)).astype(np.float32)
    p = np.arange(128)
    c[:, 384] = inv[p % 32]
    c[:, 385] = np.where((p % 64) < 32, -1.0, 1.0)
    c[:, 386] = EPS
    pp = np.zeros((L, 128, NPP), np.float32)
    for l in range(L):
        pp[l, :, BG:BG + 48] = _pack_cols(inputs["b_gate"][l])
        pp[l, :, QNG:QNG + 4] = _pack_cols(inputs["q_norm_g"][l])
        pp[l, :, KVNG:KVNG + 4] = _pack_cols(inputs["kv_norm_g"][l])
        pp[l, :, SLG:SLG + 8] = _pack_cols(inputs["sgu_ln_g"][l])
        pp[l, :, SLB:SLB + 8] = _pack_cols(inputs["sgu_ln_b"][l])
        pp[l, :, L1G:L1G + 16] = _pack_cols(inputs["ln1_g"][l])
        pp[l, :, L1B:L1B + 16] = _pack_cols(inputs["ln1_b"][l])
        pp[l, :, L2G:L2G + 16] = _pack_cols(inputs["ln2_g"][l])
        pp[l, :, L2B:L2B + 16] = _pack_cols(inputs["ln2_b"][l])
        pp[l, :, CW:CW + 264] = _pack_cols(inputs["conv_w"][l])
        pp[l, :, CB:CB + 88] = _pack_cols(inputs["conv_b"][l])
    esink = np.ascontiguousarray(np.broadcast_to(np.asarray(inputs["sinks"], np.float32)[:L, None, :], (L, 128, 16)))
    bsbc = np.ascontiguousarray(np.broadcast_to(np.asarray(inputs["sgu_b"], np.float32)[:L].reshape(L, 1, 1024), (L, 128, 1024)))
    wst = np.ascontiguousarray(np.asarray(inputs["sgu_w"], np.float32)[:L].transpose(0, 3, 1, 2).reshape(L, 128, 1024))
    shared = {"consts": c, "pp": pp, "esink": esink, "bsbc": bsbc, "wst": wst}
    for k_ in ("w_in", "w_uq", "w_ukv", "w_proj_a", "w_proj_b", "w_proj_c", "w_o", "w_up", "w_down"):
        shared[k_] = np.ascontiguousarray(np.asarray(inputs[k_], np.float32)[:L])
    return shared


_NC_CACHE = {}


def kernel(**inputs):
    x = np.asarray(inputs["x"], np.float32)
    pos = np.asarray(inputs["positions"], np.int32)
    B, S_len, _ = x.shape
    L = 2
    key = (S_len, L)
    if key not in _NC_CACHE:
        _NC_CACHE[key] = build(S_len, L)
    nc = _NC_CACHE[key]
    shared = _host_prep(inputs, L)
    in_maps = []
    for b in range(B):
        m = dict(shared)
        m["x"] = np.ascontiguousarray(x[b])
        m["pos_bc"] = np.ascontiguousarray(np.broadcast_to(pos[b][None, :], (128, S_len)))
        in_maps.append(m)
    res = run_bass_kernel_spmd(nc, in_maps, core_ids=list(range(B)))
    return np.stack([np.asarray(r["out"], np.float32) for r in res.results], axis=0)
```

```python
from contextlib import ExitStack
import numpy as np
import concourse.bass as bass
import concourse.mybir as mybir
from concourse.bass_utils import run_bass_kernel_spmd

F32 = mybir.dt.float32
BF16 = mybir.dt.bfloat16
I32 = mybir.dt.int32
AF = mybir.ActivationFunctionType
ALU = mybir.AluOpType

D = 2048
NIN = 10560
DFF = 5632
TG = 1024
EPS = 1e-5
ALPHA = 4.0 ** 0.25
ENGS = ("pe", "act", "dve", "pool", "sp")
FENCED = ("pe", "act", "dve", "sp", "pool")
DMA_ENG = {}

BG, QNG, KVNG, SLG, SLB, L1G, L1B, L2G, L2B, CW, CB, NPP = 0, 48, 52, 56, 64, 72, 88, 104, 120, 136, 400, 488


class _Op:
    __slots__ = ("eng", "fn", "deps", "idx", "eidx", "is_dma", "semkey", "ordinal", "signal", "tick")


class Sched:
    def __init__(self):
        self.ops = []
        self.eng_ops = {e: [] for e in ENGS}
        self.last_w = {}
        self.readers = {}
        self.dma_count = {}
        self.dma_since_fence = []

    def add(self, eng, fn, reads=(), writes=(), dma=False, sem=None, nofence=False):
        op = _Op()
        op.eng, op.fn, op.is_dma = eng, fn, dma
        op.idx = len(self.ops)
        op.eidx = len(self.eng_ops[eng])
        op.signal = False
        op.tick = 0
        ps_reads = [k for k in reads if isinstance(k, tuple) and k and k[0] == "ps"]
        if ps_reads:
            reads = [k for k in reads if k not in ps_reads]
            writes = list(writes) + ps_reads
        deps = set()
        for k in reads:
            w = self.last_w.get(k)
            if w is not None:
                deps.add(w)
        for k in writes:
            w = self.last_w.get(k)
            if w is not None:
                deps.add(w)
            for r in self.readers.get(k, ()):
                deps.add(r)
        op.deps = deps
        for k in writes:
            self.last_w[k] = op
            self.readers[k] = []
        for k in reads:
            lst = self.readers.setdefault(k, [])
            if not dma:
                lst[:] = [r for r in lst if r.is_dma or r.eng != eng]
            lst.append(op)
        if dma:
            if sem is None:
                sem = ("dma", tuple(writes)[0] if writes else tuple(reads)[0])
            op.semkey = sem
            n = self.dma_count.get(sem, 0) + 1
            self.dma_count[sem] = n
            op.ordinal = n
            if not nofence:
                self.dma_since_fence.append(op)
        else:
            op.semkey = None
            op.ordinal = 0
        self.ops.append(op)
        self.eng_ops[eng].append(op)
        return op

    def fence(self):
        lasts = []
        for e in FENCED:
            for o in reversed(self.eng_ops[e]):
                if o.fn is not None and not o.is_dma:
                    lasts.append(o)
                    break
        dmas = list(self.dma_since_fence)
        self.dma_since_fence = []
        for e in FENCED:
            op = self.add(e, None)
            op.deps = set(o for o in lasts if o.eng != e) | set(dmas)

    def _needs_wait(self, op, d):
        if d.is_dma or d.eng != op.eng:
            return True
        return op.eng != "pe"

    def emit(self, nc, stack):
        for op in self.ops:
            for d in op.deps:
                if self._needs_wait(op, d):
                    d.signal = True
        for e in ENGS:
            t = 0
            for op in self.eng_ops[e]:
                if op.is_dma:
                    continue
                if op.signal:
                    t += 1
                op.tick = t
        eng_sem = {e: stack.enter_context(nc.semaphore("s_" + e)) for e in ENGS}
        dma_sem = {}
        for k in self.dma_count:
            dma_sem[k] = stack.enter_context(nc.semaphore("d%d" % len(dma_sem)))
        block = stack.enter_context(nc.Block())

        def run(e, eng):
            known = {}
            for op in self.eng_ops[e]:
                need = {}
                for d in op.deps:
                    if not self._needs_wait(op, d):
                        continue
                    if d.is_dma:
                        s, v = dma_sem[d.semkey], 16 * d.ordinal
                    else:
                        s, v = eng_sem[d.eng], d.tick
                    if need.get(s, 0) < v:
                        need[s] = v
                for s, v in need.items():
                    if known.get(s, 0) >= v:
                        continue
                    known[s] = v
                    eng.wait_ge(s, v)
                if op.fn is None:
                    continue
                inst = op.fn(eng)
                if op.is_dma:
                    inst.then_inc(dma_sem[op.semkey], 16)
                elif op.signal:
                    inst.then_inc(eng_sem[e], 1)

        block.tensor(lambda eng: run("pe", eng))
        block.scalar(lambda eng: run("act", eng))
        block.vector(lambda eng: run("dve", eng))
        block.gpsimd(lambda eng: run("pool", eng))
        block.sync(lambda eng: run("sp", eng))


def build(S_len=4096, L=2, upto=None):
    nc = bass.Bass("TRN2", target_bir_lowering=False)
    NTG = S_len // TG
    dt = lambda name, shape, dtype, kind: nc.dram_tensor(name, shape, dtype, kind=kind).ap()
    x_in = dt("x", [S_len, D], F32, "ExternalInput")
    pos_in = dt("pos_bc", [128, S_len], I32, "ExternalInput")
    consts_in = dt("consts", [128, 640], F32, "ExternalInput")
    pp_in = dt("pp", [L, 128, NPP], F32, "ExternalInput")
    esink_in = dt("esink", [L, 128, 16], F32, "ExternalInput")
    bsbc_in = dt("bsbc", [L, 128, 1024], F32, "ExternalInput")
    wst_in = dt("wst", [L, 128, 1024], F32, "ExternalInput")
    w_in = dt("w_in", [L, D, NIN], F32, "ExternalInput")
    w_uq = dt("w_uq", [L, 512, 3072], F32, "ExternalInput")
    w_ukv = dt("w_ukv", [L, 512, 4096], F32, "ExternalInput")
    w_pa = dt("w_proj_a", [L, 1024, D], F32, "ExternalInput")
    w_pb = dt("w_proj_b", [L, 2048, D], F32, "ExternalInput")
    w_pc = dt("w_proj_c", [L, 1024, D], F32, "ExternalInput")
    w_o = dt("w_o", [L, D, D], F32, "ExternalInput")
    w_up = dt("w_up", [L, D, 2 * DFF], F32, "ExternalInput")
    w_down = dt("w_down", [L, DFF, D], F32, "ExternalInput")
    out = dt("out", [S_len, D], F32, "ExternalOutput")
    xres = dt("xres", [16, 128, S_len], F32, "Internal")
    xbf = dt("xbf", [16, 128, S_len], BF16, "Internal")
    zscr = dt("zscr", [16, 128, TG], F32, "Internal")
    kcache = dt("kcache", [16, 128, S_len], BF16, "Internal")
    vcache = dt("vcache", [16, S_len, 128], BF16, "Internal")

    S = Sched()
    st = ExitStack()
    with st:
        NF = 53100
        arena = st.enter_context(nc.sbuf_tensor("arena", [128, NF], F32))
        psum = [st.enter_context(nc.psum_tensor("ps%d" % i, [128, 512], F32)) for i in range(8)]
        cur = [0]

        def carve(n):
            o = cur[0]
            cur[0] += n
            assert cur[0] <= NF, cur[0]
            return o

        def f32v(off, n):
            return arena[:, off:off + n]

        def bfv(off, nbf):
            return arena[:, off:off + nbf // 2].bitcast(BF16)

        o = carve(640); CONST = f32v(o, 640)
        IDENT = CONST[:, 0:128]
        INVF = CONST[:, 384:385]
        SGN = CONST[:, 385:386]
        EPSC = CONST[:, 386:387]
        o = carve(256); CB16 = bfv(o, 512)
        IDB, ONESB, TRIB, TRIPB = CB16[:, 0:128], CB16[:, 128:256], CB16[:, 256:384], CB16[:, 384:512]
        o = carve(L * NPP); PP = f32v(o, L * NPP).rearrange("p (l n) -> p l n", l=L)
        o = carve(16); ESINK = f32v(o, 16)
        o = carve(1024); BSBC = f32v(o, 1024).rearrange("p (g t) -> p g t", g=8)
        o = carve(512); WST = bfv(o, 1024).rearrange("p (g t) -> p g t", g=8)
        o = carve(1024); COS = f32v(o, 1024)
        o = carve(1024); SIN = f32v(o, 1024)
        o = carve(176); HALO = f32v(o, 176).rearrange("p (c t) -> p c t", t=2)
        o = carve(S_len // 2); KR = bfv(o, S_len)
        o = carve(8192); XB = bfv(o, 16384).rearrange("p (c n) -> p c n", c=16)
        XB_off = o
        NW = 3
        WSL = []
        for i in range(NW):
            o = carve(2048)
            WSL.append(bfv(o, 4096))
        GEN = cur[0]
        GEN_N = NF - GEN
        YB = bfv(GEN, 16384).rearrange("p (c n) -> p c n", c=16)
        YA = bfv(GEN + 8192, 8192).rearrange("p (c n) -> p c n", c=8)
        YC = bfv(GEN + 12288, 8192).rearrange("p (c n) -> p c n", c=8)

        class Temps:
            def __init__(self, base, limit):
                self.o, self.limit = base, limit

            def f32(self, n):
                o = self.o
                self.o += n
                assert self.o <= self.limit, (self.o, self.limit)
                return f32v(o, n)

            def bf(self, n):
                assert n % 2 == 0
                o = self.o
                self.o += n // 2
                assert self.o <= self.limit, (self.o, self.limit)
                return bfv(o, n)

        uid = [0]

        def U(prefix):
            uid[0] += 1
            return (prefix, uid[0])

        def dma(eng, out_ap, in_ap, reads, writes, sem=None, nofence=False):
            eng = DMA_ENG.get(eng, eng)
            return S.add(eng, lambda e, o=out_ap, i=in_ap: e.dma_start(out=o, in_=i),
                         reads=reads, writes=writes, dma=True, sem=sem, nofence=nofence)

        wctr = [0]

        def load_w(pieces, KC):
            slot = wctr[0] % NW
            wctr[0] += 1
            tot = max(p[3] + p[2] for p in pieces)
            assert KC * tot <= 4096, (KC, tot)
            view = WSL[slot][:, 0:KC * tot].rearrange("p (k n) -> p k n", k=KC)
            keys = []
            for i, (w2, c0, n, dc) in enumerate(pieces):
                key = ("w", slot, i)
                src = w2.rearrange("(k p) n -> p k n", p=128)[:, :, c0:c0 + n]
                dma("pool", view[:, :, dc:dc + n], src, reads=[], writes=[key], sem=("wsem", slot, i), nofence=True)
                keys.append(key)
            return view, keys

        pair_ctr = [0]

        def gemm(lhs_list, rhs_fn, rkeys, wkeys, evac, ntile=2, pairs=(0, 1, 2, 3), after_mm=None, splits=None):
            pr = pairs[pair_ctr[0] % len(pairs)]
            pair_ctr[0] += 1
            banks = [2 * pr + j for j in range(ntile)]
            n = len(lhs_list)

            def mk(k0, k1):
                def mm(e, lhs_list=lhs_list, banks=banks):
                    inst = None
                    for kc in range(k0, k1):
                        for j, b in enumerate(banks):
                            inst = e.matmul(psum[b][:, :], lhsT=lhs_list[kc], rhs=rhs_fn(kc, j),
                                            start=(kc == 0), stop=(kc == n - 1))
                    return inst
                return mm
            if splits is None:
                S.add("pe", mk(0, n), reads=list(wkeys) + list(rkeys), writes=[("ps", b) for b in banks])
            else:
                for (k0, k1, wks) in splits:
                    S.add("pe", mk(k0, k1), reads=list(wks) + list(rkeys), writes=[("ps", b) for b in banks])
            if after_mm is not None:
                after_mm()
            for j, b in enumerate(banks):
                evac(j, psum[b], ("ps", b))

        def act(out_ap, in_ap, func, reads, writes, bias=None, scale=None):
            kw = {}
            if bias is not None:
                kw["bias"] = bias
            if scale is not None:
                kw["scale"] = scale
            return S.add("act", lambda e: e.activation(out=out_ap, in_=in_ap, func=func, **kw), reads=reads, writes=writes)

        def tt(eng, out_ap, a, b, op, reads, writes):
            return S.add(eng, lambda e: e.tensor_tensor(out=out_ap, in0=a, in1=b, op=op), reads=reads, writes=writes)

        def stt(eng, out_ap, a, sc, b, op0, op1, reads, writes):
            return S.add(eng, lambda e: e.scalar_tensor_tensor(out=out_ap, in0=a, scalar=sc, in1=b, op0=op0, op1=op1),
                         reads=reads, writes=writes)

        def ts(eng, out_ap, a, s1, s2, op0, op1, reads, writes):
            if s2 is None:
                return S.add(eng, lambda e: e.tensor_scalar(out=out_ap, in0=a, scalar1=s1, scalar2=None, op0=op0),
                             reads=reads, writes=writes)
            return S.add(eng, lambda e: e.tensor_scalar(out=out_ap, in0=a, scalar1=s1, scalar2=s2, op0=op0, op1=op1),
                         reads=reads, writes=writes)

        def cp(eng, out_ap, in_ap, reads, writes):
            if eng == "act":
                return S.add("act", lambda e: e.copy(out=out_ap, in_=in_ap), reads=reads, writes=writes)
            return S.add(eng, lambda e: e.tensor_copy(out=out_ap, in_=in_ap), reads=reads, writes=writes)

        dma("sp", CONST, consts_in, [], ["const"])
        dma("sp", PP, pp_in.rearrange("l p n -> p l n"), [], ["pp"])
        cp("dve", IDB, IDENT, ["const"], ["idb"])
        cp("dve", TRIB, CONST[:, 128:256], ["const"], ["trib"])
        cp("dve", TRIPB, CONST[:, 256:384], ["const"], ["tripb"])
        S.add("dve", lambda e: e.memset(ONESB, 1.0), writes=["onesb"])

        T0 = Temps(GEN, NF)
        XS = [T0.f32(2048) for _ in range(2)]
        XT = [T0.f32(512) for _ in range(2)]
        XTB = [T0.bf(512) for _ in range(2)]
        for blk in range(S_len // 128 if upto != "s" else 0):
            xs = XS[blk % 2]
            kxs = ("xs", blk % 2)
            dma("sp", xs, x_in[blk * 128:(blk + 1) * 128, :], [], [kxs])
            for g in range(4):
                b = (blk * 4 + g) % 8
                kp = ("ps", b)

                def tr(e, xs=xs, g=g, b=b):
                    for i in range(4):
                        c = g * 4 + i
                        inst = e.transpose(psum[b][:, i * 128:(i + 1) * 128], xs[:, c * 128:(c + 1) * 128], IDENT)
                    return inst
                S.add("pe", tr, reads=[kxs, "const"], writes=[kp])
                i2 = (blk * 4 + g) % 2
                cp("dve", XT[i2], psum[b][:, :], [kp], [("xt", i2)])
                cp("act", XTB[i2], psum[b][:, :], [kp], [("xtb", i2)])
                if upto in ("s1",):
                    continue
                dma("sp", xres[g * 4:(g + 1) * 4, :, blk * 128:(blk + 1) * 128].rearrange("c p n -> p c n"),
                    XT[i2].rearrange("p (c n) -> p c n", c=4), [("xt", i2)], [U("xres")], sem=("xts", i2))
                if upto in ("s2",):
                    continue
                dma("sp", xbf[g * 4:(g + 1) * 4, :, blk * 128:(blk + 1) * 128].rearrange("c p n -> p c n"),
                    XTB[i2].rearrange("p (c n) -> p c n", c=4), [("xtb", i2)], [U("xbf")], sem=("xtbs", i2))
        S.fence()

        SCALE_A = 64.0 ** -0.5
        SCALE_B = 192.0 ** -0.5

        for l in range(L if upto not in ("0", "s", "s1", "s2") else 0):
            wl_in, wl_uq, wl_ukv = w_in[l], w_uq[l], w_ukv[l]
            PPl = PP[:, l, :]
            dma("sp", ESINK, esink_in[l], [], ["esink"])
            act(ESINK, ESINK, AF.Exp, ["esink"], ["esink"])
            dma("sp", BSBC, bsbc_in[l].rearrange("p (g t) -> p g t", g=8), [], ["bsbc"])
            dma("pool", WST, wst_in[l].rearrange("p (g t) -> p g t", g=8), [], ["wst"], nofence=False)
            for g in range(8):
                tt("dve", WST[:, g, :], WST[:, g, :], TRIB, ALU.mult, ["wst", "trib"], ["wst"])
            S.add("dve", lambda e: e.memset(HALO, 0.0), writes=["halo"])
            S.fence()

            for t in range(NTG):
                tok0 = t * TG
                nkb = 8 * (t + 1)
                dma("pool", XB, xbf[:, :, tok0:tok0 + TG].rearrange("c p n -> p c n"), [], ["xb"])
                TB = Temps(GEN + 8192, NF)
                POSI = f32v(GEN, 1024)
                POSF = f32v(GEN + 1024, 1024)
                dma("sp", POSI.bitcast(I32), pos_in[:, tok0:tok0 + TG], [], ["posi"])
                S1 = f32v(GEN + 2048, 1024)
                S2 = f32v(GEN + 3072, 1024)
                KI = f32v(GEN + 4096, 1024)
                cp("dve", POSF, POSI.bitcast(I32), ["posi"], ["posf"])
                ts("dve", POSF, POSF, INVF, float(1.0 / (2.0 * np.pi)), ALU.mult, ALU.mult, ["posf", "const"], ["posf"])
                cp("dve", KI.bitcast(I32), POSF, ["posf"], ["ki"])
                cp("dve", S1, KI.bitcast(I32), ["ki"], ["s1"])
                tt("dve", POSF, POSF, S1, ALU.subtract, ["posf", "s1"], ["posf"])
                act(S1, POSF, AF.Sin, ["posf"], ["s1"], scale=float(np.pi))
                act(S2, POSF, AF.Sin, ["posf"], ["s2"], scale=float(np.pi / 2))
                tt("dve", S2, S2, S2, ALU.mult, ["s2"], ["s2"])
                ts("dve", S2, S2, -2.0, 1.0, ALU.mult, ALU.add, ["s2"], ["s2"])
                stt("dve", SIN, S1, 2.0, S2, ALU.mult, ALU.mult, ["s1", "s2"], ["sin"])
                tt("dve", S1, S1, S1, ALU.mult, ["s1"], ["s1"])
                ts("dve", COS, S1, -2.0, 1.0, ALU.mult, ALU.add, ["s1"], ["cos"])
                ts("dve", SIN, SIN, SGN, None, ALU.mult, None, ["sin", "const"], ["sin"])

                RAW = TB.f32(4096).rearrange("p (c n) -> p c n", c=4)
                SQ = [TB.bf(1024) for _ in range(2)]
                CQN = TB.bf(4096).rearrange("p (c n) -> p c n", c=4)
                CKVN = TB.bf(4096).rearrange("p (c n) -> p c n", c=4)
                RINV = TB.f32(1024)
                QN = [TB.bf(2048).rearrange("p (h n) -> p h n", h=2) for _ in range(2)]
                QR = [TB.bf(1024) for _ in range(2)]
                KT0 = TB.bf(S_len)
                VT0 = TB.bf(S_len).rearrange("p (b d) -> p b d", d=128)
                RAWB = RAW.rearrange("p c n -> p (c n)").bitcast(BF16)
                KTS = [KT0, RAWB[:, 0:S_len]]
                VTS = [VT0, RAWB[:, S_len:2 * S_len].rearrange("p (b d) -> p b d", d=128)]
                PT = [TB.bf(512) for _ in range(3)]
                DEN = [TB.f32(512) for _ in range(2)]
                KST = [TB.bf(1024) for _ in range(2)]
                VST = [TB.bf(512).rearrange("p (h d) -> p h d", h=4) for _ in range(2)]
                TR1 = TB.f32(1024)
                TR2 = TB.f32(1024)

                def latent(col0, gcol, dst, tag):
                    for half in range(2):
                        wv, wk = load_w([(wl_in, col0 + half * 256, 256, 0)], 16)
                        for cc in range(2):
                            c = half * 2 + cc

                            def ev(j, ps, kp, c=c):
                                sl = slice(j * 512, (j + 1) * 512)
                                cp("dve", RAW[:, c, sl], ps[:, :], [kp], [("raw", c, j)])
                                i2 = (c * 2 + j) % 2
                                act(SQ[i2][:, 0:512], ps[:, :], AF.Square, [kp], [("sq", i2)])
                                S.add("pe", lambda e, i2=i2, j=j, c=c: e.matmul(psum[6 + j][:, :], lhsT=ONESB, rhs=SQ[i2][:, 0:512],
                                                                                 start=(c == 0), stop=(c == 3)),
                                      reads=[("sq", i2), "onesb"], writes=[("ps", 6 + j)])
                            gemm([wv[:, k, cc * 128:(cc + 1) * 128] for k in range(16)],
                                 lambda k, j: XB[:, k, j * 512:(j + 1) * 512], ["xb"], wk, ev, pairs=(0, 1, 2))
                    for j in range(2):
                        sl = slice(j * 512, (j + 1) * 512)
                        act(RINV[:, sl], psum[6 + j][:, :], AF.Sqrt, [("ps", 6 + j)], [("rinv", j)], bias=EPSC, scale=1.0 / 512.0)
                        S.add("dve", lambda e, sl=sl: e.reciprocal(out=RINV[:, sl], in_=RINV[:, sl]), reads=[("rinv", j)], writes=[("rinv", j)])
                        for c in range(4):
                            stt("dve", dst[:, c, sl], RAW[:, c, sl], PPl[:, gcol + c:gcol + c + 1], RINV[:, sl], ALU.mult, ALU.mult,
                                [("raw", c, j), ("rinv", j), "pp"], [(tag, j)])

                latent(1280, QNG, CQN, "cqn")
                latent(1792, KVNG, CKVN, "ckvn")
                wv, wk = load_w([(wl_in, 2304, 64, 0), (wl_in, 2304, 64, 64), (wl_in, 2336, 32, 128), (wl_in, 2304, 32, 160),
                                 (wl_in, 2336, 32, 192), (wl_in, 2304, 32, 224)], 16)

                def ev_kr0(j, ps, kp):
                    sl = slice(j * 512, (j + 1) * 512)
                    tt("dve", TR1[:, sl], ps[:, :], COS[:, sl], ALU.mult, [kp, "cos"], [("tr1", j)])

                def ev_kr1(j, ps, kp):
                    sl = slice(j * 512, (j + 1) * 512)
                    tt("dve", TR2[:, sl], ps[:, :], SIN[:, sl], ALU.mult, [kp, "sin"], [("tr2", j)])
                    tt("dve", KR[:, tok0 + j * 512: tok0 + (j + 1) * 512], TR1[:, sl], TR2[:, sl], ALU.add,
                       [("tr1", j), ("tr2", j)], [("kr", t, j)])
                gemm([wv[:, k, 0:128] for k in range(16)], lambda k, j: XB[:, k, j * 512:(j + 1) * 512], ["xb"], wk, ev_kr0, pairs=(0, 1, 2))
                gemm([wv[:, k, 128:256] for k in range(16)], lambda k, j: XB[:, k, j * 512:(j + 1) * 512], ["xb"], wk, ev_kr1, pairs=(0, 1, 2))

                ckeys = [("ckvn", 0), ("ckvn", 1)]
                for hg in range(4):
                    wv, wk = load_w([(wl_ukv, hg * 1024, 1024, 0)], 4)
                    for hh in range(4):
                        h = hg * 4 + hh

                        def ev_k(j, ps, kp, h=h):
                            i2 = h % 2
                            sl = slice(j * 512, (j + 1) * 512)
                            cp("act", KST[i2][:, sl], ps[:, :], [kp], [("kst", i2, j)])
                            if j == 1:
                                dma("sp", kcache[h, :, tok0:tok0 + TG], KST[i2], [("kst", i2, 0), ("kst", i2, 1)], [("kc", h)],
                                    sem=("ksts", i2))
                        gemm([wv[:, k, hh * 256:hh * 256 + 128] for k in range(4)],
                             lambda k, j: CKVN[:, k, j * 512:(j + 1) * 512], ckeys, wk, ev_k, pairs=(0, 1, 2))
                    for blk in range(8):
                        b = 6 + (blk % 2)
                        kp = ("ps", b)

                        def mmv(e, wv=wv, blk=blk, b=b):
                            for k in range(4):
                                inst = e.matmul(psum[b][:, :].rearrange("p (h d) -> p h d", h=4),
                                                lhsT=CKVN[:, k, blk * 128:(blk + 1) * 128],
                                                rhs=wv.rearrange("p k (h d) -> p k h d", h=4)[:, k, :, 128:256],
                                                start=(k == 0), stop=(k == 3))
                            return inst
                        S.add("pe", mmv, reads=ckeys + wk, writes=[kp])
                        i2 = blk % 2
                        cp("act", VST[i2], psum[b][:, :].rearrange("p (h d) -> p h d", h=4), [kp], [("vst", i2)])
                        dma("sp", vcache[hg * 4:(hg + 1) * 4, tok0 + blk * 128: tok0 + (blk + 1) * 128, :].rearrange("h t d -> t h d"),
                            VST[i2], [("vst", i2)], [("vc", hg * 4 + i) for i in range(4)], sem=("vsts", i2))

                def q_pieces(hp):
                    h0 = 2 * hp
                    return [(wl_uq, h0 * 192, 128, 0), (wl_uq, (h0 + 1) * 192, 128, 128),
                            (wl_uq, h0 * 192 + 128, 64, 256), (wl_uq, (h0 + 1) * 192 + 128, 64, 320),
                            (wl_uq, h0 * 192 + 160, 32, 384), (wl_uq, h0 * 192 + 128, 32, 416),
                            (wl_uq, (h0 + 1) * 192 + 160, 32, 448), (wl_uq, (h0 + 1) * 192 + 128, 32, 480)]
                S.fence()
                ntok = (t + 1) * TG

                def load_kv(h):
                    dma("pool", KTS[h % 2][:, 0:ntok], kcache[h, :, 0:ntok], [("kc", h)], [("kt", h % 2)])
                    dma("pool", VTS[h % 2][:, 0:nkb, :], vcache[h, 0:ntok, :].rearrange("(b p) d -> p b d", p=128), [("vc", h)], [("vt", h % 2)])
                q_next = load_w(q_pieces(0), 4)
                load_kv(0)
                for hp in range(8):
                    h0 = 2 * hp
                    qi = hp % 2
                    wv, wk = q_next
                    qkeys = [("cqn", 0), ("cqn", 1)]
                    for hh in range(2):
                        def ev_qn(j, ps, kp, hh=hh):
                            act(QN[qi][:, hh, j * 512:(j + 1) * 512], ps[:, :], AF.Identity, [kp], [("qn", qi, hh, j)], scale=SCALE_B)
                        gemm([wv[:, k, hh * 128:(hh + 1) * 128] for k in range(4)],
                             lambda k, j: CQN[:, k, j * 512:(j + 1) * 512], qkeys, wk, ev_qn, pairs=(0,))

                    def ev_q0(j, ps, kp):
                        sl = slice(j * 512, (j + 1) * 512)
                        stt("dve", TR1[:, sl], ps[:, :], SCALE_B, COS[:, sl], ALU.mult, ALU.mult, [kp, "cos"], [("tr1", j)])

                    def ev_q1(j, ps, kp):
                        sl = slice(j * 512, (j + 1) * 512)
                        stt("dve", TR2[:, sl], ps[:, :], SCALE_B, SIN[:, sl], ALU.mult, ALU.mult, [kp, "sin"], [("tr2", j)])
                        tt("dve", QR[qi][:, sl], TR1[:, sl], TR2[:, sl], ALU.add, [("tr1", j), ("tr2", j)], [("qr", qi, j)])
                    gemm([wv[:, k, 256:384] for k in range(4)], lambda k, j: CQN[:, k, j * 512:(j + 1) * 512], qkeys, wk, ev_q0, pairs=(0,))
                    gemm([wv[:, k, 384:512] for k in range(4)], lambda k, j: CQN[:, k, j * 512:(j + 1) * 512], qkeys, wk, ev_q1, pairs=(0,))

                    if hp < 7:
                        q_next = load_w(q_pieces(hp + 1), 4)
                    for hh in range(2):
                        h = h0 + hh
                        base = hh * 64
                        if h + 1 < 16:
                            load_kv(h + 1)
                        KT, VT = KTS[h % 2], VTS[h % 2]
                        kkt, kvt = ("kt", h % 2), ("vt", h % 2)
                        krk = [("kr", tt_, jj) for tt_ in range(t + 1) for jj in range(2)]
                        steps = [(j, kb) for j in range(2) for kb in range(8 * t + 4 * (j + 1))]

                        def geom(i):
                            j, kb = steps[i]
                            dloc = kb - (8 * t + 4 * j)
                            c0 = 128 * dloc if dloc > 0 else 0
                            return j, kb, dloc, c0, 512 - c0, 6 + (i % 2), PT[i % 3], ("pt", i % 3)

                        def emit_s(i):
                            j, kb, dloc, c0, ncol, sb, pt, kpt = geom(i)
                            qs = slice(j * 512 + c0, (j + 1) * 512)

                            def mms(e, kb=kb, qs=qs, ncol=ncol, sb=sb, hh=hh, base=base, qi=qi, KT=KT):
                                e.matmul(psum[sb][:, 0:ncol], lhsT=KT[:, kb * 128:(kb + 1) * 128], rhs=QN[qi][:, hh, qs],
                                         start=True, stop=False)
                                return e.matmul(psum[sb][:, 0:ncol], lhsT=KR[base:base + 64, kb * 128:(kb + 1) * 128],
                                                rhs=QR[qi][base:base + 64, qs], start=False, stop=True)
                            S.add("pe", mms, reads=[kkt, ("qn", qi, hh, j), ("qr", qi, j)] + krk, writes=[("ps", sb)])
                            act(pt[:, 0:ncol], psum[sb][:, 0:ncol], AF.Exp, [("ps", sb)], [kpt])
                            if dloc >= 0:
                                tt("dve", pt[:, 0:128], pt[:, 0:128], TRIB, ALU.mult, [kpt, "trib"], [kpt])

                        def emit_o(i):
                            j, kb, dloc, c0, ncol, sb, pt, kpt = geom(i)
                            OB, DB = 2 + 2 * j, 3 + 2 * j
                            nk = 8 * t + 4 * (j + 1)

                            def mmo(e, kb=kb, c0=c0, ncol=ncol, pt=pt, OB=OB, DB=DB, nk=nk, VT=VT):
                                e.matmul(psum[OB][:, c0:512], lhsT=VT[:, kb, :], rhs=pt[:, 0:ncol], start=(kb == 0), stop=(kb == nk - 1))
                                return e.matmul(psum[DB][:, c0:512], lhsT=ONESB, rhs=pt[:, 0:ncol], start=(kb == 0), stop=(kb == nk - 1))
                            S.add("pe", mmo, reads=[kvt, kpt, "onesb"], writes=[("ps", OB), ("ps", DB)])
                            if kb == nk - 1:
                                dn = DEN[j]
                                S.add("dve", lambda e, dn=dn, DB=DB: e.reciprocal(out=dn, in_=psum[DB][:, :]), reads=[("ps", DB)], writes=[("den", j)])
                                tt("dve", YB[:, h, j * 512:(j + 1) * 512], psum[OB][:, :], dn, ALU.mult, [("ps", OB), ("den", j)], [("yb", h)])
                        emit_s(0)
                        for i in range(len(steps)):
                            if i + 1 < len(steps):
                                emit_s(i + 1)
                            emit_o(i)
                S.fence()

                if upto == "B":
                    continue
                TA = Temps(GEN + 12288, NF)
                QA = TA.bf(8192).rearrange("p (c n) -> p c n", c=8)
                KA = TA.bf(2 * 1152).rearrange("p (g n) -> p g n", g=2)
                VA = TA.bf(9 * 256).rearrange("p (b g d) -> p b g d", b=9, g=2)
                PTA = [TA.bf(512) for _ in range(4)]
                DNA = [TA.f32(512) for _ in range(2)]
                for q in range(4):
                    wv, wk = load_w([(wl_in, q * 256, 256, 0)], 16)
                    for cc in range(2):
                        c = q * 2 + cc

                        def ev(j, ps, kp, c=c):
                            act(QA[:, c, j * 512:(j + 1) * 512], ps[:, :], AF.Identity, [kp], [("qa", c)], scale=SCALE_A)
                        gemm([wv[:, k, cc * 128:(cc + 1) * 128] for k in range(16)], lambda k, j: XB[:, k, j * 512:(j + 1) * 512],
                             ["xb"], wk, ev)
                wv, wk = load_w([(wl_in, 1024, 64, 0), (wl_in, 1024, 64, 64), (wl_in, 1088, 64, 128), (wl_in, 1088, 64, 192)], 16)
                wv2, wk2 = load_w([(wl_in, 1152, 64, 0), (wl_in, 1152, 64, 64), (wl_in, 1216, 64, 128), (wl_in, 1216, 64, 192)], 16)
                XH = TA.bf(16 * 128).rearrange("p (c n) -> p c n", c=16)
                if t > 0:
                    dma("pool", XH, xbf[:, :, tok0 - 128:tok0].rearrange("c p n -> p c n"), [], ["xh"])
                for g in range(2):
                    def ev(j, ps, kp, g=g):
                        cp("act", KA[:, g, 128 + j * 512:128 + (j + 1) * 512], ps[:, :], [kp], [("ka", g)])
                    gemm([wv[:, k, g * 128:(g + 1) * 128] for k in range(16)], lambda k, j: XB[:, k, j * 512:(j + 1) * 512],
                         ["xb"], wk, ev)
                if t > 0:
                    def mmh(e, wv=wv):
                        for g in range(2):
                            for k in range(16):
                                inst = e.matmul(psum[0][:, g * 128:(g + 1) * 128], lhsT=wv[:, k, g * 128:(g + 1) * 128], rhs=XH[:, k, :],
                                                start=(k == 0), stop=(k == 15))
                        return inst
                    S.add("pe", mmh, reads=wk + ["xh"], writes=[("ps", 0)])
                    cp("act", KA[:, :, 0:128], psum[0][:, 0:256].rearrange("p (g n) -> p g n", g=2), [("ps", 0)], [("ka", 0), ("ka", 1)])
                for blk in range(9):
                    if blk == 0 and t == 0:
                        continue
                    b = 6 + (blk % 2)
                    src = (lambda k: XH[:, k, :]) if blk == 0 else (lambda k, blk=blk: XB[:, k, (blk - 1) * 128:blk * 128])

                    def mmv(e, src=src, b=b, wv2=wv2):
                        for k in range(16):
                            inst = e.matmul(psum[b][:, 0:256], lhsT=src(k), rhs=wv2[:, k, 0:256], start=(k == 0), stop=(k == 15))
                        return inst
                    S.add("pe", mmv, reads=wk2 + ["xb", "xh"], writes=[("ps", b)])
                    cp("act", VA[:, blk, :, :], psum[b][:, 0:256].rearrange("p (g d) -> p g d", g=2), [("ps", b)], [("va", blk)])
                qakeys = [("qa", c) for c in range(8)]
                groups = [(n, g, par) for n in range(8) for g in range(2) for par in range(2)]

                def swa_geom(gi):
                    n, g, par = groups[gi]
                    ms = [1] if (t == 0 and n == 0) else [0, 1]
                    sbs = (6, 7) if gi % 2 == 0 else (0, 1)
                    return n, g, par, par * 64, ms, sbs, 2 + 2 * (gi % 2), 3 + 2 * (gi % 2)

                def swa_s(gi):
                    n, g, par, base, ms, sbs, OB, DB = swa_geom(gi)
                    for m in ms:
                        sb = sbs[m]
                        pi = (gi % 2) * 2 + m
                        pt = PTA[pi]
                        kpt = ("pta", pi)

                        def mms(e, n=n, g=g, base=base, m=m, sb=sb):
                            return e.matmul(psum[sb][:, :].rearrange("p (i q) -> p i q", i=4),
                                            lhsT=KA[base:base + 64, g, (n + m) * 128:(n + m + 1) * 128],
                                            rhs=QA[base:base + 64, 4 * g:4 * g + 4, n * 128:(n + 1) * 128], start=True, stop=True)
                        S.add("pe", mms, reads=[("ka", g)] + qakeys, writes=[("ps", sb)])
                        act(pt, psum[sb][:, :], AF.Exp, [("ps", sb)], [kpt])
                        msk = TRIB if m == 1 else TRIPB
                        tt("dve", pt.rearrange("p (i q) -> p i q", i=4), pt.rearrange("p (i q) -> p i q", i=4),
                           msk.unsqueeze(1).broadcast_to([128, 4, 128]), ALU.mult, [kpt, "trib", "tripb"], [kpt])

                def swa_o(gi):
                    n, g, par, base, ms, sbs, OB, DB = swa_geom(gi)
                    for mi, m in enumerate(ms):
                        pi = (gi % 2) * 2 + m
                        pt = PTA[pi]
                        kpt = ("pta", pi)

                        def mmo(e, n=n, g=g, m=m, pt=pt, OB=OB, DB=DB, first=(mi == 0), last=(mi == len(ms) - 1)):
                            e.matmul(psum[OB][:, :], lhsT=VA[:, n + m, g, :], rhs=pt, start=first, stop=last)
                            return e.matmul(psum[DB][:, :], lhsT=ONESB, rhs=pt, start=first, stop=last)
                        S.add("pe", mmo, reads=[("va", n + m), kpt, "onesb"], writes=[("ps", OB), ("ps", DB)])
                    dn = DNA[gi % 2]
                    kdn = ("dna", gi % 2)
                    hsl = slice(8 * g + par, 8 * g + 8, 2)
                    tt("dve", dn.rearrange("p (i q) -> p i q", i=4), psum[DB][:, :].rearrange("p (i q) -> p i q", i=4),
                       ESINK[:, hsl].unsqueeze(2).broadcast_to([128, 4, 128]), ALU.add, [("ps", DB), "esink"], [kdn])
                    S.add("dve", lambda e, dn=dn: e.reciprocal(out=dn, in_=dn), reads=[kdn], writes=[kdn])
                    tt("dve", YA[base:base + 64, 4 * g:4 * g + 4, n * 128:(n + 1) * 128],
                       psum[OB][base:base + 64, :].rearrange("p (i q) -> p i q", i=4),
                       dn[base:base + 64, :].rearrange("p (i q) -> p i q", i=4), ALU.mult, [("ps", OB), kdn], [("ya", n, g, par)])
                swa_s(0)
                for gi in range(len(groups)):
                    if gi + 1 < len(groups):
                        swa_s(gi + 1)
                    swa_o(gi)
                S.fence()

                if upto == "A":
                    continue
                TC = Temps(GEN + 16384, NF)
                VG = TC.f32(8192).rearrange("p (c n) -> p c n", c=8)
                VNT = VG.rearrange("p c n -> p (c n)").bitcast(BF16)[:, 0:8192].rearrange("p (g b c) -> p g b c", g=8, b=8)
                VN = TC.bf(8192).rearrange("p (c n) -> p c n", c=8)
                TB2 = [TC.bf(1024)] * 2
                TS2 = [TC.bf(1024)] * 2
                MEAN = COS
                RSTD = SIN
                UG = [TC.bf(1024)] * 2
                TMPC = TC.f32(512)
                for q in range(4):
                    wv, wk = load_w([(wl_in, 3392 + q * 256, 256, 0)], 16)
                    for cc in range(2):
                        c = q * 2 + cc

                        def ev(j, ps, kp, c=c):
                            sl = slice(j * 512, (j + 1) * 512)
                            i2 = 0
                            act(VG[:, c, sl], ps[:, :], AF.Gelu, [kp], [("vg", c, j)])
                            cp("dve", TB2[i2][:, sl], VG[:, c, sl], [("vg", c, j)], [("tb2", i2, j)])
                            tt("dve", TS2[i2][:, sl], VG[:, c, sl], VG[:, c, sl], ALU.mult, [("vg", c, j)], [("ts2", i2, j)])

                            def mst(e, i2=i2, j=j, c=c, sl=sl):
                                e.matmul(psum[4 + j][:, :], lhsT=ONESB, rhs=TB2[i2][:, sl], start=(c == 0), stop=(c == 7))
                                return e.matmul(psum[6 + j][:, :], lhsT=ONESB, rhs=TS2[i2][:, sl], start=(c == 0), stop=(c == 7))
                            S.add("pe", mst, reads=[("tb2", i2, j), ("ts2", i2, j), "onesb"], writes=[("ps", 4 + j), ("ps", 6 + j)])
                        gemm([wv[:, k, cc * 128:(cc + 1) * 128] for k in range(16)], lambda k, j: XB[:, k, j * 512:(j + 1) * 512],
                             ["xb"], wk, ev, pairs=(0, 1))
                for j in range(2):
                    sl = slice(j * 512, (j + 1) * 512)
                    ts("dve", MEAN[:, sl], psum[4 + j][:, :], 1.0 / 1024.0, None, ALU.mult, None, [("ps", 4 + j)], [("mean", j)])
                    tt("dve", TMPC, MEAN[:, sl], MEAN[:, sl], ALU.mult, [("mean", j)], ["tmpc"])
                    stt("dve", RSTD[:, sl], psum[6 + j][:, :], 1.0 / 1024.0, TMPC, ALU.mult, ALU.subtract, [("ps", 6 + j), "tmpc"], [("rstd", j)])
                    act(RSTD[:, sl], RSTD[:, sl], AF.Sqrt, [("rstd", j)], [("rstd", j)], bias=EPSC)
                    S.add("dve", lambda e, sl=sl, RSTD=RSTD: e.reciprocal(out=RSTD[:, sl], in_=RSTD[:, sl]), reads=[("rstd", j)], writes=[("rstd", j)])
                    for c in range(8):
                        tt("dve", VG[:, c, sl], VG[:, c, sl], MEAN[:, sl], ALU.subtract, [("vg", c, j), ("mean", j)], [("vg", c, j)])
                        tt("dve", VG[:, c, sl], VG[:, c, sl], RSTD[:, sl], ALU.mult, [("vg", c, j), ("rstd", j)], [("vg", c, j)])
                        act(VN[:, c, sl], VG[:, c, sl], AF.Identity, [("vg", c, j), "pp"], [("vn", c, j)],
                            bias=PPl[:, SLB + c:SLB + c + 1], scale=PPl[:, SLG + c:SLG + c + 1])
                S.fence()
                PSB = [psum[b][:, :].bitcast(BF16) for b in range(8)]
                for g in range(8):
                    for half in range(2):
                        b = (g * 2 + half) % 2

                        def trn(e, g=g, half=half, b=b):
                            for i in range(4):
                                blk = half * 4 + i
                                inst = e.transpose(PSB[b][:, i * 128:(i + 1) * 128], VN[:, g, blk * 128:(blk + 1) * 128], IDB)
                            return inst
                        S.add("pe", trn, reads=[("vn", g, half), "idb"], writes=[("ps", b)])
                        cp("act", VNT[:, g, half * 4:half * 4 + 4, :], PSB[b][:, 0:512].rearrange("p (b c) -> p b c", b=4),
                           [("ps", b)], [("vnt", g, half)])
                for q in range(4):
                    wv, wk = load_w([(wl_in, 2368 + q * 256, 256, 0)], 16)
                    for cc in range(2):
                        g = q * 2 + cc
                        ug = UG[g % 2]

                        def ev(j, ps, kp, g=g, ug=ug):
                            act(ug[:, j * 512:(j + 1) * 512], ps[:, :], AF.Gelu, [kp], [("ug", 0, j)])
                        gemm([wv[:, k, cc * 128:(cc + 1) * 128] for k in range(16)], lambda k, j: XB[:, k, j * 512:(j + 1) * 512],
                             ["xb"], wk, ev, pairs=(1, 2))
                        for half in range(2):
                            b = 6 + half

                            def mix(e, g=g, half=half, b=b):
                                for i in range(4):
                                    inst = e.matmul(psum[b][:, i * 128:(i + 1) * 128], lhsT=VNT[:, g, half * 4 + i, :], rhs=WST[:, g, :],
                                                    start=True, stop=True)
                                return inst
                            S.add("pe", mix, reads=[("vnt", g, half), "wst"], writes=[("ps", b)])
                            tt("dve", TMPC.rearrange("p (b t) -> p b t", b=4), psum[b][:, :].rearrange("p (b t) -> p b t", b=4),
                               BSBC[:, g, :].unsqueeze(1).broadcast_to([128, 4, 128]), ALU.add, [("ps", b), "bsbc"], ["tmpc"])
                            tt("dve", YC[:, g, half * 512:(half + 1) * 512], TMPC, ug[:, half * 512:(half + 1) * 512], ALU.mult,
                               ["tmpc", ("ug", 0, half)], [("yc", g)])
                S.fence()

                if upto == "C":
                    continue
                TD = Temps(GEN + 16384, NF)
                MERGED = TD.bf(16384).rearrange("p (c n) -> p c n", c=16)
                GT = [TD.f32(1024) for _ in range(2)]
                MTS = [TD.f32(1024) for _ in range(2)]
                TT_ = TD.f32(1024)
                branches = [(w_pa[l], 8, YA, 0), (w_pb[l], 16, YB, 1), (w_pc[l], 8, YC, 2)]
                ykeys_all = {0: [("ya", n, g, p) for n in range(8) for g in range(2) for p in range(2)],
                             1: [("yb", h) for h in range(16)], 2: [("yc", g) for g in range(8)]}
                for jp in range(8):
                    for i, (wpr, kcn, Y, bi) in enumerate(branches):
                        gv, gk = load_w([(wl_in, 4416 + i * 2048 + jp * 256, 256, 0)], 16)
                        pv, pk = load_w([(wpr, jp * 256, 256, 0)], kcn)
                        for cc in range(2):
                            jd = jp * 2 + cc
                            gt = GT[cc]
                            mt = MTS[cc]

                            def ev_g(j, ps, kp, gt=gt, i=i, jd=jd, cc=cc):
                                act(gt[:, j * 512:(j + 1) * 512], ps[:, :], AF.Sigmoid, [kp, "pp"], [("gt", cc, j)],
                                    bias=PPl[:, BG + i * 16 + jd:BG + i * 16 + jd + 1])
                            gemm([gv[:, k, cc * 128:(cc + 1) * 128] for k in range(16)], lambda k, j: XB[:, k, j * 512:(j + 1) * 512],
                                 ["xb"], gk, ev_g)

                            def ev_p(j, ps, kp, gt=gt, mt=mt, i=i, jd=jd, cc=cc):
                                sl = slice(j * 512, (j + 1) * 512)
                                if i == 0:
                                    tt("dve", mt[:, sl], ps[:, :], gt[:, sl], ALU.mult, [kp, ("gt", cc, j)], [("mt", cc, j)])
                                elif i == 1:
                                    tt("dve", TT_[:, sl], ps[:, :], gt[:, sl], ALU.mult, [kp, ("gt", cc, j)], [("tt", j)])
                                    tt("dve", mt[:, sl], mt[:, sl], TT_[:, sl], ALU.add, [("mt", cc, j), ("tt", j)], [("mt", cc, j)])
                                else:
                                    tt("dve", TT_[:, sl], ps[:, :], gt[:, sl], ALU.mult, [kp, ("gt", cc, j)], [("tt", j)])
                                    tt("dve", MERGED[:, jd, sl], mt[:, sl], TT_[:, sl], ALU.add, [("mt", cc, j), ("tt", j)], [("merged", jd)])
                            gemm([pv[:, k, cc * 128:(cc + 1) * 128] for k in range(kcn)],
                                 lambda k, j, Y=Y: Y[:, k, j * 512:(j + 1) * 512], ykeys_all[i], pk, ev_p)
                S.fence()

                if upto == "D":
                    continue
                def ln_stage(TT, TT2, nK, lhs_groups_fn, rhs_fn, rkeys, gcol, bcol, final, xr_in_tt=True):
                    NXR = 6
                    XR = [(TT if xr_in_tt else TT2).f32(1024) for _ in range(NXR)]
                    ZB = [TT.bf(1024) for _ in range(2)]
                    ZS = [TT.bf(1024) for _ in range(2)]
                    MEAN = TT.f32(1024)
                    RSTD = TT.f32(1024)
                    TMP = TT2.f32(512)
                    XO = [TT2.bf(1024) for _ in range(2)]
                    pend = []
                    for jd in range(16):
                        xr = XR[jd % NXR]
                        kxr = ("xr", jd % NXR)
                        dma("pool", xr, xres[jd, :, tok0:tok0 + TG], [], [kxr])
                        lg = lhs_groups_fn(jd)
                        lhs, wk = lg[0], lg[1]
                        spl = lg[2] if len(lg) > 2 else None

                        def ev(j, ps, kp, jd=jd, xr=xr, kxr=kxr):
                            sl = slice(j * 512, (j + 1) * 512)
                            i2 = jd % 2
                            stt("dve", xr[:, sl], xr[:, sl], ALPHA, ps[:, :], ALU.mult, ALU.add, [kp, kxr], [kxr])
                            cp("act", ZB[i2][:, sl], xr[:, sl], [kxr], [("zb", i2, j)])
                            act(ZS[i2][:, sl], xr[:, sl], AF.Square, [kxr], [("zs", i2, j)])

                            def mst(e, i2=i2, j=j, jd=jd, sl=sl):
                                e.matmul(psum[4 + j][:, :], lhsT=ONESB, rhs=ZB[i2][:, sl], start=(jd == 0), stop=(jd == 15))
                                return e.matmul(psum[6 + j][:, :], lhsT=ONESB, rhs=ZS[i2][:, sl], start=(jd == 0), stop=(jd == 15))
                            pend.append((mst, [("zb", i2, j), ("zs", i2, j), "onesb"], [("ps", 4 + j), ("ps", 6 + j)]))
                            if j == 1:
                                dma("sp", zscr[jd, :, :], xr, [kxr], [("zscr", jd)], sem=("zst", jd % NXR))
                        old_pend = list(pend)
                        del pend[:]
                        gemm(lhs, rhs_fn, rkeys, wk, ev, pairs=(0, 1), splits=spl, after_mm=lambda old_pend=old_pend: [S.add("pe", f_, reads=r_, writes=w_) for f_, r_, w_ in old_pend])
                    for f_, r_, w_ in pend:
                        S.add("pe", f_, reads=r_, writes=w_)
                    for j in range(2):
                        sl = slice(j * 512, (j + 1) * 512)
                        ts("dve", MEAN[:, sl], psum[4 + j][:, :], 1.0 / D, None, ALU.mult, None, [("ps", 4 + j)], [("mean", j)])
                        tt("dve", TMP, MEAN[:, sl], MEAN[:, sl], ALU.mult, [("mean", j)], ["tmp"])
                        stt("dve", RSTD[:, sl], psum[6 + j][:, :], 1.0 / D, TMP, ALU.mult, ALU.subtract, [("ps", 6 + j), "tmp"], [("rstd", j)])
                        act(RSTD[:, sl], RSTD[:, sl], AF.Sqrt, [("rstd", j)], [("rstd", j)], bias=EPSC)
                        S.add("dve", lambda e, sl=sl, RSTD=RSTD: e.reciprocal(out=RSTD[:, sl], in_=RSTD[:, sl]), reads=[("rstd", j)], writes=[("rstd", j)])
                    mk = [("mean", 0), ("mean", 1), ("rstd", 0), ("rstd", 1)]
                    for jd in range(16):
                        xr = XR[jd % NXR]
                        kxr = ("xr", jd % NXR)
                        dma("pool", xr, zscr[jd, :, :], [("zscr", jd)], [kxr])
                        tt("dve", xr, xr, MEAN, ALU.subtract, [kxr] + mk, [kxr])
                        tt("dve", xr, xr, RSTD, ALU.mult, [kxr] + mk, [kxr])
                        act(xr, xr, AF.Identity, [kxr, "pp"], [kxr], bias=PPl[:, bcol + jd:bcol + jd + 1], scale=PPl[:, gcol + jd:gcol + jd + 1])
                        if not final:
                            dma("sp", xres[jd, :, tok0:tok0 + TG], xr, [kxr], [U("xres")], sem=("xrs", jd % NXR))
                            xo = XO[jd % 2]
                            cp("dve", xo, xr, [kxr], [("xo", jd % 2)])
                            dma("sp", xbf[jd, :, tok0:tok0 + TG], xo, [("xo", jd % 2)], [U("xbf")], sem=("xos", jd % 2))
                        else:
                            dma("sp", xres[jd, :, tok0:tok0 + TG], xr, [kxr], [("xresf", jd)], sem=("xrs", jd % NXR))

                TE = Temps(GEN, GEN + 16384)

                def lhs_wo(jd, cache={}):
                    jp = jd // 2
                    if jp not in cache:
                        cache.clear()
                        cache[jp] = load_w([(w_o[l], jp * 256, 256, 0)], 16)
                    wv, wk = cache[jp]
                    cc = jd % 2
                    return [wv[:, k, cc * 128:(cc + 1) * 128] for k in range(16)], wk
                ln_stage(TE, TE, 16, lhs_wo, lambda k, j: MERGED[:, k, j * 512:(j + 1) * 512], [("merged", c) for c in range(16)], L1G, L1B, False)
                S.fence()
                dma("pool", XB, xbf[:, :, tok0:tok0 + TG].rearrange("c p n -> p c n"), [], ["xb"])

                if upto == "E":
                    continue
                TF = Temps(GEN, NF)
                ACTT = TF.bf(44 * 1024).rearrange("p (c n) -> p c n", c=44)
                UP = [[TF.f32(1026) for _ in range(2)] for _ in range(2)]
                ACC = [TF.f32(1024) for _ in range(2)]
                SG = TF.f32(1024)
                for c in range(44):
                    wv, wk = load_w([(w_up[l], c * 128, 128, 0), (w_up[l], DFF + c * 128, 128, 128)], 16)
                    for hv in range(2):
                        up = UP[hv][c % 2]
                        kup = ("up", hv, c % 2)
                        cc = c + 44 * hv
                        cp("act", up[:, 0:2], HALO[:, cc, :], ["halo"], [(kup, "h")])

                        def ev(j, ps, kp, up=up, kup=kup):
                            cp("act", up[:, 2 + j * 512:2 + (j + 1) * 512], ps[:, :], [kp], [(kup, j)])
                        gemm([wv[:, k, hv * 128:(hv + 1) * 128] for k in range(16)], lambda k, j: XB[:, k, j * 512:(j + 1) * 512],
                             ["xb"], wk, ev)
                        ku = [(kup, "h"), (kup, 0), (kup, 1)]
                        acc = ACC[hv]
                        ka = ("acc", hv)
                        ts("dve", acc, up[:, 2:1026], PPl[:, CW + 2 * 88 + cc:CW + 2 * 88 + cc + 1], PPl[:, CB + cc:CB + cc + 1],
                           ALU.mult, ALU.add, ku + ["pp"], [ka])
                        stt("dve", acc, up[:, 1:1025], PPl[:, CW + 88 + cc:CW + 88 + cc + 1], acc, ALU.mult, ALU.add, ku + [ka, "pp"], [ka])
                        stt("dve", acc, up[:, 0:1024], PPl[:, CW + cc:CW + cc + 1], acc, ALU.mult, ALU.add, ku + [ka, "pp"], [ka])
                        cp("act", HALO[:, cc, :], up[:, 1024:1026], ku, ["halo"])
                    act(SG, ACC[0], AF.Silu, [("acc", 0)], ["sg"])
                    tt("dve", ACTT[:, c, :], SG, ACC[1], ALU.mult, ["sg", ("acc", 1)], [("actt", c)])
                S.fence()

                if upto == "F":
                    continue
                TGm = Temps(GEN + 22528, NF)
                last = (l == L - 1)

                def lhs_wd(jd):
                    a = load_w([(w_down[l][0:2816, :], jd * 128, 128, 0)], 22)
                    b = load_w([(w_down[l][2816:5632, :], jd * 128, 128, 0)], 22)
                    return [a[0][:, k, :] for k in range(22)] + [b[0][:, k, :] for k in range(22)], a[1] + b[1], [(0, 22, a[1]), (22, 44, b[1])]
                ln_stage(TGm, Temps(XB_off, XB_off + 8192), 44, lhs_wd, lambda k, j: ACTT[:, k, j * 512:(j + 1) * 512], [("actt", c) for c in range(44)], L2G, L2B, last, xr_in_tt=False)
                if last:
                    S.fence()
                if last:
                    TO = Temps(XB_off + 2048, XB_off + 8192)
                    XF = [TO.f32(512) for _ in range(2)]
                    OS = [TO.f32(2048) for _ in range(2)]
                    for blk in range(8):
                        osb = OS[blk % 2]
                        for g in range(4):
                            xf = XF[g % 2]
                            kxf = ("xf", g % 2)
                            dma("pool", xf.rearrange("p (c n) -> p c n", c=4),
                                xres[g * 4:(g + 1) * 4, :, tok0 + blk * 128: tok0 + (blk + 1) * 128].rearrange("c p n -> p c n"),
                                [("xresf", g * 4 + i) for i in range(4)], [kxf])
                            b = (blk * 4 + g) % 4

                            def tr(e, xf=xf, b=b):
                                for i in range(4):
                                    inst = e.transpose(psum[b][:, i * 128:(i + 1) * 128], xf[:, i * 128:(i + 1) * 128], IDENT)
                                return inst
                            S.add("pe", tr, reads=[kxf, "const"], writes=[("ps", b)])
                            cp("act" if g % 2 else "dve", osb[:, g * 512:(g + 1) * 512], psum[b][:, :], [("ps", b)], [("os", blk % 2, g)])
                        dma("sp", out[tok0 + blk * 128: tok0 + (blk + 1) * 128, :], osb, [("os", blk % 2, g) for g in range(4)],
                            [U("out")], sem=("oss", blk % 2))
                S.fence()
        S.fence()
        global LAST_SCHED
        LAST_SCHED = S
        S.emit(nc, st)
    return nc


def _pack_cols(v):
    v = np.asarray(v, np.float32).reshape(-1)
    return np.ascontiguousarray(v.reshape(-1, 128).T)


def _host_prep(inputs, L=2):
    c = np.zeros((128, 640), np.float32)
    c[:, 0:128] = np.eye(128, dtype=np.float32)
    k = np.arange(128)[:, None]
    q = np.arange(128)[None, :]
    c[:, 128:256] = (k <= q)
    c[:, 256:384] = (k > q)
    inv = (10000.0 ** (-np.arange(0, 64, 2, dtype=np.float32) / 64.0)).astype(np.float32)
    p = np.arange(128)
    c[:, 384] = inv[p % 32]
    c[:, 385] = np.where((p % 64) < 32, -1.0, 1.0)
    c[:, 386] = EPS
    pp = np.zeros((L, 128, NPP), np.float32)
    for l in range(L):
        pp[l, :, BG:BG + 48] = _pack_cols(inputs["b_gate"][l])
        pp[l, :, QNG:QNG + 4] = _pack_cols(inputs["q_norm_g"][l])
        pp[l, :, KVNG:KVNG + 4] = _pack_cols(inputs["kv_norm_g"][l])
        pp[l, :, SLG:SLG + 8] = _pack_cols(inputs["sgu_ln_g"][l])
        pp[l, :, SLB:SLB + 8] = _pack_cols(inputs["sgu_ln_b"][l])
        pp[l, :, L1G:L1G + 16] = _pack_cols(inputs["ln1_g"][l])
        pp[l, :, L1B:L1B + 16] = _pack_cols(inputs["ln1_b"][l])
        pp[l, :, L2G:L2G + 16] = _pack_cols(inputs["ln2_g"][l])
        pp[l, :, L2B:L2B + 16] = _pack_cols(inputs["ln2_b"][l])
        pp[l, :, CW:CW + 264] = _pack_cols(inputs["conv_w"][l])
        pp[l, :, CB:CB + 88] = _pack_cols(inputs["conv_b"][l])
    esink = np.ascontiguousarray(np.broadcast_to(np.asarray(inputs["sinks"], np.float32)[:L, None, :], (L, 128, 16)))
    bsbc = np.ascontiguousarray(np.broadcast_to(np.asarray(inputs["sgu_b"], np.float32)[:L].reshape(L, 1, 1024), (L, 128, 1024)))
    wst = np.ascontiguousarray(np.asarray(inputs["sgu_w"], np.float32)[:L].transpose(0, 3, 1, 2).reshape(L, 128, 1024))
    shared = {"consts": c, "pp": pp, "esink": esink, "bsbc": bsbc, "wst": wst}
    for k_ in ("w_in", "w_uq", "w_ukv", "w_proj_a", "w_proj_b", "w_proj_c", "w_o", "w_up", "w_down"):
        shared[k_] = np.ascontiguousarray(np.asarray(inputs[k_], np.float32)[:L])
    return shared


_NC_CACHE = {}


def kernel(**inputs):
    x = np.asarray(inputs["x"], np.float32)
    pos = np.asarray(inputs["positions"], np.int32)
    B, S_len, _ = x.shape
    L = 2
    key = (S_len, L)
    if key not in _NC_CACHE:
        _NC_CACHE[key] = build(S_len, L)
    nc = _NC_CACHE[key]
    shared = _host_prep(inputs, L)
    in_maps = []
    for b in range(B):
        m = dict(shared)
        m["x"] = np.ascontiguousarray(x[b])
        m["pos_bc"] = np.ascontiguousarray(np.broadcast_to(pos[b][None, :], (128, S_len)))
        in_maps.append(m)
    res = run_bass_kernel_spmd(nc, in_maps, core_ids=list(range(B)))
    return np.stack([np.asarray(r["out"], np.float32) for r in res.results], axis=0)
```
